# Optimizing a Trainium2 kernel written in Bass

```python
import jax, jax.numpy as jnp
from jax import lax
import numpy as np

D_MODEL = 1024
BATCH = 8
SEQ = 2048
DEPTH = 1
DEC_BATCH = 128
DEC_SEQ = 4
PAST_LEN = 16384
PAGE_SIZE = 128

N_META = 16
POOL_WIDTH = D_MODEL
POOL_WINDOWS = (2, 4, 8, 16)
N_POOL_GROUPS = 4
POOL_GROUP = POOL_WIDTH // N_POOL_GROUPS
POOL_BUF = max(POOL_WINDOWS) - 1
MLSTM_WIDTH = 2 * D_MODEL
N_HEADS = 4
HEAD_DIM = MLSTM_WIDTH // N_HEADS
CHUNK = 64
EPS = 1e-6
N_IN = 2 * POOL_WIDTH + 5 * MLSTM_WIDTH + 2 * N_HEADS + 2 * D_MODEL

kernel_name = 'hybrid_pool_mlstm_gated_decode_step'


def rmsnorm(x, g):
    xf = x.astype(jnp.float32)
    y = xf * lax.rsqrt(jnp.mean(xf * xf, axis=-1, keepdims=True) + EPS)
    return (y * g.astype(jnp.float32)).astype(x.dtype)


def split_projection(proj):
    sizes = (POOL_WIDTH, POOL_WIDTH, MLSTM_WIDTH, MLSTM_WIDTH, MLSTM_WIDTH, MLSTM_WIDTH, MLSTM_WIDTH,
             N_HEADS, N_HEADS, D_MODEL, D_MODEL)
    idx = np.cumsum(sizes)[:-1].tolist()
    return jnp.split(proj, idx, axis=-1)


def multiscale_pool(buf, a, pos0, w_pool, pool_scale):
    B, T, _ = a.shape
    ext = jnp.concatenate([buf.astype(jnp.float32), a.astype(jnp.float32)], axis=1)
    cs = jnp.cumsum(jnp.concatenate([jnp.zeros_like(ext[:, :1]), ext], axis=1), axis=1)
    pos = pos0 + jnp.arange(T)
    tok = ext[:, POOL_BUF:]
    outs = []
    for g, w in enumerate(POOL_WINDOWS):
        sl = slice(g * POOL_GROUP, (g + 1) * POOL_GROUP)
        hi = cs[:, POOL_BUF + 1:POOL_BUF + 1 + T, sl]
        lo = cs[:, POOL_BUF + 1 - w:POOL_BUF + 1 - w + T, sl]
        cnt = jnp.minimum(pos + 1, w).astype(jnp.float32)[None, :, None]
        outs.append((hi - lo) / cnt - tok[:, :, sl])
    pooled = jnp.stack(outs, axis=2).astype(a.dtype)
    mixed = jnp.einsum('btgc,gcd->btgd', pooled, w_pool).reshape(B, T, POOL_WIDTH)
    return mixed * pool_scale


def mlstm_chunk(carry, inp):
    C, n, m = carry
    q, k, v, ig, lf = inp
    L = q.shape[2]
    b = jnp.cumsum(lf, axis=-1)
    causal = jnp.tril(jnp.ones((L, L), dtype=bool))
    logw = jnp.where(causal, b[..., :, None] - b[..., None, :] + ig[..., None, :], -jnp.inf)
    log_inter = b + m[..., None]
    m_t = jnp.maximum(log_inter, jnp.max(logw, axis=-1))
    w_intra = jnp.exp(logw - m_t[..., None])
    w_inter = jnp.exp(log_inter - m_t)
    s = jnp.einsum('bhtd,bhsd->bhts', q, k) * w_intra
    num = jnp.einsum('bhts,bhsv->bhtv', s, v) + w_inter[..., None] * jnp.einsum('bhtd,bhdv->bhtv', q, C)
    den = jnp.sum(s, axis=-1) + w_inter * jnp.einsum('bhtd,bhd->bht', q, n)
    h = num / jnp.maximum(jnp.abs(den), jnp.exp(-m_t))[..., None]
    m_new = m_t[..., -1]
    w_state = jnp.exp(b[..., -1:] - b + ig - m_new[..., None])
    decay = jnp.exp(b[..., -1] + m - m_new)
    kw = k * w_state[..., None]
    C_new = decay[..., None, None] * C + jnp.einsum('bhsd,bhsv->bhdv', kw, v)
    n_new = decay[..., None] * n + jnp.sum(kw, axis=2)
    return (C_new, n_new, m_new), h


def mlstm_scan(q, k, v, ig, lf, state, n_lead):
    B, H, T, Dh = q.shape
    h_lead = []
    if n_lead > 0:
        state, h0 = mlstm_chunk(state, (q[:, :, :n_lead], k[:, :, :n_lead], v[:, :, :n_lead],
                                        ig[:, :, :n_lead], lf[:, :, :n_lead]))
        h_lead = [h0]
    rest = T - n_lead
    chunk = CHUNK if rest % CHUNK == 0 else rest
    nc = rest // chunk

    def blocks(t):
        t = t[:, :, n_lead:]
        t = t.reshape(t.shape[:2] + (nc, chunk) + t.shape[3:])
        return jnp.moveaxis(t, 2, 0)

    xs = (blocks(q), blocks(k), blocks(v), blocks(ig), blocks(lf))
    state, hs = lax.scan(mlstm_chunk, state, xs)
    hs = jnp.moveaxis(hs, 0, 2).reshape(B, H, rest, Dh)
    return state, jnp.concatenate(h_lead + [hs], axis=2)


def hybrid_layer(x, pos0, n_lead, pool_buf, C, n, m, norm_g, w_in, gate_bias, w_pool, pool_scale,
                 mh_norm_g, w_branch_pool, w_branch_mlstm, w_out):
    B, T, _ = x.shape
    f32 = jnp.float32
    hn = rmsnorm(x, norm_g)
    proj = jnp.einsum('btd,dn->btn', hn, w_in)
    a_pool, z_pool, q, k, v, o, z_m, i_pre, f_pre, g_pool, g_mlstm = split_projection(proj)

    y_pool = multiscale_pool(pool_buf, a_pool, pos0, w_pool, pool_scale) * jax.nn.silu(z_pool)
    new_buf = jnp.concatenate([pool_buf.astype(a_pool.dtype), a_pool], axis=1)[:, -POOL_BUF:]

    def heads(t):
        return t.reshape(B, T, N_HEADS, HEAD_DIM).transpose(0, 2, 1, 3).astype(f32)
    qh = heads(q)
    kh = heads(k) * (HEAD_DIM ** -0.5)
    vh = heads(v)
    gb = gate_bias.astype(f32)
    ig = (i_pre.astype(f32) + gb[:N_HEADS]).transpose(0, 2, 1)
    lf = jax.nn.log_sigmoid(f_pre.astype(f32) + gb[N_HEADS:]).transpose(0, 2, 1)
    state = (C.astype(f32), n.astype(f32), m.astype(f32))
    (C_new, n_new, m_new), h = mlstm_scan(qh, kh, vh, ig, lf, state, n_lead)
    h = h.transpose(0, 2, 1, 3)
    mu = jnp.mean(h, axis=-1, keepdims=True)
    var = jnp.mean(jnp.square(h - mu), axis=-1, keepdims=True)
    h = ((h - mu) * lax.rsqrt(var + EPS)).reshape(B, T, MLSTM_WIDTH) * mh_norm_g.astype(f32)
    y_mlstm = jax.nn.sigmoid(o) * h.astype(x.dtype) * jax.nn.silu(z_m)

    u = (jax.nn.sigmoid(g_pool) * jnp.einsum('btp,pd->btd', y_pool, w_branch_pool)
         + jax.nn.sigmoid(g_mlstm) * jnp.einsum('btm,md->btd', y_mlstm, w_branch_mlstm))
    x_out = x + jnp.einsum('btd,de->bte', u, w_out)
    return x_out, (new_buf, C_new, n_new, m_new)


def setup_inputs(seed: int = 0) -> dict:
    key = jax.random.key(seed)
    ks = jax.random.split(key, 20)
    f32 = jnp.float32

    def nrm(k, shape, s):
        return jax.random.normal(k, shape, f32) * s

    f_bias = jnp.linspace(3.0, 6.0, N_HEADS, dtype=f32)
    gate_bias = jnp.concatenate([nrm(ks[9], (DEPTH, N_HEADS), 0.1),
                                 f_bias[None] + nrm(ks[10], (DEPTH, N_HEADS), 0.1)], axis=-1)
    return {
        'x_prompt': nrm(ks[0], (BATCH, SEQ, D_MODEL), 1.0),
        'x_sample': nrm(ks[1], (DEC_BATCH, DEC_SEQ, D_MODEL), 1.0),
        'state_pool': nrm(ks[2], (DEPTH, DEC_BATCH, POOL_BUF, POOL_WIDTH), 1.0),
        'state_C': nrm(ks[3], (DEPTH, DEC_BATCH, N_HEADS, HEAD_DIM, HEAD_DIM), HEAD_DIM ** -0.5),
        'state_n': nrm(ks[4], (DEPTH, DEC_BATCH, N_HEADS, HEAD_DIM), 1.0),
        'state_m': nrm(ks[5], (DEPTH, DEC_BATCH, N_HEADS), 1.0),
        'meta_tokens': nrm(ks[6], (N_META, D_MODEL), 1.0),
        'norm_g': 1.0 + nrm(ks[7], (DEPTH, D_MODEL), 0.02),
        'w_in': nrm(ks[8], (DEPTH, D_MODEL, N_IN), D_MODEL ** -0.5),
        'gate_bias': gate_bias,
        'w_pool': nrm(ks[11], (DEPTH, N_POOL_GROUPS, POOL_GROUP, POOL_GROUP), POOL_GROUP ** -0.5),
        'pool_scale': 1.0 + nrm(ks[12], (DEPTH, POOL_WIDTH), 0.02),
        'mh_norm_g': 1.0 + nrm(ks[13], (DEPTH, MLSTM_WIDTH), 0.02),
        'w_branch_pool': nrm(ks[14], (DEPTH, POOL_WIDTH, D_MODEL), POOL_WIDTH ** -0.5),
        'w_branch_mlstm': nrm(ks[15], (DEPTH, MLSTM_WIDTH, D_MODEL), MLSTM_WIDTH ** -0.5),
        'w_out': nrm(ks[16], (DEPTH, D_MODEL, D_MODEL), D_MODEL ** -0.5),
        'final_g': 1.0 + nrm(ks[17], (D_MODEL,), 0.02),
    }


def reference(x_prompt, x_sample, state_pool, state_C, state_n, state_m, meta_tokens, norm_g, w_in,
              gate_bias, w_pool, pool_scale, mh_norm_g, w_branch_pool, w_branch_mlstm, w_out, final_g):
    bp = x_prompt.shape[0]
    meta = jnp.broadcast_to(meta_tokens[None].astype(x_prompt.dtype), (bp, N_META, D_MODEL))
    xp = jnp.concatenate([meta, x_prompt], axis=1)
    xs = x_sample
    new_p = ([], [], [], [])
    new_s = ([], [], [], [])
    for l in range(DEPTH):
        weights = (norm_g[l], w_in[l], gate_bias[l], w_pool[l], pool_scale[l], mh_norm_g[l],
                   w_branch_pool[l], w_branch_mlstm[l], w_out[l])
        zero_buf = jnp.zeros((bp, POOL_BUF, POOL_WIDTH), xp.dtype)
        zero_C = jnp.zeros((bp, N_HEADS, HEAD_DIM, HEAD_DIM), jnp.float32)
        zero_n = jnp.zeros((bp, N_HEADS, HEAD_DIM), jnp.float32)
        zero_m = jnp.zeros((bp, N_HEADS), jnp.float32)
        xp, sp = hybrid_layer(xp, 0, N_META, zero_buf, zero_C, zero_n, zero_m, *weights)
        xs, ss = hybrid_layer(xs, PAST_LEN, 0, state_pool[l], state_C[l], state_n[l], state_m[l], *weights)
        for lst, a in zip(new_p, sp):
            lst.append(a)
        for lst, a in zip(new_s, ss):
            lst.append(a)
    y_prompt = rmsnorm(xp, final_g)[:, N_META:]
    y_sample = rmsnorm(xs, final_g)
    pool_p = jnp.stack(new_p[0]).astype(x_prompt.dtype)
    C_p = jnp.stack(new_p[1]).astype(x_prompt.dtype)
    n_p = jnp.stack(new_p[2]).astype(x_prompt.dtype)
    m_p = jnp.stack(new_p[3]).astype(x_prompt.dtype)
    pool_s = jnp.stack(new_s[0]).astype(state_pool.dtype)
    C_s = jnp.stack(new_s[1]).astype(state_C.dtype)
    n_s = jnp.stack(new_s[2]).astype(state_n.dtype)
    m_s = jnp.stack(new_s[3]).astype(state_m.dtype)
    return (y_prompt, y_sample, pool_p, C_p, n_p, m_p, pool_s, C_s, n_s, m_s)
```

```python
import math
import numpy as np
import concourse.bass as bass
import concourse.mybir as mybir
from concourse.bass_utils import run_bass_kernel_spmd

F32 = mybir.dt.float32
BF16 = mybir.dt.bfloat16
AF = mybir.ActivationFunctionType
ALU = mybir.AluOpType
AX = mybir.AxisListType

D = 1024
NIN = 14344
NH = 4
HD = 512
EPS = 1e-6
C_A, C_ZP, C_Q, C_K, C_V, C_O, C_ZM, C_I, C_F, C_GP, C_GM = 0, 1024, 2048, 4096, 6144, 8192, 10240, 12288, 12292, 12296, 13320
LNK = -0.5 * math.log(HD)
import os as _os
TICK_ON = _os.environ.get("TICK", "1") == "1"


class Buf:
    __slots__ = ("name", "w", "rs", "excl")

    def __init__(self, name, excl=False):
        self.name = name
        self.w = None
        self.rs = []
        self.excl = excl


class DSem:
    def __init__(self, sem):
        self.sem = sem
        self.count = 0


class Op:
    __slots__ = ("eng", "fn", "reads", "writes", "dsem", "dval", "deps", "signal", "sigval", "waits", "idx")


class Prog:
    ENGS = ("pe", "act", "dve", "pool", "sp")

    def __init__(self, nc):
        self.nc = nc
        self.ops = []
        self.esem = {e: nc.alloc_semaphore("es_" + e) for e in self.ENGS}
        self.dsems = []
        self.nbuf = 0
        self.defer_mode = False
        self.deferred = []
        self._open_group = False

    def buf(self, name=None, excl=False):
        self.nbuf += 1
        return Buf(name or ("b%d" % self.nbuf), excl)

    def dsem(self, name):
        d = DSem(self.nc.alloc_semaphore("ds_" + name))
        self.dsems.append(d)
        return d

    def op(self, eng, fn, reads=(), writes=(), hold=False):
        o = Op()
        o.eng = eng
        o.fn = fn
        rd = [b for b in reads if b is not None]
        wr = [b for b in writes if b is not None]
        o.writes = wr + [b for b in rd if b.excl]
        o.reads = [b for b in rd if not b.excl]
        o.dsem = None
        o.dval = 0
        o.signal = False
        o.sigval = 0
        if self.defer_mode:
            o.idx = -1
            if self._open_group:
                self.deferred[-1].append(o)
            else:
                self.deferred.append([o])
            self._open_group = hold
        else:
            o.idx = len(self.ops)
            self.ops.append(o)
        return o

    def flush(self, n=None):
        k = len(self.deferred) if n is None else min(n, len(self.deferred))
        for grp in self.deferred[:k]:
            for o in grp:
                o.idx = len(self.ops)
                self.ops.append(o)
        del self.deferred[:k]
        if not self.deferred:
            self._open_group = False

    def dma(self, eng, fn, dsem, reads=(), writes=(), n=1):
        o = self.op(eng, fn, reads, writes)
        o.dsem = dsem
        dsem.count += 16 * n
        o.dval = dsem.count
        return o

    def _skip(self, d, o):
        return d.eng == o.eng and d.eng == "pe" and o.dsem is None and d.dsem is None

    def finalize(self):
        for o in self.ops:
            deps = {}
            for b in o.reads:
                if b.w is not None:
                    deps[b.w.idx] = b.w
            for b in o.writes:
                if b.w is not None:
                    deps[b.w.idx] = b.w
                for r in b.rs:
                    deps[r.idx] = r
            deps.pop(o.idx, None)
            for b in o.reads:
                b.rs.append(o)
            for b in o.writes:
                b.w = o
                b.rs = []
            o.deps = list(deps.values())
        for o in self.ops:
            for d in o.deps:
                if d.dsem is None and not self._skip(d, o):
                    d.signal = True
        cnt = {e: 0 for e in self.ENGS}
        for o in self.ops:
            if o.dsem is None and o.signal:
                cnt[o.eng] += 1
                o.sigval = cnt[o.eng]
        seen = {e: {} for e in self.ENGS}
        nw = 0
        for o in self.ops:
            need = {}
            for d in o.deps:
                if d.dsem is not None:
                    key, sem, val = ("d", id(d.dsem)), d.dsem.sem, d.dval
                else:
                    if self._skip(d, o):
                        continue
                    key, sem, val = ("e", d.eng), self.esem[d.eng], d.sigval
                if seen[o.eng].get(key, 0) >= val:
                    continue
                if key not in need or need[key][1] < val:
                    need[key] = (sem, val)
            o.waits = list(need.values())
            for key, (sem, val) in need.items():
                seen[o.eng][key] = val
            nw += len(o.waits)
        self.nwaits = nw

    def emit(self):
        nc = self.nc
        self.finalize()
        prog = self

        def run(ename, e):
            for o in prog.ops:
                if o.eng != ename:
                    continue
                for sem, val in o.waits:
                    e.wait_ge(sem, val)
                r = o.fn(e)
                if o.dsem is not None:
                    if not isinstance(r, (list, tuple)):
                        r = [r]
                    for ins in r:
                        ins.then_inc(o.dsem.sem, 16)
                elif o.signal:
                    r.then_inc(prog.esem[ename], 1)
            if ename == "sp":
                for d in prog.dsems:
                    if d.count > 0:
                        e.wait_ge(d.sem, d.count)

        with nc.Block() as block:
            @block.tensor
            def _(e):
                run("pe", e)

            @block.scalar
            def _(e):
                run("act", e)

            @block.vector
            def _(e):
                run("dve", e)

            @block.gpsimd
            def _(e):
                run("pool", e)

            @block.sync
            def _(e):
                run("sp", e)


POOL_WINDOWS = (2, 4, 8, 16)


def _pool_mats(kind):
    cur, hist = [], []
    for w in POOL_WINDOWS:
        if kind == "P":
            c = np.zeros((128, 128), np.float32)
            h = np.zeros((128, 128), np.float32)
            for t in range(128):
                for j in range(t - w + 1, t + 1):
                    if j >= 0:
                        c[j, t] += 1.0 / w
                    else:
                        h[128 + j, t] += 1.0 / w
                c[t, t] -= 1.0
            cur.append(c)
            hist.append([h])
        elif kind == "P0":
            h = np.zeros((16, 128), np.float32)
            for t in range(128):
                for j in range(t - w + 1, t + 1):
                    if j < 0:
                        h[16 + j, t] += 1.0 / w
            hist.append([h])
            cur.append(None)
        elif kind == "M":
            c = np.zeros((16, 16), np.float32)
            for t in range(16):
                cnt = min(t + 1, w)
                for j in range(max(0, t - w + 1), t + 1):
                    c[j, t] += 1.0 / cnt
                c[t, t] -= 1.0
            cur.append(c)
            hist.append([])
        elif kind == "S":
            c = np.zeros((64, 64), np.float32)
            h0 = np.zeros((120, 64), np.float32)
            h1 = np.zeros((120, 64), np.float32)
            for s in range(16):
                for t in range(4):
                    col = s * 4 + t
                    for e in range(15 + t - w + 1, 15 + t + 1):
                        if e >= 15:
                            c[s * 4 + (e - 15), col] += 1.0 / w
                        else:
                            r = s * 15 + e
                            if r < 120:
                                h0[r, col] += 1.0 / w
                            else:
                                h1[r - 120, col] += 1.0 / w
                    c[col, col] -= 1.0
            cur.append(c)
            hist.append([h0, h1])
    return cur, hist


class Pack:
    def __init__(self):
        self.items = []
        self.off = {}
        self.w = 0

    def add(self, name, arr):
        arr = np.asarray(arr, np.float32)
        assert arr.ndim == 2 and arr.shape[0] <= 128
        self.off[name] = (self.w, arr.shape[0], arr.shape[1])
        self.items.append(arr)
        self.w += arr.shape[1]

    def build(self):
        w = max(64, self.w)
        out = np.zeros((128, w), np.float32)
        for (name, (o, r, c)), a in zip(self.off.items(), self.items):
            out[:r, o:o + c] = a
        return out


def _struct_consts():
    pf = Pack()
    pb = Pack()
    pf.add("ident", np.eye(128))
    for kind, ntok, nseq, L in (("S", 64, 16, 4), ("M", 16, 1, 16), ("P", 128, 1, 128)):
        seq = np.arange(ntok) // L
        same = (seq[:, None] == seq[None, :])
        causal = same & (np.arange(ntok)[:, None] <= np.arange(ntok)[None, :])
        pf.add("mask" + kind, causal.astype(np.float32))
        E = (np.arange(nseq)[:, None] == seq[None, :]).astype(np.float32)
        pb.add("ET" + kind, E.T)
    pf.add("ones4", np.ones((4, 128)))
    pf.add("I4", np.eye(4))
    for kind in ("S", "M", "P", "P0"):
        cur, hist = _pool_mats(kind)
        for g in range(4):
            if cur[g] is not None:
                pb.add("pc%s%d" % (kind, g), cur[g])
            for j, h in enumerate(hist[g]):
                pb.add("ph%s%d_%d" % (kind, g, j), h)
    cm = np.zeros((16, 64), np.float32)
    for s in range(16):
        cm[s, 4 * s:4 * s + 4] = 1.0
    pb.add("colmask", np.broadcast_to(cm.reshape(1, 1024), (128, 1024)))
    pb.add("ones", np.ones((128, 8)))
    pb.add("identb", np.eye(128))
    seqS = np.arange(64) // 4
    pf.add("ETf", (seqS[:, None] == np.arange(16)[None, :]).astype(np.float32))
    return pf, pb


def build(nchunk=16, with_sample=True, with_meta=True, dbg=None):
    nc = bass.Bass("TRN2", target_bir_lowering=False)
    P = Prog(nc)
    SEQ = 128 * nchunk
    pf, pb = _struct_consts()
    cf_np = pf.build()
    cb_np = pb.build()

    def din(name, shape, dt=F32):
        return nc.dram_tensor(name, list(shape), dt, kind="ExternalInput")

    def dout(name, shape, dt=F32):
        return nc.dram_tensor(name, list(shape), dt, kind="ExternalOutput")

    xp = din("xp", [SEQ, D])
    xs = din("xs", [64, D])
    spool = din("spool", [240, D])
    sC = din("sC", [16, NH, HD, HD])
    snT = din("snT", [128, 256])
    smT = din("smT", [4, 64])
    meta = din("meta", [16, D])
    w_in = din("w_in", [D, NIN])
    w_pool = din("w_pool", [4, 256, 256])
    wbp = din("wbp", [D, D])
    wbm = din("wbm", [2 * D, D])
    wout = din("wout", [D, D])
    vecs = din("vecs", [128, 128])
    fgb = din("fgb", [1, D])
    cfd = din("cf", list(cf_np.shape))
    cbd = din("cb", list(cb_np.shape))
    y_p = dout("y_p", [SEQ, D])
    y_s = dout("y_s", [64, D])
    pool_p = dout("pool_p", [15, D])
    C_p = dout("C_p", [NH, HD, HD])
    nT_p = dout("nT_p", [128, 64])
    m_p = dout("m_p", [1, 64])
    pool_s = dout("pool_s", [16, 15, D])
    C_s = dout("C_s", [16, NH, HD, HD])
    nT_s = dout("nT_s", [128, 256])
    m_s = dout("m_s", [1, 64])
    NBLK = 36
    wblk = nc.dram_tensor("wblk", [NBLK, 128, 8, 512], BF16, kind="Internal")

    def sb(name, shape, dt=F32):
        return nc.alloc_sbuf_tensor(name, list(shape), dt)

    cf = sb("cf_t", cf_np.shape)
    cb = sb("cb_t", cb_np.shape, BF16)
    vec = sb("vec_t", [128, 128])
    fg = sb("fg_t", [128, D])
    wg = sb("wg_t", [128, 8, 64], BF16)
    wpl = sb("wpl_t", [128, 4, 2, 256], BF16)
    NSLOT = 6
    ring = [sb("ring%d" % i, [128, 8, 512], BF16) for i in range(NSLOT)]
    xt = [sb("xt%d" % i, [128, D]) for i in range(3)]
    hnT2 = [sb("hnT%d" % i, [128, 8, 128], BF16) for i in range(2)]
    atok = [sb("atok%d" % i, [128, D], BF16) for i in range(2)]
    af32 = sb("af32", [128, D])
    hs = [sb("hs%d" % i, [120, D], BF16) for i in range(2)]
    szpT = sb("szpT", [128, 8, 128])
    pooledT = sb("pooledT", [128, 8, 128], BF16)
    ypT = sb("ypT", [128, 8, 128], BF16)
    sgm = sb("sgm", [128, D])
    upg = sb("upg", [128, D])
    ubf = sb("ubf", [128, D], BF16)
    uT = sb("uT", [128, 8, 128], BF16)
    xo = sb("xo", [128, D])
    NHB = 2
    qT = [sb("qT%d" % i, [128, 4, 128], BF16) for i in range(NHB)]
    kT = [sb("kT%d" % i, [128, 4, 128], BF16) for i in range(NHB)]
    kw2 = [sb("kw2%d" % i, [128, 512], BF16) for i in range(NHB)]
    qtok = [sb("qtok%d" % i, [128, 512], BF16) for i in range(NHB)]
    ktok = [sb("ktok%d" % i, [128, 512], BF16) for i in range(NHB)]
    vtok = [sb("vtok%d" % i, [128, 512], BF16) for i in range(NHB)]
    Gt = [sb("G%d" % i, [128, 512]) for i in range(NHB)]
    GR = [sb("GR%d" % i, [128, 512]) for i in range(NHB)]
    SWT = [sb("SWT%d" % i, [128, 128], BF16) for i in range(NHB)]
    yln = sb("yln", [128, 2048], BF16)
    yT = sb("yT", [128, 16, 128], BF16)
    Cst = [sb("Cst%d" % i, [128, 4, 512]) for i in range(NH)]
    Cbf = [sb("Cbf%d" % i, [128, 4, 512], BF16) for i in range(NH)]
    nst = sb("nst", [128, 256])
    nbf = sb("nbf", [128, 256], BF16)
    qTm = [sb("qTm%d" % i, [128, 4, 64], BF16) for i in range(2)]
    vm = [sb("vm%d" % i, [64, 512], BF16) for i in range(2)]
    igT = sb("igT", [4, 128])
    efT = sb("efT", [4, 128])
    cs = [sb("cs%d" % i, [4, 128]) for i in range(2)]
    gT = sb("gT", [4, 128])
    argT = sb("argT", [4, 128])
    G3 = sb("G3", [96, 128])
    G3T = sb("G3T", [128, 96])
    mprevT = sb("mprevT", [4, 64])
    mxT = sb("mxT", [4, 16])
    gmax = sb("gmax", [4, 16])
    decT = sb("decT", [4, 16])
    mnewT = sb("mnewT", [4, 16])
    DM = sb("DM", [4, 128])
    decb = sb("decb", [128, 128])
    ngbf = sb("ngbf", [4, 1])
    sm1 = sb("sm1", [128, 16])
    st6 = sb("st6", [128, 8])
    ntmp = sb("ntmp", [128, 16])

    def ps(name, shape, dt=F32):
        return nc.alloc_psum_tensor(name, list(shape), dt)

    NPJ = 3
    PJ = [ps("PJ%d" % i, [128, 512]) for i in range(NPJ)]
    TP = ps("TP", [128, 1024], BF16)
    STp = ps("STp", [128, 512])
    NUM = [ps("NUM%d" % i, [128, 512]) for i in range(1)]
    CU = [ps("CU%d" % i, [128, 512]) for i in range(2)]
    PJR = PJ + CU
    PJN = ["PJ0", "PJ1", "PJ2", "CU0", "CU1"]
    NROT = len(PJR)

    B = {}

    def bf(name, excl=False):
        B[name] = P.buf(name, excl)
        return B[name]

    for n_ in ("cf cb vec fg wg wpl xsq hnT0 hnT1 sm1p sm1t sm1h0 sm1h1 sm1h2 sm1h3 af32 hs0 hs1 szpT pooledT ypT sgp sgm upg ubf uT xo yout yln0 yln1 yln2 yln3 yT0 yT1 yT2 yT3 nst nbf "
               "igT efT cs0 cs1 gT argT G3 G3T mprevT mxT gmax decT mnewT DM decb ngbf sm1 st6 ntmp xt0 xt1 xt2 atok0 atok1 "
               "qTm0 qTm1 vm0 vm1").split():
        bf(n_)
    for i in range(NHB):
        for n_ in ("qT", "kT", "kw2", "qtok", "ktok", "vtok", "og", "G", "GR", "SWT"):
            bf("%s%d" % (n_, i))
    for i in range(NH):
        bf("Cst%d" % i)
        bf("Cbf%d" % i)
    for i in range(NSLOT):
        bf("ring%d" % i)
    for n_ in ("PJ0", "PJ1", "PJ2", "TP", "STp", "NUM0", "CU0", "CU1"):
        bf(n_, excl=True)
    wbuf = [P.buf("wblk%d" % j) for j in range(NBLK)]
    ring_ds = [P.dsem("ring%d" % i) for i in range(NSLOT)]
    x_ds = [P.dsem("x%d" % i) for i in range(3)]
    c_ds = [P.dsem("c%d" % i) for i in range(NH)]
    out_ds = P.dsem("out")
    cout_ds = [P.dsem("co%d" % i) for i in range(NH)]
    o_ds = {k: P.dsem("o_" + k) for k in ("nTs", "ms", "ps", "nTp", "mp", "pp")}
    yo_ds = P.dsem("yo")
    misc_ds = [P.dsem("misc%d" % i) for i in range(12)]
    cv_ds = [P.dsem("cv%d" % i) for i in range(NBLK)]

    def cfs(name):
        o, r, c = pf.off[name]
        return cf[0:r, o:o + c]

    def cbs(name):
        o, r, c = pb.off[name]
        return cb[0:r, o:o + c]

    blocks = []
    BI = {}

    def addblk(key, t, r0, c0):
        BI[key] = len(blocks)
        blocks.append((t, r0, c0))

    for h in range(NH):
        addblk(("q", h), w_in, 0, C_Q + 512 * h)
        addblk(("k", h), w_in, 0, C_K + 512 * h)
        addblk(("v", h), w_in, 0, C_V + 512 * h)
        addblk(("o", h), w_in, 0, C_O + 512 * h)
        addblk(("z", h), w_in, 0, C_ZM + 512 * h)
    for j in range(2):
        addblk(("a", j), w_in, 0, C_A + 512 * j)
    for j in range(2):
        addblk(("zp", j), w_in, 0, C_ZP + 512 * j)
    for j in range(2):
        addblk(("gp", j), w_in, 0, C_GP + 512 * j)
    for j in range(2):
        addblk(("gm", j), w_in, 0, C_GM + 512 * j)
    for j in range(2):
        addblk(("bp", j), wbp, 0, 512 * j)
    for j in range(2):
        for mh in range(2):
            addblk(("bm", j, mh), wbm, 1024 * mh, 512 * j)
    for j in range(2):
        addblk(("wo", j), wout, 0, 512 * j)
    assert len(blocks) == NBLK

    P.dma("sp", lambda e: e.dma_start(out=cf[:], in_=cfd[:, :]), misc_ds[0], writes=[B["cf"]])
    P.dma("pool", lambda e: e.dma_start(out=cb[:], in_=cbd[:, :]), misc_ds[1], writes=[B["cb"]])
    P.dma("sp", lambda e: e.dma_start(out=vec[:], in_=vecs[:, :]), misc_ds[2], writes=[B["vec"]])
    P.dma("sp", lambda e: e.dma_start(out=fg[:], in_=fgb[0:1, :].partition_broadcast(128)), misc_ds[3], writes=[B["fg"]])
    P.dma("pool", lambda e: e.dma_start(out=wg[:], in_=w_in[:, C_I - 56:C_I + 8].rearrange("(kc p) c -> p kc c", p=128)),
          misc_ds[4], writes=[B["wg"]])
    P.dma("pool", lambda e: e.dma_start(out=wpl[:], in_=w_pool.ap().rearrange("g (cl c) d -> c g cl d", c=128)),
          misc_ds[5], writes=[B["wpl"]])
    for j, (t, r0, c0) in enumerate(blocks):
        P.dma("pool", lambda e, j=j, t=t, r0=r0, c0=c0: e.dma_start(
            out=wblk[j].rearrange("p kc c -> kc p c"),
            in_=t[r0:r0 + 1024, c0:c0 + 512].rearrange("(kc p) c -> kc p c", p=128)),
            cv_ds[j], writes=[wbuf[j]])
    P.op("dve", lambda e: e.tensor_scalar(out=ngbf[:], in0=vec[0:4, 33:34], scalar1=-1.0, scalar2=None, op0=ALU.mult),
         reads=[B["vec"]], writes=[B["ngbf"]])
    P.op("pool", lambda e: e.memset(G3[:], 0.0), writes=[B["G3"]])
    P.op("pool", lambda e: e.memset(sm1[:], 0.0), writes=[B["sm1p"], B["sm1t"], B["sm1h0"], B["sm1h1"], B["sm1h2"], B["sm1h3"]])
    P.op("pool", lambda e: e.memset(DM[:], 0.0), writes=[B["DM"]])
    P.op("pool", lambda e: e.memset(nst[:], 0.0), writes=[B["nst"]])

    rstate = {"slot": 0, "xi": 0, "ai": 0}

    def tick(n=1):
        if TICK_ON:
            P.flush(n)

    def load_block(key):
        j = BI[key]
        s = rstate["slot"]
        rstate["slot"] = (s + 1) % NSLOT
        P.dma("sp", lambda e, j=j, s=s: e.dma_start(out=ring[s][:], in_=wblk[j]), ring_ds[s],
              reads=[wbuf[j]], writes=[B["ring%d" % s]])
        return ring[s], B["ring%d" % s]

    def run_pass(kind, x_src, y_dst, ci=0):
        nt = {"S": 64, "M": 16, "P": 128}[kind]
        nseq = 16 if kind == "S" else 1
        L = nt // nseq
        zero_state = (kind == "M") or (kind == "P" and not with_meta and ci == 0)
        xi = rstate["xi"]
        rstate["xi"] = (xi + 1) % 3
        hi = rstate.get("hi", 0)
        rstate["hi"] = 1 - hi
        X, BX = xt[xi], B["xt%d" % xi]
        hnT, BhnT = hnT2[hi], B["hnT%d" % hi]
        ai = rstate["ai"]
        rstate["ai"] = 1 - ai
        A, BA = atok[ai], B["atok%d" % ai]
        Aprev, BAprev = atok[1 - ai], B["atok%d" % (1 - ai)]
        maskT = cfs("mask" + kind)
        ident = cfs("ident")
        identb = None

        P.dma("sp", lambda e: e.dma_start(out=X[0:nt, :], in_=x_src), x_ds[xi], writes=[BX])
        if kind == "S":
            P.dma("sp", lambda e: e.dma_start(out=mprevT[:], in_=smT[:, :]), misc_ds[7], writes=[B["mprevT"]])
            for i_ in range(2):
                P.dma("pool", lambda e, i_=i_: e.dma_start(out=hs[i_][:], in_=spool[120 * i_:120 * (i_ + 1), :]),
                      misc_ds[9 + i_], writes=[B["hs%d" % i_]])
        elif zero_state:
            P.op("pool", lambda e: e.memset(mprevT[:], 0.0), writes=[B["mprevT"]])

        P.op("act", lambda e: e.activation(out=xo[0:nt, :], in_=X[0:nt, :], func=AF.Square, accum_out=sm1[0:nt, 0:1]),
             reads=[BX], writes=[B["xo"], B["sm1p"]])
        P.op("act", lambda e: e.activation(out=sm1[0:nt, 1:2], in_=sm1[0:nt, 0:1], func=AF.Ln, scale=1.0 / D, bias=EPS),
             reads=[B["sm1p"]], writes=[B["sm1p"]])
        P.op("act", lambda e: e.activation(out=sm1[0:nt, 2:3], in_=sm1[0:nt, 1:2], func=AF.Exp, scale=-0.5),
             reads=[B["sm1p"]], writes=[B["sm1p"]])
        P.op("dve", lambda e: e.tensor_scalar(out=xo[0:nt, :], in0=X[0:nt, :], scalar1=sm1[0:nt, 2:3], scalar2=None, op0=ALU.mult),
             reads=[BX, B["sm1p"]], writes=[B["xo"]])
        for half in range(2):
            pjv = rstate.get("pj", 0)
            rstate["pj"] = (pjv + 1) % NROT
            pj, bpj = PJR[pjv], B[PJN[pjv]]
            for q4 in range(4):
                kc = half * 4 + q4
                P.op("pe", lambda e, kc=kc, q4=q4, pj=pj: e.transpose(out=pj[:, q4 * nt:(q4 + 1) * nt], in_=xo[0:nt, kc * 128:(kc + 1) * 128],
                                                                     identity=ident[0:nt, 0:nt]),
                     reads=[B["xo"], B["cf"]], writes=[bpj], hold=True)
            for q4 in range(4):
                kc = half * 4 + q4
                P.op("dve", lambda e, kc=kc, q4=q4, pj=pj: e.tensor_scalar(out=hnT[:, kc, 0:nt], in0=pj[:, q4 * nt:(q4 + 1) * nt],
                                                                         scalar1=vec[:, kc:kc + 1], scalar2=None, op0=ALU.mult),
                     reads=[bpj, B["vec"]], writes=[BhnT], hold=(q4 != 3))

        pjv = rstate.get("pj", 0)
        rstate["pj"] = (pjv + 1) % NROT
        GP, bGP = PJR[pjv], B[PJN[pjv]]
        for gi in range(2):
            for kc in range(8):
                P.op("pe", lambda e, gi=gi, kc=kc: e.matmul(out=GP[0:4, gi * 128:gi * 128 + nt], lhsT=wg[:, kc, 56 + gi * 4:60 + gi * 4],
                                                            rhs=hnT[:, kc, 0:nt], start=(kc == 0), stop=(kc == 7)),
                     reads=[B["wg"], BhnT], writes=[bGP], hold=True)
        P.op("act", lambda e: e.activation(out=igT[:, 0:nt], in_=GP[0:4, 0:nt], func=AF.Identity, bias=vec[0:4, 32:33], scale=1.0),
             reads=[bGP, B["vec"]], writes=[B["igT"]], hold=True)
        P.op("act", lambda e: e.activation(out=efT[:, 0:nt], in_=GP[0:4, 128:128 + nt], func=AF.Exp, bias=ngbf[:, 0:1], scale=-1.0),
             reads=[bGP, B["ngbf"]], writes=[B["efT"]])
        P.op("act", lambda e: e.activation(out=cs[0][:, 0:nt], in_=efT[:, 0:nt], func=AF.Ln, bias=1.0, scale=1.0),
             reads=[B["efT"]], writes=[B["cs0"]])
        cur = 0
        sh = 1
        while sh < L:
            src, dst = cs[cur], cs[1 - cur]
            sv = src[:, 0:nt].rearrange("p (s l) -> p s l", l=L)
            dv = dst[:, 0:nt].rearrange("p (s l) -> p s l", l=L)
            P.op("dve", lambda e, sv=sv, dv=dv, sh=sh: e.tensor_copy(out=dv[:, :, 0:sh], in_=sv[:, :, 0:sh]),
                 reads=[B["cs%d" % cur]], writes=[B["cs%d" % (1 - cur)]])
            P.op("dve", lambda e, sv=sv, dv=dv, sh=sh: e.tensor_tensor(out=dv[:, :, sh:L], in0=sv[:, :, sh:L], in1=sv[:, :, 0:L - sh], op=ALU.add),
                 reads=[B["cs%d" % cur]], writes=[B["cs%d" % (1 - cur)]])
            cur = 1 - cur
            sh *= 2
        nbT, BnbT = cs[cur], B["cs%d" % cur]
        P.op("dve", lambda e: e.tensor_tensor(out=gT[:, 0:nt], in0=igT[:, 0:nt], in1=nbT[:, 0:nt], op=ALU.add),
             reads=[B["igT"], BnbT], writes=[B["gT"]])
        P.op("dve", lambda e: e.tensor_reduce(out=gmax[:, 0:nseq], in_=gT[:, 0:nt].rearrange("p (s l) -> p s l", l=L), axis=AX.X, op=ALU.max),
             reads=[B["gT"]], writes=[B["gmax"]])
        P.op("dve", lambda e: e.tensor_tensor(out=mxT[:, 0:nseq], in0=gmax[:, 0:nseq], in1=mprevT[:, 0:nseq], op=ALU.max),
             reads=[B["gmax"], B["mprevT"]], writes=[B["mxT"]])
        nbL = nbT[:, 0:nt].rearrange("p (s l) -> p s l", l=L)[:, :, L - 1]
        P.op("dve", lambda e: e.tensor_tensor(out=mnewT[:, 0:nseq], in0=mxT[:, 0:nseq], in1=nbL, op=ALU.subtract),
             reads=[B["mxT"], BnbT], writes=[B["mnewT"]])
        P.op("dve", lambda e: e.tensor_tensor(out=decT[:, 0:nseq], in0=mprevT[:, 0:nseq], in1=mxT[:, 0:nseq], op=ALU.subtract),
             reads=[B["mxT"], B["mprevT"]], writes=[B["decT"]])
        P.op("act", lambda e: e.activation(out=decT[:, 0:nseq], in_=decT[:, 0:nseq], func=AF.Exp), reads=[B["decT"]], writes=[B["decT"]])

        def bc(t):
            return t[:, 0:nseq].unsqueeze(2).to_broadcast([4, nseq, L])

        g3v = gT[:, 0:nt].rearrange("p (s l) -> p s l", l=L)
        a3v = argT[:, 0:nt].rearrange("p (s l) -> p s l", l=L)
        nb3v = nbT[:, 0:nt].rearrange("p (s l) -> p s l", l=L)
        for (row, in0v, subt, bias, rd) in ((0, g3v, mprevT, LNK, [B["gT"], B["mprevT"]]),
                                             (32, g3v, mxT, LNK, [B["gT"], B["mxT"]]),
                                             (64, nb3v, mprevT, 0.0, [BnbT, B["mprevT"]])):
            P.op("dve", lambda e, in0v=in0v, subt=subt: e.tensor_tensor(out=a3v, in0=in0v, in1=bc(subt), op=ALU.subtract),
                 reads=rd, writes=[B["argT"]])
            if bias != 0.0:
                P.op("dve", lambda e, bias=bias: e.tensor_scalar(out=argT[:, 0:nt], in0=argT[:, 0:nt], scalar1=bias, scalar2=None, op0=ALU.add),
                     reads=[B["argT"]], writes=[B["argT"]])
            P.op("act", lambda e, row=row: e.activation(out=G3[row:row + 4, 0:nt], in_=argT[:, 0:nt], func=AF.Exp),
                 reads=[B["argT"]], writes=[B["G3"]])
        yield "A"
        P.op("pe", lambda e: e.transpose(out=STp[0:nt, 256:352], in_=G3[0:96, 0:nt], identity=ident[0:96, 0:96]),
             reads=[B["G3"], B["cf"]], writes=[B["STp"]])
        P.op("dve", lambda e: e.tensor_copy(out=G3T[0:nt, :], in_=STp[0:nt, 256:352]), reads=[B["STp"]], writes=[B["G3T"]])
        I4 = cfs("I4")
        for (off, src, bsrc) in ((0, decT, B["decT"]), (64, mnewT, B["mnewT"])):
            P.op("dve", lambda e, off=off, src=src: e.tensor_tensor(
                out=DM[:, off:off + 4 * nseq].rearrange("p (s h) -> p s h", h=4),
                in0=src[:, 0:nseq].unsqueeze(2).to_broadcast([4, nseq, 4]),
                in1=I4.unsqueeze(1).to_broadcast([4, nseq, 4]), op=ALU.mult),
                reads=[bsrc, B["cf"]], writes=[B["DM"]])
        P.op("pe", lambda e: e.matmul(out=STp[:, 384:512], lhsT=cfs("ones4"), rhs=DM[:, :], start=True, stop=True),
             reads=[B["DM"], B["cf"]], writes=[B["STp"]])
        P.op("act", lambda e: e.activation(out=decb[:, :], in_=STp[:, 384:512], func=AF.Copy), reads=[B["STp"]], writes=[B["decb"]])
        if kind != "S":
            P.op("dve", lambda e: e.tensor_copy(out=mprevT[:, 0:1], in_=mnewT[:, 0:1]), reads=[B["mnewT"]], writes=[B["mprevT"]])

        yield "B"
        def proj_tok(key, evac):
            w, bw = load_block(key)
            pjv = rstate.get("pj", 0)
            rstate["pj"] = (pjv + 1) % NROT
            pj, bpj = PJR[pjv], B[PJN[pjv]]
            for kc in range(8):
                P.op("pe", lambda e, kc=kc, w=w, pj=pj: e.matmul(out=pj[0:nt, :], lhsT=hnT[:, kc, 0:nt], rhs=w[:, kc, :],
                                                               start=(kc == 0), stop=(kc == 7)),
                     reads=[BhnT, bw], writes=[bpj])
            evac(pj, bpj)
            tick()
            return w, bw

        def proj_fm(w, bw, evac):
            pjv = rstate.get("pj", 0)
            rstate["pj"] = (pjv + 1) % NROT
            pj, bpj = PJR[pjv], B[PJN[pjv]]
            for cc in range(4):
                for kc in range(8):
                    P.op("pe", lambda e, cc=cc, kc=kc, w=w, pj=pj: e.matmul(out=pj[:, cc * nt:(cc + 1) * nt], lhsT=w[:, kc, cc * 128:(cc + 1) * 128],
                                                                          rhs=hnT[:, kc, 0:nt], start=(kc == 0), stop=(kc == 7)),
                         reads=[BhnT, bw], writes=[bpj])
            evac(pj, bpj)
            tick()

        def head_proj(h):
            hb = h % NHB
            def tok_then_T(key, tokt, btok, dstT, bdst, extra_evac=None):
                w, bw = load_block(key)
                pjv = rstate.get("pj", 0)
                rstate["pj"] = (pjv + 1) % NROT
                pj, bpj = PJR[pjv], B[PJN[pjv]]
                for kc in range(8):
                    P.op("pe", lambda e, kc=kc, w=w, pj=pj: e.matmul(out=pj[0:nt, :], lhsT=hnT[:, kc, 0:nt], rhs=w[:, kc, :],
                                                                   start=(kc == 0), stop=(kc == 7)),
                         reads=[BhnT, bw], writes=[bpj])
                P.op("dve", lambda e, pj=pj: e.tensor_copy(out=tokt[0:nt, :], in_=pj[0:nt, :]), reads=[bpj], writes=[btok])
                if extra_evac is not None:
                    extra_evac(pj, bpj)
                tick()
                for dc in range(4):
                    P.op("pe", lambda e, dc=dc: e.transpose(out=TP[:, 512 + dc * nt:512 + (dc + 1) * nt], in_=tokt[0:nt, dc * 128:(dc + 1) * 128],
                                                            identity=cbs_ident[0:nt, 0:nt]),
                         reads=[btok, B["cb"]], writes=[B["TP"]])
                P.op("act", lambda e: e.activation(out=dstT[:, :, 0:nt], in_=TP[:, 512:512 + 4 * nt].rearrange("p (c t) -> p c t", t=nt), func=AF.Copy),
                     reads=[B["TP"]], writes=[bdst])
                tick()

            tok_then_T(("q", h), qtok[hb], B["qtok%d" % hb], qT[hb], B["qT%d" % hb])
            tok_then_T(("k", h), ktok[hb], B["ktok%d" % hb], kT[hb], B["kT%d" % hb],
                       extra_evac=lambda pj, bpj: P.op("act", lambda e: e.activation(
                           out=kw2[hb][0:nt, :], in_=pj[0:nt, :], func=AF.Copy, scale=G3T[0:nt, 32 + h:33 + h]),
                           reads=[bpj, B["G3T"]], writes=[B["kw2%d" % hb]]))
            proj_tok(("v", h), lambda pj, bpj: P.op("dve", lambda e: e.tensor_copy(out=vtok[hb][0:nt, :], in_=pj[0:nt, :]),
                                                    reads=[bpj], writes=[B["vtok%d" % hb]]))
            proj_tok(("o", h), lambda pj, bpj: P.op("act", lambda e: e.activation(out=GR[hb][0:nt, :], in_=pj[0:nt, :], func=AF.Sigmoid),
                                                    reads=[bpj], writes=[B["GR%d" % hb]]))

            def ev_z(pj, bpj):
                P.op("act", lambda e: e.activation(out=Gt[hb][0:nt, :], in_=pj[0:nt, :], func=AF.Silu),
                     reads=[bpj], writes=[B["G%d" % hb]])
                P.op("pool", lambda e: e.tensor_tensor(out=Gt[hb][0:nt, :], in0=Gt[hb][0:nt, :], in1=GR[hb][0:nt, :], op=ALU.mult),
                     reads=[B["G%d" % hb], B["GR%d" % hb]], writes=[B["G%d" % hb]])
            proj_tok(("z", h), ev_z)

        def head_mlstm(h):
            hb = h % NHB
            nm, bnm = NUM[0], B["NUM0"]
            def state_update(s):
                if kind == "S":
                    slot = (h * 16 + s) % NH
                    cst, bcst = Cst[slot], B["Cst%d" % slot]
                    vi = (h * 16 + s) % 2
                    P.op("act", lambda e, vi=vi, s=s: e.activation(out=vm[vi][:, :], in_=vtok[hb][0:64, :], func=AF.Copy, scale=cfs("ETf")[0:64, s:s + 1]),
                         reads=[B["vtok%d" % hb], B["cf"]], writes=[B["vm%d" % vi]])
                    rv, brv = vm[vi], B["vm%d" % vi]
                else:
                    cst, bcst = Cst[h], B["Cst%d" % h]
                    rv, brv = vtok[hb], B["vtok%d" % hb]
                dcol = s * 4 + h
                for dc in range(4):
                    cuv = rstate.get("cu", 0)
                    rstate["cu"] = 1 - cuv
                    cu, bcu = CU[cuv], B["CU%d" % cuv]
                    P.op("pe", lambda e, dc=dc, cu=cu, rv=rv: e.matmul(out=cu[:, :], lhsT=kw2[hb][0:nt, dc * 128:(dc + 1) * 128], rhs=rv[0:nt, :],
                                                                      start=True, stop=True),
                         reads=[B["kw2%d" % hb], brv], writes=[bcu])
                    if zero_state:
                        P.op("dve", lambda e, dc=dc, cu=cu, cst=cst: e.tensor_copy(out=cst[:, dc, :], in_=cu[:, :]), reads=[bcu], writes=[bcst])
                    else:
                        P.op("dve", lambda e, dc=dc, cu=cu, cst=cst: e.scalar_tensor_tensor(out=cst[:, dc, :], in0=cst[:, dc, :], scalar=decb[:, dcol:dcol + 1],
                                                                                          in1=cu[:, :], op0=ALU.mult, op1=ALU.add),
                             reads=[bcu, bcst, B["decb"]], writes=[bcst])
                if kind == "S":
                    P.dma("pool", lambda e, s=s, cst=cst: e.dma_start(out=C_s[s, h].rearrange("(dc p) v -> p dc v", p=128), in_=cst[:]),
                          cout_ds[slot], reads=[bcst])

            def state_cast():
                P.op("pool", lambda e: e.tensor_copy(out=Cbf[h][:], in_=Cst[h][:]), reads=[B["Cst%d" % h]], writes=[B["Cbf%d" % h]])

            for dc in range(4):
                P.op("pe", lambda e, dc=dc: e.matmul(out=STp[0:nt, 0:nt], lhsT=kT[hb][:, dc, 0:nt], rhs=qT[hb][:, dc, 0:nt],
                                                     start=(dc == 0), stop=(dc == 3)),
                     reads=[B["kT%d" % hb], B["qT%d" % hb]], writes=[B["STp"]])
            P.op("dve", lambda e: e.scalar_tensor_tensor(out=SWT[hb][0:nt, 0:nt], in0=STp[0:nt, 0:nt], scalar=G3T[0:nt, h:h + 1],
                                                         in1=maskT, op0=ALU.mult, op1=ALU.mult),
                 reads=[B["STp"], B["G3T"], B["cf"]], writes=[B["SWT%d" % hb]])
            tick()
            ETb = cbs("ET" + kind)
            for dc in range(4):
                P.op("pe", lambda e, dc=dc: e.matmul(out=STp[:, 208 + dc * 16:208 + dc * 16 + nseq], lhsT=kw2[hb][0:nt, dc * 128:(dc + 1) * 128],
                                                     rhs=ETb[0:nt, 0:nseq], start=True, stop=True),
                     reads=[B["kw2%d" % hb], B["cb"]], writes=[B["STp"]])
            nv = nst[:, h * 4 * nseq:(h + 1) * 4 * nseq].rearrange("p (c s) -> p c s", s=nseq) if kind == "S" else \
                nst[:, h * 4:(h + 1) * 4].unsqueeze(2)
            pv = STp[:, 208:272].rearrange("p (c s) -> p c s", s=16)[:, :, 0:nseq]
            if zero_state:
                P.op("dve", lambda e: e.tensor_copy(out=nv, in_=pv), reads=[B["STp"]], writes=[B["nst"]])
            else:
                dv_ = decb[:, 0:4 * nseq].rearrange("p (s h) -> p h s", h=4)[:, h, :].unsqueeze(1).to_broadcast([128, 4, nseq])
                P.op("dve", lambda e: e.tensor_tensor(out=nv, in0=nv, in1=dv_, op=ALU.mult), reads=[B["nst"], B["decb"]], writes=[B["nst"]])
                P.op("dve", lambda e: e.tensor_tensor(out=nv, in0=nv, in1=pv, op=ALU.add), reads=[B["nst"], B["STp"]], writes=[B["nst"]])
            tick()
            ones_b = cbs("ones")
            n_den = 1 + (0 if zero_state else 4 * nseq)
            idx = [0]

            def den_mm(lhsT, rhs, reads):
                i = idx[0]
                idx[0] += 1
                P.op("pe", lambda e: e.matmul(out=STp[0:nt, 200:201], lhsT=lhsT, rhs=rhs, start=(i == 0), stop=(i == n_den - 1)),
                     reads=reads, writes=[B["STp"]])

            if kind != "S":
                state_update(0)
                tick(1)
            n_num = 1 + (0 if zero_state else 4 * nseq)
            P.op("pe", lambda e: e.matmul(out=nm[0:nt, :], lhsT=SWT[hb][0:nt, 0:nt], rhs=vtok[hb][0:nt, :], start=True, stop=(n_num == 1)),
                 reads=[B["SWT%d" % hb], B["vtok%d" % hb]], writes=[bnm])
            den_mm(SWT[hb][0:nt, 0:nt], ones_b[0:nt, 0:1], [B["SWT%d" % hb], B["cb"]])
            cnt = 1
            for s in range(nseq):
                if zero_state:
                    break
                if kind == "S":
                    slot = (h * 16 + s) % NH
                    cst, bcst, cbf, bcbf = Cst[slot], B["Cst%d" % slot], Cbf[slot], B["Cbf%d" % slot]
                    P.dma("sp", lambda e, s=s, cst=cst: e.dma_start(out=cst[:], in_=sC[s, h].rearrange("(dc p) v -> p dc v", p=128)),
                          c_ds[slot], writes=[bcst])
                    P.op("act", lambda e, cst=cst, cbf=cbf: e.activation(out=cbf[:], in_=cst[:], func=AF.Copy), reads=[bcst], writes=[bcbf])
                    qi = (h * 16 + s) % 2
                    cmv = cbs("colmask")[:, s * 64:(s + 1) * 64]
                    P.op("dve", lambda e, qi=qi, cmv=cmv: e.tensor_tensor(out=qTm[qi][:, :, :], in0=qT[hb][:, :, 0:64],
                                                                          in1=cmv.unsqueeze(1).to_broadcast([128, 4, 64]), op=ALU.mult),
                         reads=[B["qT%d" % hb], B["cb"]], writes=[B["qTm%d" % qi]])
                    lq, blq = qTm[qi], B["qTm%d" % qi]
                else:
                    cst, bcst, cbf, bcbf = Cst[h], B["Cst%d" % h], Cbf[h], B["Cbf%d" % h]
                    lq, blq = qT[hb], B["qT%d" % hb]
                for dc in range(4):
                    cnt += 1
                    P.op("pe", lambda e, dc=dc, lq=lq, cbf=cbf, last=(cnt == n_num): e.matmul(
                        out=nm[0:nt, :], lhsT=lq[:, dc, 0:nt], rhs=cbf[:, dc, :], start=False, stop=last),
                        reads=[blq, bcbf], writes=[bnm])
                    ncol = (h * 4 + dc) * nseq + s if kind == "S" else (h * 4 + dc)
                    den_mm(lq[:, dc, 0:nt], nbf[:, ncol:ncol + 1], [blq, B["nbf"]])
                if kind == "S":
                    state_update(s)
            if kind != "S":
                state_cast()
            tick(1)
            c0 = 4 + 3 * h
            P.op("dve", lambda e: e.tensor_scalar(out=sm1[0:nt, c0 + 1:c0 + 2], in0=STp[0:nt, 200:201], scalar1=-1.0, scalar2=None, op0=ALU.mult),
                 reads=[B["STp"]], writes=[B["sm1h%d" % h]])
            P.op("dve", lambda e: e.scalar_tensor_tensor(out=sm1[0:nt, c0:c0 + 1], in0=STp[0:nt, 200:201], scalar=G3T[0:nt, 64 + h:65 + h],
                                                         in1=sm1[0:nt, c0 + 1:c0 + 2], op0=ALU.max, op1=ALU.max),
                 reads=[B["STp"], B["G3T"], B["sm1h%d" % h]], writes=[B["sm1h%d" % h]])
            P.op("dve", lambda e: e.bn_stats(out=st6[0:nt, 0:6], in_=nm[0:nt, :]), reads=[bnm], writes=[B["st6"]])
            P.op("dve", lambda e: e.bn_aggr(out=st6[0:nt, 6:8], in_=st6[0:nt, 0:6]), reads=[B["st6"]], writes=[B["st6"]])
            tick()
            P.op("dve", lambda e: e.tensor_scalar(out=sm1[0:nt, c0 + 1:c0 + 2], in0=sm1[0:nt, c0:c0 + 1], scalar1=sm1[0:nt, c0:c0 + 1], scalar2=EPS,
                                                  op0=ALU.mult, op1=ALU.mult),
                 reads=[B["sm1h%d" % h]], writes=[B["sm1h%d" % h]])
            P.op("dve", lambda e: e.tensor_scalar(out=sm1[0:nt, c0 + 1:c0 + 2], in0=sm1[0:nt, c0 + 1:c0 + 2], scalar1=st6[0:nt, 7:8], scalar2=None, op0=ALU.add),
                 reads=[B["sm1h%d" % h], B["st6"]], writes=[B["sm1h%d" % h]])
            P.op("act", lambda e: e.activation(out=sm1[0:nt, c0 + 1:c0 + 2], in_=sm1[0:nt, c0 + 1:c0 + 2], func=AF.Ln),
                 reads=[B["sm1h%d" % h]], writes=[B["sm1h%d" % h]])
            P.op("act", lambda e: e.activation(out=sm1[0:nt, c0 + 2:c0 + 3], in_=sm1[0:nt, c0 + 1:c0 + 2], func=AF.Exp, scale=-0.5),
                 reads=[B["sm1h%d" % h]], writes=[B["sm1h%d" % h]])
            P.op("act", lambda e: e.activation(out=GR[hb][0:nt, :], in_=Gt[hb][0:nt, :], func=AF.Copy, scale=sm1[0:nt, c0 + 2:c0 + 3]),
                 reads=[B["sm1h%d" % h], B["G%d" % hb]], writes=[B["GR%d" % hb]])
            P.op("dve", lambda e: e.scalar_tensor_tensor(out=yln[0:nt, h * 512:(h + 1) * 512], in0=nm[0:nt, :], scalar=st6[0:nt, 6:7],
                                                         in1=GR[hb][0:nt, :], op0=ALU.subtract, op1=ALU.mult),
                 reads=[bnm, B["st6"], B["GR%d" % hb]], writes=[B["yln%d" % h]])
            tick(1)
        def ytrans(h):
            identb = cbs_ident
            for mc in range(4):
                P.op("pe", lambda e, mc=mc: e.transpose(out=TP[:, mc * nt:(mc + 1) * nt], in_=yln[0:nt, h * 512 + mc * 128:h * 512 + (mc + 1) * 128],
                                                        identity=identb[0:nt, 0:nt]),
                     reads=[B["yln%d" % h], B["cb"]], writes=[B["TP"]])
            for mc in range(4):
                P.op("act", lambda e, mc=mc: e.activation(out=yT[:, h * 4 + mc, 0:nt], in_=TP[:, mc * nt:(mc + 1) * nt], func=AF.Copy,
                                                          scale=vec[:, 16 + h * 4 + mc:17 + h * 4 + mc]),
                     reads=[B["TP"], B["vec"]], writes=[B["yT%d" % h]])

        cbs_ident = cbs("identb")
        B.setdefault("identb", B["cb"])

        if kind == "S":
            P.dma("sp", lambda e: e.dma_start(out=nst[:], in_=snT[:, :]), misc_ds[8], writes=[B["nst"]])
            P.op("act", lambda e: e.activation(out=nbf[:], in_=nst[:], func=AF.Copy), reads=[B["nst"]], writes=[B["nbf"]])
        head_proj(0)
        head_proj(1)
        head_mlstm(0)
        head_proj(2)
        yield "C1"
        head_mlstm(1)
        ytrans(0)
        head_proj(3)
        head_mlstm(2)
        ytrans(1)
        head_mlstm(3)
        ytrans(2)
        yield "C2"
        if kind != "S":
            P.op("act", lambda e: e.activation(out=nbf[:, 0:16], in_=nst[:, 0:16], func=AF.Copy), reads=[B["nst"]], writes=[B["nbf"]])

        for j in range(2):
            def ev_a(pj, bpj, j=j):
                P.op("act", lambda e: e.activation(out=A[0:nt, j * 512:(j + 1) * 512], in_=pj[0:nt, :], func=AF.Copy), reads=[bpj], writes=[BA])
                P.op("dve", lambda e: e.tensor_copy(out=af32[0:nt, j * 512:(j + 1) * 512], in_=pj[0:nt, :]), reads=[bpj], writes=[B["af32"]])
            proj_tok(("a", j), ev_a)
        if kind == "M":
            yield "D"
            return
        for j in range(2):
            w, bw = load_block(("zp", j))
            proj_fm(w, bw, lambda pj, bpj, j=j: P.op("act", lambda e: e.activation(
                out=szpT[:, 4 * j:4 * j + 4, 0:nt], in_=pj[:, 0:4 * nt].rearrange("p (c t) -> p c t", t=nt), func=AF.Silu),
                reads=[bpj], writes=[B["szpT"]]))
        for j in range(2):
            proj_tok(("gp", j), lambda pj, bpj, j=j: P.op("act", lambda e: e.activation(out=upg[0:nt, j * 512:(j + 1) * 512], in_=pj[0:nt, :], func=AF.Sigmoid),
                                                          reads=[bpj], writes=[B["upg"]]))
        for j in range(2):
            proj_tok(("gm", j), lambda pj, bpj, j=j: P.op("act", lambda e: e.activation(out=sgm[0:nt, j * 512:(j + 1) * 512], in_=pj[0:nt, :], func=AF.Sigmoid),
                                                          reads=[bpj], writes=[B["sgm"]]))
        if kind == "S":
            hists = [(hs[0], B["hs0"], 120, "phS%d_0"), (hs[1], B["hs1"], 120, "phS%d_1")]
            pck = "pcS%d"
        elif kind == "M":
            hists = []
            pck = "pcM%d"
        else:
            pck = "pcP%d"
            if ci == 0:
                hists = [(Aprev, BAprev, 16, "phP0%d_0")] if with_meta else []
            else:
                hists = [(Aprev, BAprev, 128, "phP%d_0")]
        for half in range(2):
            pjv = rstate.get("pj", 0)
            rstate["pj"] = (pjv + 1) % NROT
            pj, bpj = PJR[pjv], B[PJN[pjv]]
            for c4 in range(4):
                cc = half * 4 + c4
                g = cc // 2
                terms = [(A, BA, nt, pck % g)] + [(ht, bh, kk, nm_ % g) for (ht, bh, kk, nm_) in hists]
                for ti, (src, bsrc, kk, pname) in enumerate(terms):
                    P.op("pe", lambda e, c4=c4, cc=cc, src=src, kk=kk, pname=pname, ti=ti, nterm=len(terms), pj=pj: e.matmul(
                        out=pj[:, c4 * nt:(c4 + 1) * nt], lhsT=src[0:kk, cc * 128:(cc + 1) * 128], rhs=cbs(pname)[0:kk, 0:nt],
                        start=(ti == 0), stop=(ti == nterm - 1)),
                        reads=[bsrc, B["cb"]], writes=[bpj])
            P.op("dve", lambda e, half=half, pj=pj: e.tensor_copy(out=pooledT[:, 4 * half:4 * half + 4, 0:nt],
                                                                  in_=pj[:, 0:4 * nt].rearrange("p (c t) -> p c t", t=nt)),
                 reads=[bpj], writes=[B["pooledT"]])
        for half in range(2):
            pjv = rstate.get("pj", 0)
            rstate["pj"] = (pjv + 1) % NROT
            pj, bpj = PJR[pjv], B[PJN[pjv]]
            for d4 in range(4):
                dc = half * 4 + d4
                g, dl = dc // 2, dc % 2
                for cl in range(2):
                    P.op("pe", lambda e, d4=d4, g=g, dl=dl, cl=cl, pj=pj: e.matmul(
                        out=pj[:, d4 * nt:(d4 + 1) * nt], lhsT=wpl[:, g, cl, dl * 128:(dl + 1) * 128], rhs=pooledT[:, 2 * g + cl, 0:nt],
                        start=(cl == 0), stop=(cl == 1)),
                        reads=[B["wpl"], B["pooledT"]], writes=[bpj])
            for d4 in range(4):
                dc = half * 4 + d4
                P.op("dve", lambda e, d4=d4, dc=dc, pj=pj: e.scalar_tensor_tensor(
                    out=ypT[:, dc, 0:nt], in0=pj[:, d4 * nt:(d4 + 1) * nt], scalar=vec[:, 8 + dc:9 + dc], in1=szpT[:, dc, 0:nt],
                    op0=ALU.mult, op1=ALU.mult),
                    reads=[bpj, B["vec"], B["szpT"]], writes=[B["ypT"]])
        for j in range(2):
            w, bw = load_block(("bp", j))
            pjv = rstate.get("pj", 0)
            rstate["pj"] = (pjv + 1) % NROT
            pj, bpj = PJR[pjv], B[PJN[pjv]]
            for pc in range(8):
                P.op("pe", lambda e, pc=pc, w=w, pj=pj: e.matmul(out=pj[0:nt, :], lhsT=ypT[:, pc, 0:nt], rhs=w[:, pc, :], start=(pc == 0), stop=(pc == 7)),
                     reads=[B["ypT"], bw], writes=[bpj])
            P.op("dve", lambda e, j=j, pj=pj: e.tensor_tensor(out=upg[0:nt, j * 512:(j + 1) * 512], in0=pj[0:nt, :], in1=upg[0:nt, j * 512:(j + 1) * 512], op=ALU.mult),
                 reads=[bpj, B["upg"]], writes=[B["upg"]])
        yield "D"
        ytrans(3)
        for j in range(2):
            pjv = rstate.get("pj", 0)
            rstate["pj"] = (pjv + 1) % NROT
            pj, bpj = PJR[pjv], B[PJN[pjv]]
            for mh in range(2):
                w, bw = load_block(("bm", j, mh))
                for m8 in range(8):
                    mc = mh * 8 + m8
                    P.op("pe", lambda e, mc=mc, m8=m8, w=w, pj=pj: e.matmul(out=pj[0:nt, :], lhsT=yT[:, mc, 0:nt], rhs=w[:, m8, :],
                                                                          start=(mc == 0), stop=(mc == 15)),
                         reads=[B["yT%d" % (mc // 4)], bw], writes=[bpj])
            P.op("dve", lambda e, j=j, pj=pj: e.tensor_tensor(out=sgm[0:nt, j * 512:(j + 1) * 512], in0=pj[0:nt, :], in1=sgm[0:nt, j * 512:(j + 1) * 512], op=ALU.mult),
                 reads=[bpj, B["sgm"]], writes=[B["sgm"]])
            P.op("pool", lambda e, j=j: e.tensor_tensor(out=ubf[0:nt, j * 512:(j + 1) * 512], in0=sgm[0:nt, j * 512:(j + 1) * 512],
                                                        in1=upg[0:nt, j * 512:(j + 1) * 512], op=ALU.add),
                 reads=[B["sgm"], B["upg"]], writes=[B["ubf"]])
        for dc in range(8):
            P.op("pe", lambda e, dc=dc: e.transpose(out=TP[:, dc * nt:(dc + 1) * nt], in_=ubf[0:nt, dc * 128:(dc + 1) * 128], identity=cbs_ident[0:nt, 0:nt]),
                 reads=[B["ubf"], B["cb"]], writes=[B["TP"]])
        P.op("act", lambda e: e.activation(out=uT[:, :, 0:nt], in_=TP[:, 0:8 * nt].rearrange("p (c t) -> p c t", t=nt), func=AF.Copy),
             reads=[B["TP"]], writes=[B["uT"]])
        for j in range(2):
            w, bw = load_block(("wo", j))
            pjv = rstate.get("pj", 0)
            rstate["pj"] = (pjv + 1) % NROT
            pj, bpj = PJR[pjv], B[PJN[pjv]]
            for dc in range(8):
                P.op("pe", lambda e, dc=dc, w=w, pj=pj: e.matmul(out=pj[0:nt, :], lhsT=uT[:, dc, 0:nt], rhs=w[:, dc, :], start=(dc == 0), stop=(dc == 7)),
                     reads=[B["uT"], bw], writes=[bpj])
            P.op("dve", lambda e, j=j, pj=pj: e.tensor_tensor(out=xo[0:nt, j * 512:(j + 1) * 512], in0=pj[0:nt, :], in1=X[0:nt, j * 512:(j + 1) * 512], op=ALU.add),
                 reads=[bpj, BX], writes=[B["xo"]])
        P.op("act", lambda e: e.activation(out=upg[0:nt, :], in_=xo[0:nt, :], func=AF.Square, accum_out=sm1[0:nt, 3:4]),
             reads=[B["xo"]], writes=[B["upg"], B["sm1t"]])
        P.op("act", lambda e: e.activation(out=sm1[0:nt, 3:4], in_=sm1[0:nt, 3:4], func=AF.Ln, scale=1.0 / D, bias=EPS),
             reads=[B["sm1t"]], writes=[B["sm1t"]])
        P.op("act", lambda e: e.activation(out=sm1[0:nt, 3:4], in_=sm1[0:nt, 3:4], func=AF.Exp, scale=-0.5),
             reads=[B["sm1t"]], writes=[B["sm1t"]])
        if y_dst is not None:
            P.op("dve", lambda e: e.scalar_tensor_tensor(out=xo[0:nt, :], in0=xo[0:nt, :], scalar=sm1[0:nt, 3:4], in1=fg[0:nt, :],
                                                         op0=ALU.mult, op1=ALU.mult),
                 reads=[B["xo"], B["sm1t"], B["fg"]], writes=[B["xo"]])
            P.dma("pool", lambda e: e.dma_start(out=y_dst, in_=xo[0:nt, :]), yo_ds, reads=[B["xo"]])

    passes = []
    if with_meta:
        passes.append(("M", meta[:, :], None, 0))
    for ci in range(nchunk):
        passes.append(("P", xp[ci * 128:(ci + 1) * 128, :], y_p[ci * 128:(ci + 1) * 128, :], ci))
    if with_sample:
        passes.append(("S", xs[:, :], y_s[:, :], 0))
        P.dma("pool", lambda e: e.dma_start(out=pool_s[:, 0:11, :], in_=spool.ap().rearrange("(s j) d -> s j d", j=15)[:, 4:15, :]),
              out_ds)
    gens = [run_pass(*p) for p in passes]
    last_p = max([i for i, p in enumerate(passes) if p[0] == "P"])

    def adv(i):
        try:
            next(gens[i])
        except StopIteration:
            pass

    def after_C2(i):
        if passes[i][0] == "S":
            P.dma("pool", lambda e: e.dma_start(out=nT_s[:, :], in_=nst[:, :]), o_ds["nTs"], reads=[B["nst"]])
            P.dma("pool", lambda e: e.dma_start(out=m_s[:, :], in_=decb[0:1, 64:128]), o_ds["ms"], reads=[B["decb"]])
        if i == last_p:
            for h in range(NH):
                P.dma("pool", lambda e, h=h: e.dma_start(out=C_p[h].rearrange("(dc p) v -> p dc v", p=128), in_=Cst[h][:]), cout_ds[h],
                      reads=[B["Cst%d" % h]])
            P.dma("pool", lambda e: e.dma_start(out=nT_p[:, :], in_=nst[:, 0:64]), o_ds["nTp"], reads=[B["nst"]])
            P.dma("pool", lambda e: e.dma_start(out=m_p[:, :], in_=decb[0:1, 64:128]), o_ds["mp"], reads=[B["decb"]])

    def after_D(i):
        if i == last_p:
            P.dma("pool", lambda e: e.dma_start(out=pool_p[:, :], in_=af32[113:128, :]), o_ds["pp"], reads=[B["af32"]])

    def after_E(i):
        if passes[i][0] == "S":
            P.dma("pool", lambda e: [e.dma_start(out=pool_s[s, 11:15, :], in_=af32[4 * s:4 * s + 4, :]) for s in range(16)],
                  o_ds["ps"], reads=[B["af32"]], n=16)

    npass = len(gens)

    def genA(i):
        P.defer_mode = True
        adv(i)
        P.defer_mode = False

    genA(0)
    P.flush()
    adv(0)
    for i in range(npass):
        if i + 1 < npass:
            genA(i + 1)
        adv(i)
        adv(i)
        P.flush()
        after_C2(i)
        adv(i)
        after_D(i)
        if i + 1 < npass:
            adv(i + 1)
        adv(i)
        after_E(i)
    P.emit()
    return nc, (cf_np, cb_np), P


def _core_inputs(b, x_prompt, x_sample, state_pool, state_C, state_n, state_m, meta_tokens, norm_g, w_in, gate_bias,
                 w_pool, pool_scale, mh_norm_g, w_branch_pool, w_branch_mlstm, w_out, final_g, consts, nchunk=16):
    cf_np, cb_np = consts
    f = np.float32
    sl = slice(16 * b, 16 * b + 16)
    vecs = np.zeros((128, 128), f)
    vecs[:, 0:8] = np.asarray(norm_g[0], f).reshape(8, 128).T
    vecs[:, 8:16] = np.asarray(pool_scale[0], f).reshape(8, 128).T
    vecs[:, 16:32] = np.asarray(mh_norm_g[0], f).reshape(16, 128).T
    vecs[0:4, 32] = np.asarray(gate_bias[0, 0:4], f)
    vecs[0:4, 33] = np.asarray(gate_bias[0, 4:8], f)
    sn = np.asarray(state_n[0, sl], f)
    snT = np.ascontiguousarray(sn.reshape(16, 4, 4, 128).transpose(3, 1, 2, 0)).reshape(128, 256)
    smT = np.zeros((4, 64), f)
    smT[:, 0:16] = np.asarray(state_m[0, sl], f).T
    return {
        "xp": np.ascontiguousarray(x_prompt[b][:128 * nchunk], f),
        "xs": np.ascontiguousarray(np.asarray(x_sample[sl], f).reshape(64, D)),
        "spool": np.ascontiguousarray(np.asarray(state_pool[0, sl], f).reshape(240, D)),
        "sC": np.ascontiguousarray(state_C[0, sl], f),
        "snT": snT, "smT": smT,
        "meta": np.ascontiguousarray(meta_tokens, f),
        "w_in": np.ascontiguousarray(w_in[0], f),
        "w_pool": np.ascontiguousarray(w_pool[0], f),
        "wbp": np.ascontiguousarray(w_branch_pool[0], f),
        "wbm": np.ascontiguousarray(w_branch_mlstm[0], f),
        "wout": np.ascontiguousarray(w_out[0], f),
        "vecs": vecs,
        "fgb": np.ascontiguousarray(np.asarray(final_g, f).reshape(1, D)),
        "cf": cf_np, "cb": cb_np,
    }


def _gather(results, nb):
    f = np.float32
    y_p = np.stack([r["y_p"] for r in results]).astype(f)
    y_s = np.concatenate([r["y_s"].reshape(16, 4, D) for r in results]).astype(f)
    pool_p = np.stack([r["pool_p"] for r in results])[None].astype(f)
    C_p = np.stack([r["C_p"] for r in results])[None].astype(f)
    n_p = np.stack([r["nT_p"][:, 0:16].reshape(128, 4, 4).transpose(1, 2, 0).reshape(4, 512) for r in results])[None].astype(f)
    m_p = np.stack([r["m_p"][0, 0:4] for r in results])[None].astype(f)
    pool_s = np.concatenate([r["pool_s"] for r in results])[None].astype(f)
    C_s = np.concatenate([r["C_s"] for r in results])[None].astype(f)
    n_s = np.concatenate([r["nT_s"].reshape(128, 4, 4, 16).transpose(3, 1, 2, 0).reshape(16, 4, 512) for r in results])[None].astype(f)
    m_s = np.concatenate([r["m_s"][0].reshape(16, 4) for r in results])[None].astype(f)
    return (y_p, y_s, pool_p, C_p, n_p, m_p, pool_s, C_s, n_s, m_s)


_CACHE = {}


def kernel(**inputs):
    if "nc" not in _CACHE:
        _CACHE["nc"] = build()
    nc, consts, _ = _CACHE["nc"]
    in_maps = [_core_inputs(b, consts=consts, **inputs) for b in range(8)]
    res = run_bass_kernel_spmd(nc, in_maps, core_ids=list(range(8)))
    return _gather(res.results, 8)
```

```python
import math
import numpy as np
import concourse.bass as bass
import concourse.mybir as mybir
from concourse.bass_utils import run_bass_kernel_spmd

F32 = mybir.dt.float32
BF16 = mybir.dt.bfloat16
AF = mybir.ActivationFunctionType
ALU = mybir.AluOpType
AX = mybir.AxisListType

D = 1024
NIN = 14344
NH = 4
HD = 512
EPS = 1e-6
C_A, C_ZP, C_Q, C_K, C_V, C_O, C_ZM, C_I, C_F, C_GP, C_GM = 0, 1024, 2048, 4096, 6144, 8192, 10240, 12288, 12292, 12296, 13320
LNK = -0.5 * math.log(HD)
import os as _os
TICK_ON = _os.environ.get("TICK", "1") == "1"


class Buf:
    __slots__ = ("name", "w", "rs", "excl")

    def __init__(self, name, excl=False):
        self.name = name
        self.w = None
        self.rs = []
        self.excl = excl


class DSem:
    def __init__(self, sem):
        self.sem = sem
        self.count = 0


class Op:
    __slots__ = ("eng", "fn", "reads", "writes", "dsem", "dval", "deps", "signal", "sigval", "waits", "idx", "ndma")


class Prog:
    ENGS = ("pe", "act", "dve", "pool", "sp")

    def __init__(self, nc):
        self.nc = nc
        self.ops = []
        self.esem = {e: nc.alloc_semaphore("es_" + e) for e in self.ENGS}
        self.dsems = []
        self.nbuf = 0
        self.defer_mode = None
        self.deferred = {}
        self._open_group = {}

    def buf(self, name=None, excl=False):
        self.nbuf += 1
        return Buf(name or ("b%d" % self.nbuf), excl)

    def dsem(self, name):
        d = DSem(self.nc.alloc_semaphore("ds_" + name))
        self.dsems.append(d)
        return d

    def op(self, eng, fn, reads=(), writes=(), hold=False):
        o = Op()
        o.eng = eng
        o.fn = fn
        rd = [b for b in reads if b is not None]
        wr = [b for b in writes if b is not None]
        o.writes = wr + [b for b in rd if b.excl]
        o.reads = [b for b in rd if not b.excl]
        o.dsem = None
        o.dval = 0
        o.signal = False
        o.sigval = 0
        if self.defer_mode:
            o.idx = -1
            q = self.deferred.setdefault(self.defer_mode, [])
            if self._open_group.get(self.defer_mode, False) and q:
                q[-1].append(o)
            else:
                q.append([o])
            self._open_group[self.defer_mode] = hold
        else:
            o.idx = len(self.ops)
            self.ops.append(o)
        return o

    def flush(self, n=None, q=None):
        for name in ([q] if q is not None else list(self.deferred.keys())):
            lst = self.deferred.get(name, [])
            k = len(lst) if n is None else min(n, len(lst))
            for grp in lst[:k]:
                for o in grp:
                    o.idx = len(self.ops)
                    self.ops.append(o)
                    if o.dsem is not None:
                        o.dsem.count += 16 * o.ndma
                        o.dval = o.dsem.count
            del lst[:k]
            if not lst:
                self._open_group[name] = False

    def dma(self, eng, fn, dsem, reads=(), writes=(), n=1, hold=False):
        o = self.op(eng, fn, reads, writes, hold=hold)
        o.dsem = dsem
        o.ndma = n
        if o.idx >= 0:
            dsem.count += 16 * n
            o.dval = dsem.count
        return o

    def _skip(self, d, o):
        return d.eng == o.eng and d.eng == "pe" and o.dsem is None and d.dsem is None

    def finalize(self):
        for o in self.ops:
            deps = {}
            for b in o.reads:
                if b.w is not None:
                    deps[b.w.idx] = b.w
            for b in o.writes:
                if b.w is not None:
                    deps[b.w.idx] = b.w
                for r in b.rs:
                    deps[r.idx] = r
            deps.pop(o.idx, None)
            for b in o.reads:
                b.rs.append(o)
            for b in o.writes:
                b.w = o
                b.rs = []
            o.deps = list(deps.values())
        for o in self.ops:
            for d in o.deps:
                if d.dsem is None and not self._skip(d, o):
                    d.signal = True
        cnt = {e: 0 for e in self.ENGS}
        for o in self.ops:
            if o.dsem is None and o.signal:
                cnt[o.eng] += 1
                o.sigval = cnt[o.eng]
        seen = {e: {} for e in self.ENGS}
        nw = 0
        for o in self.ops:
            need = {}
            for d in o.deps:
                if d.dsem is not None:
                    key, sem, val = ("d", id(d.dsem)), d.dsem.sem, d.dval
                else:
                    if self._skip(d, o):
                        continue
                    key, sem, val = ("e", d.eng), self.esem[d.eng], d.sigval
                if seen[o.eng].get(key, 0) >= val:
                    continue
                if key not in need or need[key][1] < val:
                    need[key] = (sem, val)
            o.waits = list(need.values())
            for key, (sem, val) in need.items():
                seen[o.eng][key] = val
            nw += len(o.waits)
        self.nwaits = nw

    def emit(self):
        nc = self.nc
        self.finalize()
        prog = self

        def run(ename, e):
            for o in prog.ops:
                if o.eng != ename:
                    continue
                for sem, val in o.waits:
                    e.wait_ge(sem, val)
                r = o.fn(e)
                if o.dsem is not None:
                    if not isinstance(r, (list, tuple)):
                        r = [r]
                    for ins in r:
                        ins.then_inc(o.dsem.sem, 16)
                elif o.signal:
                    r.then_inc(prog.esem[ename], 1)
            if ename == "sp":
                for d in prog.dsems:
                    if d.count > 0:
                        e.wait_ge(d.sem, d.count)

        with nc.Block() as block:
            @block.tensor
            def _(e):
                run("pe", e)

            @block.scalar
            def _(e):
                run("act", e)

            @block.vector
            def _(e):
                run("dve", e)

            @block.gpsimd
            def _(e):
                run("pool", e)

            @block.sync
            def _(e):
                run("sp", e)


POOL_WINDOWS = (2, 4, 8, 16)


def _pool_mats(kind):
    cur, hist = [], []
    for w in POOL_WINDOWS:
        if kind == "P":
            c = np.zeros((128, 128), np.float32)
            h = np.zeros((128, 128), np.float32)
            for t in range(128):
                for j in range(t - w + 1, t + 1):
                    if j >= 0:
                        c[j, t] += 1.0 / w
                    else:
                        h[128 + j, t] += 1.0 / w
                c[t, t] -= 1.0
            cur.append(c)
            hist.append([h])
        elif kind == "P0":
            h = np.zeros((16, 128), np.float32)
            for t in range(128):
                for j in range(t - w + 1, t + 1):
                    if j < 0:
                        h[16 + j, t] += 1.0 / w
            hist.append([h])
            cur.append(None)
        elif kind == "M":
            c = np.zeros((16, 16), np.float32)
            for t in range(16):
                cnt = min(t + 1, w)
                for j in range(max(0, t - w + 1), t + 1):
                    c[j, t] += 1.0 / cnt
                c[t, t] -= 1.0
            cur.append(c)
            hist.append([])
        elif kind == "S":
            c = np.zeros((64, 64), np.float32)
            h0 = np.zeros((120, 64), np.float32)
            h1 = np.zeros((120, 64), np.float32)
            for s in range(16):
                for t in range(4):
                    col = s * 4 + t
                    for e in range(15 + t - w + 1, 15 + t + 1):
                        if e >= 15:
                            c[s * 4 + (e - 15), col] += 1.0 / w
                        else:
                            r = s * 15 + e
                            if r < 120:
                                h0[r, col] += 1.0 / w
                            else:
                                h1[r - 120, col] += 1.0 / w
                    c[col, col] -= 1.0
            cur.append(c)
            hist.append([h0, h1])
    return cur, hist


class Pack:
    def __init__(self):
        self.items = []
        self.off = {}
        self.w = 0

    def add(self, name, arr):
        arr = np.asarray(arr, np.float32)
        assert arr.ndim == 2 and arr.shape[0] <= 128
        self.off[name] = (self.w, arr.shape[0], arr.shape[1])
        self.items.append(arr)
        self.w += arr.shape[1]

    def build(self):
        w = max(64, self.w)
        out = np.zeros((128, w), np.float32)
        for (name, (o, r, c)), a in zip(self.off.items(), self.items):
            out[:r, o:o + c] = a
        return out


def _struct_consts():
    pf = Pack()
    pb = Pack()
    pf.add("ident", np.eye(128))
    for kind, ntok, nseq, L in (("S", 64, 16, 4), ("M", 16, 1, 16), ("P", 128, 1, 128)):
        seq = np.arange(ntok) // L
        same = (seq[:, None] == seq[None, :])
        causal = same & (np.arange(ntok)[:, None] <= np.arange(ntok)[None, :])
        pf.add("mask" + kind, causal.astype(np.float32))
        E = (np.arange(nseq)[:, None] == seq[None, :]).astype(np.float32)
        pb.add("ET" + kind, E.T)
    pf.add("ones4", np.ones((4, 128)))
    pf.add("I4", np.eye(4))
    for kind in ("S", "M", "P", "P0"):
        cur, hist = _pool_mats(kind)
        for g in range(4):
            if cur[g] is not None:
                pb.add("pc%s%d" % (kind, g), cur[g])
            for j, h in enumerate(hist[g]):
                pb.add("ph%s%d_%d" % (kind, g, j), h)
    cm = np.zeros((16, 64), np.float32)
    for s in range(16):
        cm[s, 4 * s:4 * s + 4] = 1.0
    pb.add("colmask", np.broadcast_to(cm.reshape(1, 1024), (128, 1024)))
    pb.add("ones", np.ones((128, 8)))
    pb.add("identb", np.eye(128))
    seqS = np.arange(64) // 4
    pf.add("ETf", (seqS[:, None] == np.arange(16)[None, :]).astype(np.float32))
    return pf, pb


def build(nchunk=16, with_sample=True, with_meta=True, dbg=None):
    nc = bass.Bass("TRN2", target_bir_lowering=False)
    P = Prog(nc)
    SEQ = 128 * nchunk
    pf, pb = _struct_consts()
    cf_np = pf.build()
    cb_np = pb.build()

    def din(name, shape, dt=F32):
        return nc.dram_tensor(name, list(shape), dt, kind="ExternalInput")

    def dout(name, shape, dt=F32):
        return nc.dram_tensor(name, list(shape), dt, kind="ExternalOutput")

    xp = din("xp", [SEQ, D])
    xs = din("xs", [64, D])
    spool = din("spool", [240, D])
    sC = din("sC", [16, NH, HD, HD])
    snT = din("snT", [128, 256])
    smT = din("smT", [4, 64])
    meta = din("meta", [16, D])
    w_in = din("w_in", [D, NIN])
    w_pool = din("w_pool", [4, 256, 256])
    wbp = din("wbp", [D, D])
    wbm = din("wbm", [2 * D, D])
    wout = din("wout", [D, D])
    vecs = din("vecs", [128, 128])
    fgb = din("fgb", [1, D])
    cfd = din("cf", list(cf_np.shape))
    cbd = din("cb", list(cb_np.shape))
    y_p = dout("y_p", [SEQ, D])
    y_s = dout("y_s", [64, D])
    pool_p = dout("pool_p", [15, D])
    C_p = dout("C_p", [NH, HD, HD])
    nT_p = dout("nT_p", [128, 64])
    m_p = dout("m_p", [1, 64])
    pool_s = dout("pool_s", [16, 15, D])
    C_s = dout("C_s", [16, NH, HD, HD])
    nT_s = dout("nT_s", [128, 256])
    m_s = dout("m_s", [1, 64])
    NBLK = 36
    wblk = nc.dram_tensor("wblk", [NBLK, 128, 8, 512], BF16, kind="Internal")

    def sb(name, shape, dt=F32):
        return nc.alloc_sbuf_tensor(name, list(shape), dt)

    cf = sb("cf_t", cf_np.shape)
    cb = sb("cb_t", cb_np.shape, BF16)
    vec = sb("vec_t", [128, 128])
    fg = sb("fg_t", [128, D])
    wg = sb("wg_t", [128, 8, 64], BF16)
    wpl = sb("wpl_t", [128, 4, 2, 256], BF16)
    NSLOT = 6
    ring = [sb("ring%d" % i, [128, 8, 512], BF16) for i in range(NSLOT)]
    xt = [sb("xt%d" % i, [128, D]) for i in range(3)]
    hnT2 = [sb("hnT%d" % i, [128, 8, 128], BF16) for i in range(2)]
    atok = [sb("atok%d" % i, [128, D], BF16) for i in range(2)]
    af32 = sb("af32", [128, D])
    hs = [sb("hs%d" % i, [120, D], BF16) for i in range(2)]
    szpT = sb("szpT", [128, 8, 128])
    pooledT = sb("pooledT", [128, 8, 128], BF16)
    ypT = sb("ypT", [128, 8, 128], BF16)
    sgm = sb("sgm", [128, D])
    upg = sb("upg", [128, D])
    ubf = sb("ubf", [128, D], BF16)
    uT = sb("uT", [128, 8, 128], BF16)
    xo = sb("xo", [128, D])
    NHB = 2
    qT = [sb("qT%d" % i, [128, 4, 128], BF16) for i in range(NHB)]
    kT = [sb("kT%d" % i, [128, 4, 128], BF16) for i in range(NHB)]
    kw2 = [sb("kw2%d" % i, [128, 512], BF16) for i in range(NHB)]
    qtok = [sb("qtok%d" % i, [128, 512], BF16) for i in range(NHB)]
    ktok = [sb("ktok%d" % i, [128, 512], BF16) for i in range(NHB)]
    vtok = [sb("vtok%d" % i, [128, 512], BF16) for i in range(NHB)]
    Gt = [sb("G%d" % i, [128, 512]) for i in range(NHB)]
    GR = [sb("GR%d" % i, [128, 512]) for i in range(NHB)]
    SWT = [sb("SWT%d" % i, [128, 128], BF16) for i in range(NHB)]
    yln = sb("yln", [128, 2048], BF16)
    yT = sb("yT", [128, 16, 128], BF16)
    Cst = [sb("Cst%d" % i, [128, 4, 512]) for i in range(NH)]
    Cbf = [sb("Cbf%d" % i, [128, 4, 512], BF16) for i in range(NH)]
    nst = sb("nst", [128, 256])
    nbf = sb("nbf", [128, 256], BF16)
    qTm = [sb("qTm%d" % i, [128, 4, 64], BF16) for i in range(2)]
    vm = [sb("vm%d" % i, [64, 512], BF16) for i in range(2)]
    igT = sb("igT", [4, 128])
    efT = sb("efT", [4, 128])
    cs = [sb("cs%d" % i, [4, 128]) for i in range(2)]
    gT = sb("gT", [4, 128])
    argT = sb("argT", [4, 128])
    G3 = sb("G3", [96, 128])
    G3T = sb("G3T", [128, 96])
    mprevT = sb("mprevT", [4, 64])
    mxT = sb("mxT", [4, 16])
    gmax = sb("gmax", [4, 16])
    decT = sb("decT", [4, 16])
    mnewT = sb("mnewT", [4, 16])
    DM = sb("DM", [4, 128])
    decb = sb("decb", [128, 128])
    ngbf = sb("ngbf", [4, 1])
    sm1 = sb("sm1", [128, 16])
    st6 = sb("st6", [128, 8])
    ntmp = sb("ntmp", [128, 16])

    def ps(name, shape, dt=F32):
        return nc.alloc_psum_tensor(name, list(shape), dt)

    NPJ = 3
    PJ = [ps("PJ%d" % i, [128, 512]) for i in range(NPJ)]
    TP = ps("TP", [128, 1024], BF16)
    STp = ps("STp", [128, 512])
    NUM = [ps("NUM%d" % i, [128, 512]) for i in range(1)]
    CU = [ps("CU%d" % i, [128, 512]) for i in range(2)]
    PJR = PJ + CU
    PJN = ["PJ0", "PJ1", "PJ2", "CU0", "CU1"]
    NROT = len(PJR)

    B = {}

    def bf(name, excl=False):
        B[name] = P.buf(name, excl)
        return B[name]

    for n_ in ("cf cb vec fg wg wpl xsq hnT0 hnT1 sm1p sm1t sm1h0 sm1h1 sm1h2 sm1h3 af32 hs0 hs1 szpT pooledT ypT sgp sgm upg ubf uT xo yout yln0 yln1 yln2 yln3 yT0 yT1 yT2 yT3 nst nbf "
               "igT efT cs0 cs1 gT argT G3 G3T mprevT mxT gmax decT mnewT DM decb ngbf sm1 st6 ntmp xt0 xt1 xt2 atok0 atok1 "
               "qTm0 qTm1 vm0 vm1").split():
        bf(n_)
    for i in range(NHB):
        for n_ in ("qT", "kT", "kw2", "qtok", "ktok", "vtok", "og", "G", "GR", "SWT"):
            bf("%s%d" % (n_, i))
    for i in range(NH):
        bf("Cst%d" % i)
        bf("Cbf%d" % i)
    for i in range(NSLOT):
        bf("ring%d" % i)
    for n_ in ("PJ0", "PJ1", "PJ2", "TP", "STp", "NUM0", "CU0", "CU1"):
        bf(n_, excl=True)
    wbuf = [P.buf("wblk%d" % j) for j in range(NBLK)]
    ring_ds = [P.dsem("ring%d" % i) for i in range(NSLOT)]
    x_ds = [P.dsem("x%d" % i) for i in range(3)]
    c_ds = [P.dsem("c%d" % i) for i in range(NH)]
    out_ds = P.dsem("out")
    cout_ds = [P.dsem("co%d" % i) for i in range(NH)]
    o_ds = {k: P.dsem("o_" + k) for k in ("nTs", "ms", "ps", "nTp", "mp", "pp")}
    yo_ds = P.dsem("yo")
    misc_ds = [P.dsem("misc%d" % i) for i in range(12)]
    cv_ds = [P.dsem("cv%d" % i) for i in range(NBLK)]

    def cfs(name):
        o, r, c = pf.off[name]
        return cf[0:r, o:o + c]

    def cbs(name):
        o, r, c = pb.off[name]
        return cb[0:r, o:o + c]

    blocks = []
    BI = {}

    def addblk(key, t, r0, c0):
        BI[key] = len(blocks)
        blocks.append((t, r0, c0))

    for h in range(NH):
        addblk(("q", h), w_in, 0, C_Q + 512 * h)
        addblk(("k", h), w_in, 0, C_K + 512 * h)
        addblk(("v", h), w_in, 0, C_V + 512 * h)
        addblk(("o", h), w_in, 0, C_O + 512 * h)
        addblk(("z", h), w_in, 0, C_ZM + 512 * h)
    for j in range(2):
        addblk(("a", j), w_in, 0, C_A + 512 * j)
    for j in range(2):
        addblk(("zp", j), w_in, 0, C_ZP + 512 * j)
    for j in range(2):
        addblk(("gp", j), w_in, 0, C_GP + 512 * j)
    for j in range(2):
        addblk(("gm", j), w_in, 0, C_GM + 512 * j)
    for j in range(2):
        addblk(("bp", j), wbp, 0, 512 * j)
    for j in range(2):
        for mh in range(2):
            addblk(("bm", j, mh), wbm, 1024 * mh, 512 * j)
    for j in range(2):
        addblk(("wo", j), wout, 0, 512 * j)
    assert len(blocks) == NBLK

    P.dma("sp", lambda e: e.dma_start(out=cf[:], in_=cfd[:, :]), misc_ds[0], writes=[B["cf"]])
    P.dma("pool", lambda e: e.dma_start(out=cb[:], in_=cbd[:, :]), misc_ds[1], writes=[B["cb"]])
    P.dma("sp", lambda e: e.dma_start(out=vec[:], in_=vecs[:, :]), misc_ds[2], writes=[B["vec"]])
    P.dma("sp", lambda e: e.dma_start(out=fg[:], in_=fgb[0:1, :].partition_broadcast(128)), misc_ds[3], writes=[B["fg"]])
    P.dma("pool", lambda e: e.dma_start(out=wg[:], in_=w_in[:, C_I - 56:C_I + 8].rearrange("(kc p) c -> p kc c", p=128)),
          misc_ds[4], writes=[B["wg"]])
    P.dma("pool", lambda e: e.dma_start(out=wpl[:], in_=w_pool.ap().rearrange("g (cl c) d -> c g cl d", c=128)),
          misc_ds[5], writes=[B["wpl"]])
    for j, (t, r0, c0) in enumerate(blocks):
        P.dma("pool", lambda e, j=j, t=t, r0=r0, c0=c0: e.dma_start(
            out=wblk[j].rearrange("p kc c -> kc p c"),
            in_=t[r0:r0 + 1024, c0:c0 + 512].rearrange("(kc p) c -> kc p c", p=128)),
            cv_ds[j], writes=[wbuf[j]])
    P.op("dve", lambda e: e.tensor_scalar(out=ngbf[:], in0=vec[0:4, 33:34], scalar1=-1.0, scalar2=None, op0=ALU.mult),
         reads=[B["vec"]], writes=[B["ngbf"]])
    P.op("pool", lambda e: e.memset(G3[:], 0.0), writes=[B["G3"]])
    P.op("pool", lambda e: e.memset(sm1[:], 0.0), writes=[B["sm1p"], B["sm1t"], B["sm1h0"], B["sm1h1"], B["sm1h2"], B["sm1h3"]])
    P.op("pool", lambda e: e.memset(DM[:], 0.0), writes=[B["DM"]])
    P.op("pool", lambda e: e.memset(nst[:], 0.0), writes=[B["nst"]])

    rstate = {"slot": 0, "xi": 0, "ai": 0}

    def tick(n=1):
        if TICK_ON:
            P.flush(n)

    def load_block(key):
        j = BI[key]
        s = rstate["slot"]
        rstate["slot"] = (s + 1) % NSLOT
        P.dma("sp", lambda e, j=j, s=s: e.dma_start(out=ring[s][:], in_=wblk[j]), ring_ds[s],
              reads=[wbuf[j]], writes=[B["ring%d" % s]])
        return ring[s], B["ring%d" % s]

    def run_pass(kind, x_src, y_dst, ci=0):
        nt = {"S": 64, "M": 16, "P": 128}[kind]
        nseq = 16 if kind == "S" else 1
        L = nt // nseq
        zero_state = (kind == "M") or (kind == "P" and not with_meta and ci == 0)
        xi = rstate["xi"]
        rstate["xi"] = (xi + 1) % 3
        hi = rstate.get("hi", 0)
        rstate["hi"] = 1 - hi
        X, BX = xt[xi], B["xt%d" % xi]
        hnT, BhnT = hnT2[hi], B["hnT%d" % hi]
        ai = rstate["ai"]
        rstate["ai"] = 1 - ai
        A, BA = atok[ai], B["atok%d" % ai]
        Aprev, BAprev = atok[1 - ai], B["atok%d" % (1 - ai)]
        maskT = cfs("mask" + kind)
        ident = cfs("ident")
        identb = None

        P.dma("sp", lambda e: e.dma_start(out=X[0:nt, :], in_=x_src), x_ds[xi], writes=[BX])
        if kind == "S":
            P.dma("sp", lambda e: e.dma_start(out=mprevT[:], in_=smT[:, :]), misc_ds[7], writes=[B["mprevT"]])
            for i_ in range(2):
                P.dma("pool", lambda e, i_=i_: e.dma_start(out=hs[i_][:], in_=spool[120 * i_:120 * (i_ + 1), :]),
                      misc_ds[9 + i_], writes=[B["hs%d" % i_]])
        elif zero_state:
            P.op("pool", lambda e: e.memset(mprevT[:], 0.0), writes=[B["mprevT"]])

        P.op("act", lambda e: e.activation(out=xo[0:nt, :], in_=X[0:nt, :], func=AF.Square, accum_out=sm1[0:nt, 0:1]),
             reads=[BX], writes=[B["xo"], B["sm1p"]])
        P.op("act", lambda e: e.activation(out=sm1[0:nt, 1:2], in_=sm1[0:nt, 0:1], func=AF.Ln, scale=1.0 / D, bias=EPS),
             reads=[B["sm1p"]], writes=[B["sm1p"]])
        P.op("act", lambda e: e.activation(out=sm1[0:nt, 2:3], in_=sm1[0:nt, 1:2], func=AF.Exp, scale=-0.5),
             reads=[B["sm1p"]], writes=[B["sm1p"]])
        P.op("dve", lambda e: e.tensor_scalar(out=xo[0:nt, :], in0=X[0:nt, :], scalar1=sm1[0:nt, 2:3], scalar2=None, op0=ALU.mult),
             reads=[BX, B["sm1p"]], writes=[B["xo"]])
        for half in range(2):
            pjv = rstate.get("pj", 0)
            rstate["pj"] = (pjv + 1) % NROT
            pj, bpj = PJR[pjv], B[PJN[pjv]]
            for q4 in range(4):
                kc = half * 4 + q4
                P.op("pe", lambda e, kc=kc, q4=q4, pj=pj: e.transpose(out=pj[:, q4 * nt:(q4 + 1) * nt], in_=xo[0:nt, kc * 128:(kc + 1) * 128],
                                                                     identity=ident[0:nt, 0:nt]),
                     reads=[B["xo"], B["cf"]], writes=[bpj], hold=True)
            for q4 in range(4):
                kc = half * 4 + q4
                P.op("dve", lambda e, kc=kc, q4=q4, pj=pj: e.tensor_scalar(out=hnT[:, kc, 0:nt], in0=pj[:, q4 * nt:(q4 + 1) * nt],
                                                                         scalar1=vec[:, kc:kc + 1], scalar2=None, op0=ALU.mult),
                     reads=[bpj, B["vec"]], writes=[BhnT], hold=(q4 != 3))

        pjv = rstate.get("pj", 0)
        rstate["pj"] = (pjv + 1) % NROT
        GP, bGP = PJR[pjv], B[PJN[pjv]]
        for gi in range(2):
            for kc in range(8):
                P.op("pe", lambda e, gi=gi, kc=kc: e.matmul(out=GP[0:4, gi * 128:gi * 128 + nt], lhsT=wg[:, kc, 56 + gi * 4:60 + gi * 4],
                                                            rhs=hnT[:, kc, 0:nt], start=(kc == 0), stop=(kc == 7)),
                     reads=[B["wg"], BhnT], writes=[bGP], hold=True)
        P.op("act", lambda e: e.activation(out=igT[:, 0:nt], in_=GP[0:4, 0:nt], func=AF.Identity, bias=vec[0:4, 32:33], scale=1.0),
             reads=[bGP, B["vec"]], writes=[B["igT"]], hold=True)
        P.op("act", lambda e: e.activation(out=efT[:, 0:nt], in_=GP[0:4, 128:128 + nt], func=AF.Exp, bias=ngbf[:, 0:1], scale=-1.0),
             reads=[bGP, B["ngbf"]], writes=[B["efT"]])
        P.op("act", lambda e: e.activation(out=cs[0][:, 0:nt], in_=efT[:, 0:nt], func=AF.Ln, bias=1.0, scale=1.0),
             reads=[B["efT"]], writes=[B["cs0"]])
        cur = 0
        sh = 1
        while sh < L:
            src, dst = cs[cur], cs[1 - cur]
            sv = src[:, 0:nt].rearrange("p (s l) -> p s l", l=L)
            dv = dst[:, 0:nt].rearrange("p (s l) -> p s l", l=L)
            P.op("dve", lambda e, sv=sv, dv=dv, sh=sh: e.tensor_copy(out=dv[:, :, 0:sh], in_=sv[:, :, 0:sh]),
                 reads=[B["cs%d" % cur]], writes=[B["cs%d" % (1 - cur)]])
            P.op("dve", lambda e, sv=sv, dv=dv, sh=sh: e.tensor_tensor(out=dv[:, :, sh:L], in0=sv[:, :, sh:L], in1=sv[:, :, 0:L - sh], op=ALU.add),
                 reads=[B["cs%d" % cur]], writes=[B["cs%d" % (1 - cur)]])
            cur = 1 - cur
            sh *= 2
        nbT, BnbT = cs[cur], B["cs%d" % cur]
        P.op("dve", lambda e: e.tensor_tensor(out=gT[:, 0:nt], in0=igT[:, 0:nt], in1=nbT[:, 0:nt], op=ALU.add),
             reads=[B["igT"], BnbT], writes=[B["gT"]])
        P.op("dve", lambda e: e.tensor_reduce(out=gmax[:, 0:nseq], in_=gT[:, 0:nt].rearrange("p (s l) -> p s l", l=L), axis=AX.X, op=ALU.max),
             reads=[B["gT"]], writes=[B["gmax"]])
        P.op("dve", lambda e: e.tensor_tensor(out=mxT[:, 0:nseq], in0=gmax[:, 0:nseq], in1=mprevT[:, 0:nseq], op=ALU.max),
             reads=[B["gmax"], B["mprevT"]], writes=[B["mxT"]])
        nbL = nbT[:, 0:nt].rearrange("p (s l) -> p s l", l=L)[:, :, L - 1]
        P.op("dve", lambda e: e.tensor_tensor(out=mnewT[:, 0:nseq], in0=mxT[:, 0:nseq], in1=nbL, op=ALU.subtract),
             reads=[B["mxT"], BnbT], writes=[B["mnewT"]])
        P.op("dve", lambda e: e.tensor_tensor(out=decT[:, 0:nseq], in0=mprevT[:, 0:nseq], in1=mxT[:, 0:nseq], op=ALU.subtract),
             reads=[B["mxT"], B["mprevT"]], writes=[B["decT"]])
        P.op("act", lambda e: e.activation(out=decT[:, 0:nseq], in_=decT[:, 0:nseq], func=AF.Exp), reads=[B["decT"]], writes=[B["decT"]])

        def bc(t):
            return t[:, 0:nseq].unsqueeze(2).to_broadcast([4, nseq, L])

        g3v = gT[:, 0:nt].rearrange("p (s l) -> p s l", l=L)
        a3v = argT[:, 0:nt].rearrange("p (s l) -> p s l", l=L)
        nb3v = nbT[:, 0:nt].rearrange("p (s l) -> p s l", l=L)
        for (row, in0v, subt, bias, rd) in ((0, g3v, mprevT, LNK, [B["gT"], B["mprevT"]]),
                                             (32, g3v, mxT, LNK, [B["gT"], B["mxT"]]),
                                             (64, nb3v, mprevT, 0.0, [BnbT, B["mprevT"]])):
            P.op("dve", lambda e, in0v=in0v, subt=subt: e.tensor_tensor(out=a3v, in0=in0v, in1=bc(subt), op=ALU.subtract),
                 reads=rd, writes=[B["argT"]])
            if bias != 0.0:
                P.op("dve", lambda e, bias=bias: e.tensor_scalar(out=argT[:, 0:nt], in0=argT[:, 0:nt], scalar1=bias, scalar2=None, op0=ALU.add),
                     reads=[B["argT"]], writes=[B["argT"]])
            P.op("act", lambda e, row=row: e.activation(out=G3[row:row + 4, 0:nt], in_=argT[:, 0:nt], func=AF.Exp),
                 reads=[B["argT"]], writes=[B["G3"]])
        yield "A"
        P.op("pe", lambda e: e.transpose(out=STp[0:nt, 256:352], in_=G3[0:96, 0:nt], identity=ident[0:96, 0:96]),
             reads=[B["G3"], B["cf"]], writes=[B["STp"]])
        P.op("dve", lambda e: e.tensor_copy(out=G3T[0:nt, :], in_=STp[0:nt, 256:352]), reads=[B["STp"]], writes=[B["G3T"]])
        I4 = cfs("I4")
        for (off, src, bsrc) in ((0, decT, B["decT"]), (64, mnewT, B["mnewT"])):
            P.op("dve", lambda e, off=off, src=src: e.tensor_tensor(
                out=DM[:, off:off + 4 * nseq].rearrange("p (s h) -> p s h", h=4),
                in0=src[:, 0:nseq].unsqueeze(2).to_broadcast([4, nseq, 4]),
                in1=I4.unsqueeze(1).to_broadcast([4, nseq, 4]), op=ALU.mult),
                reads=[bsrc, B["cf"]], writes=[B["DM"]])
        P.op("pe", lambda e: e.matmul(out=STp[:, 384:512], lhsT=cfs("ones4"), rhs=DM[:, :], start=True, stop=True),
             reads=[B["DM"], B["cf"]], writes=[B["STp"]])
        P.op("act", lambda e: e.activation(out=decb[:, :], in_=STp[:, 384:512], func=AF.Copy), reads=[B["STp"]], writes=[B["decb"]])
        if kind != "S":
            P.op("dve", lambda e: e.tensor_copy(out=mprevT[:, 0:1], in_=mnewT[:, 0:1]), reads=[B["mnewT"]], writes=[B["mprevT"]])

        yield "B"
        def proj_tok(key, evac):
            w, bw = load_block(key)
            pjv = rstate.get("pj", 0)
            rstate["pj"] = (pjv + 1) % NROT
            pj, bpj = PJR[pjv], B[PJN[pjv]]
            for kc in range(8):
                P.op("pe", lambda e, kc=kc, w=w, pj=pj: e.matmul(out=pj[0:nt, :], lhsT=hnT[:, kc, 0:nt], rhs=w[:, kc, :],
                                                               start=(kc == 0), stop=(kc == 7)),
                     reads=[BhnT, bw], writes=[bpj])
            evac(pj, bpj)
            tick()
            return w, bw

        def proj_fm(w, bw, evac):
            pjv = rstate.get("pj", 0)
            rstate["pj"] = (pjv + 1) % NROT
            pj, bpj = PJR[pjv], B[PJN[pjv]]
            for cc in range(4):
                for kc in range(8):
                    P.op("pe", lambda e, cc=cc, kc=kc, w=w, pj=pj: e.matmul(out=pj[:, cc * nt:(cc + 1) * nt], lhsT=w[:, kc, cc * 128:(cc + 1) * 128],
                                                                          rhs=hnT[:, kc, 0:nt], start=(kc == 0), stop=(kc == 7)),
                         reads=[BhnT, bw], writes=[bpj])
            evac(pj, bpj)
            tick()

        def head_proj(h):
            hb = h % NHB
            def tok_then_T(key, tokt, btok, dstT, bdst, extra_evac=None):
                w, bw = load_block(key)
                pjv = rstate.get("pj", 0)
                rstate["pj"] = (pjv + 1) % NROT
                pj, bpj = PJR[pjv], B[PJN[pjv]]
                for kc in range(8):
                    P.op("pe", lambda e, kc=kc, w=w, pj=pj: e.matmul(out=pj[0:nt, :], lhsT=hnT[:, kc, 0:nt], rhs=w[:, kc, :],
                                                                   start=(kc == 0), stop=(kc == 7)),
                         reads=[BhnT, bw], writes=[bpj])
                P.op("dve", lambda e, pj=pj: e.tensor_copy(out=tokt[0:nt, :], in_=pj[0:nt, :]), reads=[bpj], writes=[btok])
                if extra_evac is not None:
                    extra_evac(pj, bpj)
                tick()
                for dc in range(4):
                    P.op("pe", lambda e, dc=dc: e.transpose(out=TP[:, 512 + dc * nt:512 + (dc + 1) * nt], in_=tokt[0:nt, dc * 128:(dc + 1) * 128],
                                                            identity=cbs_ident[0:nt, 0:nt]),
                         reads=[btok, B["cb"]], writes=[B["TP"]])
                P.op("act", lambda e: e.activation(out=dstT[:, :, 0:nt], in_=TP[:, 512:512 + 4 * nt].rearrange("p (c t) -> p c t", t=nt), func=AF.Copy),
                     reads=[B["TP"]], writes=[bdst])
                tick()

            tok_then_T(("q", h), qtok[hb], B["qtok%d" % hb], qT[hb], B["qT%d" % hb])
            tok_then_T(("k", h), ktok[hb], B["ktok%d" % hb], kT[hb], B["kT%d" % hb],
                       extra_evac=lambda pj, bpj: P.op("act", lambda e: e.activation(
                           out=kw2[hb][0:nt, :], in_=pj[0:nt, :], func=AF.Copy, scale=G3T[0:nt, 32 + h:33 + h]),
                           reads=[bpj, B["G3T"]], writes=[B["kw2%d" % hb]]))
            proj_tok(("v", h), lambda pj, bpj: P.op("dve", lambda e: e.tensor_copy(out=vtok[hb][0:nt, :], in_=pj[0:nt, :]),
                                                    reads=[bpj], writes=[B["vtok%d" % hb]]))
            proj_tok(("o", h), lambda pj, bpj: P.op("act", lambda e: e.activation(out=GR[hb][0:nt, :], in_=pj[0:nt, :], func=AF.Sigmoid),
                                                    reads=[bpj], writes=[B["GR%d" % hb]]))

            def ev_z(pj, bpj):
                P.op("act", lambda e: e.activation(out=Gt[hb][0:nt, :], in_=pj[0:nt, :], func=AF.Silu),
                     reads=[bpj], writes=[B["G%d" % hb]])
                P.op("pool", lambda e: e.tensor_tensor(out=Gt[hb][0:nt, :], in0=Gt[hb][0:nt, :], in1=GR[hb][0:nt, :], op=ALU.mult),
                     reads=[B["G%d" % hb], B["GR%d" % hb]], writes=[B["G%d" % hb]])
            proj_tok(("z", h), ev_z)

        def head_mlstm(h):
            hb = h % NHB
            nm, bnm = NUM[0], B["NUM0"]
            def state_update(s):
                if kind == "S":
                    slot = (h * 16 + s) % NH
                    cst, bcst = Cst[slot], B["Cst%d" % slot]
                    vi = (h * 16 + s) % 2
                    P.op("act", lambda e, vi=vi, s=s: e.activation(out=vm[vi][:, :], in_=vtok[hb][0:64, :], func=AF.Copy, scale=cfs("ETf")[0:64, s:s + 1]),
                         reads=[B["vtok%d" % hb], B["cf"]], writes=[B["vm%d" % vi]])
                    rv, brv = vm[vi], B["vm%d" % vi]
                else:
                    cst, bcst = Cst[h], B["Cst%d" % h]
                    rv, brv = vtok[hb], B["vtok%d" % hb]
                dcol = s * 4 + h
                for dc in range(4):
                    cuv = rstate.get("cu", 0)
                    rstate["cu"] = 1 - cuv
                    cu, bcu = CU[cuv], B["CU%d" % cuv]
                    P.op("pe", lambda e, dc=dc, cu=cu, rv=rv: e.matmul(out=cu[:, :], lhsT=kw2[hb][0:nt, dc * 128:(dc + 1) * 128], rhs=rv[0:nt, :],
                                                                      start=True, stop=True),
                         reads=[B["kw2%d" % hb], brv], writes=[bcu])
                    if zero_state:
                        P.op("dve", lambda e, dc=dc, cu=cu, cst=cst: e.tensor_copy(out=cst[:, dc, :], in_=cu[:, :]), reads=[bcu], writes=[bcst])
                    else:
                        P.op("dve", lambda e, dc=dc, cu=cu, cst=cst: e.scalar_tensor_tensor(out=cst[:, dc, :], in0=cst[:, dc, :], scalar=decb[:, dcol:dcol + 1],
                                                                                          in1=cu[:, :], op0=ALU.mult, op1=ALU.add),
                             reads=[bcu, bcst, B["decb"]], writes=[bcst])
                if kind == "S":
                    P.dma("pool", lambda e, s=s, cst=cst: e.dma_start(out=C_s[s, h].rearrange("(dc p) v -> p dc v", p=128), in_=cst[:]),
                          cout_ds[slot], reads=[bcst])

            def state_cast():
                P.op("pool", lambda e: e.tensor_copy(out=Cbf[h][:], in_=Cst[h][:]), reads=[B["Cst%d" % h]], writes=[B["Cbf%d" % h]])

            for dc in range(4):
                P.op("pe", lambda e, dc=dc: e.matmul(out=STp[0:nt, 0:nt], lhsT=kT[hb][:, dc, 0:nt], rhs=qT[hb][:, dc, 0:nt],
                                                     start=(dc == 0), stop=(dc == 3)),
                     reads=[B["kT%d" % hb], B["qT%d" % hb]], writes=[B["STp"]])
            P.op("dve", lambda e: e.scalar_tensor_tensor(out=SWT[hb][0:nt, 0:nt], in0=STp[0:nt, 0:nt], scalar=G3T[0:nt, h:h + 1],
                                                         in1=maskT, op0=ALU.mult, op1=ALU.mult),
                 reads=[B["STp"], B["G3T"], B["cf"]], writes=[B["SWT%d" % hb]])
            tick()
            ETb = cbs("ET" + kind)
            for dc in range(4):
                P.op("pe", lambda e, dc=dc: e.matmul(out=STp[:, 208 + dc * 16:208 + dc * 16 + nseq], lhsT=kw2[hb][0:nt, dc * 128:(dc + 1) * 128],
                                                     rhs=ETb[0:nt, 0:nseq], start=True, stop=True),
                     reads=[B["kw2%d" % hb], B["cb"]], writes=[B["STp"]])
            nv = nst[:, h * 4 * nseq:(h + 1) * 4 * nseq].rearrange("p (c s) -> p c s", s=nseq) if kind == "S" else \
                nst[:, h * 4:(h + 1) * 4].unsqueeze(2)
            pv = STp[:, 208:272].rearrange("p (c s) -> p c s", s=16)[:, :, 0:nseq]
            if zero_state:
                P.op("dve", lambda e: e.tensor_copy(out=nv, in_=pv), reads=[B["STp"]], writes=[B["nst"]])
            else:
                dv_ = decb[:, 0:4 * nseq].rearrange("p (s h) -> p h s", h=4)[:, h, :].unsqueeze(1).to_broadcast([128, 4, nseq])
                P.op("dve", lambda e: e.tensor_tensor(out=nv, in0=nv, in1=dv_, op=ALU.mult), reads=[B["nst"], B["decb"]], writes=[B["nst"]])
                P.op("dve", lambda e: e.tensor_tensor(out=nv, in0=nv, in1=pv, op=ALU.add), reads=[B["nst"], B["STp"]], writes=[B["nst"]])
            tick()
            ones_b = cbs("ones")
            n_den = 1 + (0 if zero_state else 4 * nseq)
            idx = [0]

            def den_mm(lhsT, rhs, reads):
                i = idx[0]
                idx[0] += 1
                P.op("pe", lambda e: e.matmul(out=STp[0:nt, 200:201], lhsT=lhsT, rhs=rhs, start=(i == 0), stop=(i == n_den - 1)),
                     reads=reads, writes=[B["STp"]])

            if kind != "S":
                state_update(0)
                tick(1)
            n_num = 1 + (0 if zero_state else 4 * nseq)
            P.op("pe", lambda e: e.matmul(out=nm[0:nt, :], lhsT=SWT[hb][0:nt, 0:nt], rhs=vtok[hb][0:nt, :], start=True, stop=(n_num == 1)),
                 reads=[B["SWT%d" % hb], B["vtok%d" % hb]], writes=[bnm])
            den_mm(SWT[hb][0:nt, 0:nt], ones_b[0:nt, 0:1], [B["SWT%d" % hb], B["cb"]])
            cnt = 1
            for s in range(nseq):
                if zero_state:
                    break
                if kind == "S":
                    slot = (h * 16 + s) % NH
                    cst, bcst, cbf, bcbf = Cst[slot], B["Cst%d" % slot], Cbf[slot], B["Cbf%d" % slot]
                    P.dma("sp", lambda e, s=s, cst=cst: e.dma_start(out=cst[:], in_=sC[s, h].rearrange("(dc p) v -> p dc v", p=128)),
                          c_ds[slot], writes=[bcst])
                    P.op("act", lambda e, cst=cst, cbf=cbf: e.activation(out=cbf[:], in_=cst[:], func=AF.Copy), reads=[bcst], writes=[bcbf])
                    qi = (h * 16 + s) % 2
                    cmv = cbs("colmask")[:, s * 64:(s + 1) * 64]
                    P.op("dve", lambda e, qi=qi, cmv=cmv: e.tensor_tensor(out=qTm[qi][:, :, :], in0=qT[hb][:, :, 0:64],
                                                                          in1=cmv.unsqueeze(1).to_broadcast([128, 4, 64]), op=ALU.mult),
                         reads=[B["qT%d" % hb], B["cb"]], writes=[B["qTm%d" % qi]])
                    lq, blq = qTm[qi], B["qTm%d" % qi]
                else:
                    cst, bcst, cbf, bcbf = Cst[h], B["Cst%d" % h], Cbf[h], B["Cbf%d" % h]
                    lq, blq = qT[hb], B["qT%d" % hb]
                for dc in range(4):
                    cnt += 1
                    P.op("pe", lambda e, dc=dc, lq=lq, cbf=cbf, last=(cnt == n_num): e.matmul(
                        out=nm[0:nt, :], lhsT=lq[:, dc, 0:nt], rhs=cbf[:, dc, :], start=False, stop=last),
                        reads=[blq, bcbf], writes=[bnm])
                    ncol = (h * 4 + dc) * nseq + s if kind == "S" else (h * 4 + dc)
                    den_mm(lq[:, dc, 0:nt], nbf[:, ncol:ncol + 1], [blq, B["nbf"]])
                if kind == "S":
                    state_update(s)
            if kind != "S":
                state_cast()
            tick(1)
            c0 = 4 + 3 * h
            P.op("dve", lambda e: e.tensor_scalar(out=sm1[0:nt, c0 + 1:c0 + 2], in0=STp[0:nt, 200:201], scalar1=-1.0, scalar2=None, op0=ALU.mult),
                 reads=[B["STp"]], writes=[B["sm1h%d" % h]])
            P.op("dve", lambda e: e.scalar_tensor_tensor(out=sm1[0:nt, c0:c0 + 1], in0=STp[0:nt, 200:201], scalar=G3T[0:nt, 64 + h:65 + h],
                                                         in1=sm1[0:nt, c0 + 1:c0 + 2], op0=ALU.max, op1=ALU.max),
                 reads=[B["STp"], B["G3T"], B["sm1h%d" % h]], writes=[B["sm1h%d" % h]])
            P.op("dve", lambda e: e.bn_stats(out=st6[0:nt, 0:6], in_=nm[0:nt, :]), reads=[bnm], writes=[B["st6"]])
            P.op("dve", lambda e: e.bn_aggr(out=st6[0:nt, 6:8], in_=st6[0:nt, 0:6]), reads=[B["st6"]], writes=[B["st6"]])
            tick()
            P.op("dve", lambda e: e.tensor_scalar(out=sm1[0:nt, c0 + 1:c0 + 2], in0=sm1[0:nt, c0:c0 + 1], scalar1=sm1[0:nt, c0:c0 + 1], scalar2=EPS,
                                                  op0=ALU.mult, op1=ALU.mult),
                 reads=[B["sm1h%d" % h]], writes=[B["sm1h%d" % h]])
            P.op("dve", lambda e: e.tensor_scalar(out=sm1[0:nt, c0 + 1:c0 + 2], in0=sm1[0:nt, c0 + 1:c0 + 2], scalar1=st6[0:nt, 7:8], scalar2=None, op0=ALU.add),
                 reads=[B["sm1h%d" % h], B["st6"]], writes=[B["sm1h%d" % h]])
            prev_mode = P.defer_mode
            P.defer_mode = "L"
            P.op("act", lambda e: e.activation(out=sm1[0:nt, c0 + 1:c0 + 2], in_=sm1[0:nt, c0 + 1:c0 + 2], func=AF.Ln),
                 reads=[B["sm1h%d" % h]], writes=[B["sm1h%d" % h]], hold=True)
            P.op("act", lambda e: e.activation(out=sm1[0:nt, c0 + 2:c0 + 3], in_=sm1[0:nt, c0 + 1:c0 + 2], func=AF.Exp, scale=-0.5),
                 reads=[B["sm1h%d" % h]], writes=[B["sm1h%d" % h]], hold=True)
            P.op("act", lambda e: e.activation(out=GR[hb][0:nt, :], in_=Gt[hb][0:nt, :], func=AF.Copy, scale=sm1[0:nt, c0 + 2:c0 + 3]),
                 reads=[B["sm1h%d" % h], B["G%d" % hb]], writes=[B["GR%d" % hb]])
            P.op("dve", lambda e: e.scalar_tensor_tensor(out=yln[0:nt, h * 512:(h + 1) * 512], in0=nm[0:nt, :], scalar=st6[0:nt, 6:7],
                                                         in1=GR[hb][0:nt, :], op0=ALU.subtract, op1=ALU.mult),
                 reads=[bnm, B["st6"], B["GR%d" % hb]], writes=[B["yln%d" % h]])
            P.defer_mode = prev_mode
        def ytrans(h):
            P.flush(q="L")
            identb = cbs_ident
            for mc in range(4):
                P.op("pe", lambda e, mc=mc: e.transpose(out=TP[:, mc * nt:(mc + 1) * nt], in_=yln[0:nt, h * 512 + mc * 128:h * 512 + (mc + 1) * 128],
                                                        identity=identb[0:nt, 0:nt]),
                     reads=[B["yln%d" % h], B["cb"]], writes=[B["TP"]])
            for mc in range(4):
                P.op("act", lambda e, mc=mc: e.activation(out=yT[:, h * 4 + mc, 0:nt], in_=TP[:, mc * nt:(mc + 1) * nt], func=AF.Copy,
                                                          scale=vec[:, 16 + h * 4 + mc:17 + h * 4 + mc]),
                     reads=[B["TP"], B["vec"]], writes=[B["yT%d" % h]])

        cbs_ident = cbs("identb")
        B.setdefault("identb", B["cb"])

        if kind == "S":
            P.dma("sp", lambda e: e.dma_start(out=nst[:], in_=snT[:, :]), misc_ds[8], writes=[B["nst"]])
            P.op("act", lambda e: e.activation(out=nbf[:], in_=nst[:], func=AF.Copy), reads=[B["nst"]], writes=[B["nbf"]])
        head_proj(0)
        head_proj(1)
        head_mlstm(0)
        head_proj(2)
        yield "C1"
        head_mlstm(1)
        ytrans(0)
        head_proj(3)
        head_mlstm(2)
        ytrans(1)
        head_mlstm(3)
        ytrans(2)
        yield "C2"
        if kind != "S":
            P.op("act", lambda e: e.activation(out=nbf[:, 0:16], in_=nst[:, 0:16], func=AF.Copy), reads=[B["nst"]], writes=[B["nbf"]])

        for j in range(2):
            def ev_a(pj, bpj, j=j):
                P.op("act", lambda e: e.activation(out=A[0:nt, j * 512:(j + 1) * 512], in_=pj[0:nt, :], func=AF.Copy), reads=[bpj], writes=[BA])
                P.op("dve", lambda e: e.tensor_copy(out=af32[0:nt, j * 512:(j + 1) * 512], in_=pj[0:nt, :]), reads=[bpj], writes=[B["af32"]])
            proj_tok(("a", j), ev_a)
        if kind == "M":
            yield "D"
            return
        for j in range(2):
            w, bw = load_block(("zp", j))
            proj_fm(w, bw, lambda pj, bpj, j=j: P.op("act", lambda e: e.activation(
                out=szpT[:, 4 * j:4 * j + 4, 0:nt], in_=pj[:, 0:4 * nt].rearrange("p (c t) -> p c t", t=nt), func=AF.Silu),
                reads=[bpj], writes=[B["szpT"]]))
        for j in range(2):
            proj_tok(("gp", j), lambda pj, bpj, j=j: P.op("act", lambda e: e.activation(out=upg[0:nt, j * 512:(j + 1) * 512], in_=pj[0:nt, :], func=AF.Sigmoid),
                                                          reads=[bpj], writes=[B["upg"]]))
        for j in range(2):
            proj_tok(("gm", j), lambda pj, bpj, j=j: P.op("act", lambda e: e.activation(out=sgm[0:nt, j * 512:(j + 1) * 512], in_=pj[0:nt, :], func=AF.Sigmoid),
                                                          reads=[bpj], writes=[B["sgm"]]))
        if kind == "S":
            hists = [(hs[0], B["hs0"], 120, "phS%d_0"), (hs[1], B["hs1"], 120, "phS%d_1")]
            pck = "pcS%d"
        elif kind == "M":
            hists = []
            pck = "pcM%d"
        else:
            pck = "pcP%d"
            if ci == 0:
                hists = [(Aprev, BAprev, 16, "phP0%d_0")] if with_meta else []
            else:
                hists = [(Aprev, BAprev, 128, "phP%d_0")]
        for half in range(2):
            pjv = rstate.get("pj", 0)
            rstate["pj"] = (pjv + 1) % NROT
            pj, bpj = PJR[pjv], B[PJN[pjv]]
            for c4 in range(4):
                cc = half * 4 + c4
                g = cc // 2
                terms = [(A, BA, nt, pck % g)] + [(ht, bh, kk, nm_ % g) for (ht, bh, kk, nm_) in hists]
                for ti, (src, bsrc, kk, pname) in enumerate(terms):
                    P.op("pe", lambda e, c4=c4, cc=cc, src=src, kk=kk, pname=pname, ti=ti, nterm=len(terms), pj=pj: e.matmul(
                        out=pj[:, c4 * nt:(c4 + 1) * nt], lhsT=src[0:kk, cc * 128:(cc + 1) * 128], rhs=cbs(pname)[0:kk, 0:nt],
                        start=(ti == 0), stop=(ti == nterm - 1)),
                        reads=[bsrc, B["cb"]], writes=[bpj])
            P.op("dve", lambda e, half=half, pj=pj: e.tensor_copy(out=pooledT[:, 4 * half:4 * half + 4, 0:nt],
                                                                  in_=pj[:, 0:4 * nt].rearrange("p (c t) -> p c t", t=nt)),
                 reads=[bpj], writes=[B["pooledT"]])
        for half in range(2):
            pjv = rstate.get("pj", 0)
            rstate["pj"] = (pjv + 1) % NROT
            pj, bpj = PJR[pjv], B[PJN[pjv]]
            for d4 in range(4):
                dc = half * 4 + d4
                g, dl = dc // 2, dc % 2
                for cl in range(2):
                    P.op("pe", lambda e, d4=d4, g=g, dl=dl, cl=cl, pj=pj: e.matmul(
                        out=pj[:, d4 * nt:(d4 + 1) * nt], lhsT=wpl[:, g, cl, dl * 128:(dl + 1) * 128], rhs=pooledT[:, 2 * g + cl, 0:nt],
                        start=(cl == 0), stop=(cl == 1)),
                        reads=[B["wpl"], B["pooledT"]], writes=[bpj])
            for d4 in range(4):
                dc = half * 4 + d4
                P.op("dve", lambda e, d4=d4, dc=dc, pj=pj: e.scalar_tensor_tensor(
                    out=ypT[:, dc, 0:nt], in0=pj[:, d4 * nt:(d4 + 1) * nt], scalar=vec[:, 8 + dc:9 + dc], in1=szpT[:, dc, 0:nt],
                    op0=ALU.mult, op1=ALU.mult),
                    reads=[bpj, B["vec"], B["szpT"]], writes=[B["ypT"]])
        for j in range(2):
            w, bw = load_block(("bp", j))
            pjv = rstate.get("pj", 0)
            rstate["pj"] = (pjv + 1) % NROT
            pj, bpj = PJR[pjv], B[PJN[pjv]]
            for pc in range(8):
                P.op("pe", lambda e, pc=pc, w=w, pj=pj: e.matmul(out=pj[0:nt, :], lhsT=ypT[:, pc, 0:nt], rhs=w[:, pc, :], start=(pc == 0), stop=(pc == 7)),
                     reads=[B["ypT"], bw], writes=[bpj])
            P.op("dve", lambda e, j=j, pj=pj: e.tensor_tensor(out=upg[0:nt, j * 512:(j + 1) * 512], in0=pj[0:nt, :], in1=upg[0:nt, j * 512:(j + 1) * 512], op=ALU.mult),
                 reads=[bpj, B["upg"]], writes=[B["upg"]])
        yield "D"
        ytrans(3)
        for j in range(2):
            pjv = rstate.get("pj", 0)
            rstate["pj"] = (pjv + 1) % NROT
            pj, bpj = PJR[pjv], B[PJN[pjv]]
            for mh in range(2):
                w, bw = load_block(("bm", j, mh))
                for m8 in range(8):
                    mc = mh * 8 + m8
                    P.op("pe", lambda e, mc=mc, m8=m8, w=w, pj=pj: e.matmul(out=pj[0:nt, :], lhsT=yT[:, mc, 0:nt], rhs=w[:, m8, :],
                                                                          start=(mc == 0), stop=(mc == 15)),
                         reads=[B["yT%d" % (mc // 4)], bw], writes=[bpj])
            P.op("dve", lambda e, j=j, pj=pj: e.tensor_tensor(out=sgm[0:nt, j * 512:(j + 1) * 512], in0=pj[0:nt, :], in1=sgm[0:nt, j * 512:(j + 1) * 512], op=ALU.mult),
                 reads=[bpj, B["sgm"]], writes=[B["sgm"]])
            P.op("pool", lambda e, j=j: e.tensor_tensor(out=ubf[0:nt, j * 512:(j + 1) * 512], in0=sgm[0:nt, j * 512:(j + 1) * 512],
                                                        in1=upg[0:nt, j * 512:(j + 1) * 512], op=ALU.add),
                 reads=[B["sgm"], B["upg"]], writes=[B["ubf"]])
        for dc in range(8):
            P.op("pe", lambda e, dc=dc: e.transpose(out=TP[:, dc * nt:(dc + 1) * nt], in_=ubf[0:nt, dc * 128:(dc + 1) * 128], identity=cbs_ident[0:nt, 0:nt]),
                 reads=[B["ubf"], B["cb"]], writes=[B["TP"]])
        P.op("act", lambda e: e.activation(out=uT[:, :, 0:nt], in_=TP[:, 0:8 * nt].rearrange("p (c t) -> p c t", t=nt), func=AF.Copy),
             reads=[B["TP"]], writes=[B["uT"]])
        for j in range(2):
            w, bw = load_block(("wo", j))
            pjv = rstate.get("pj", 0)
            rstate["pj"] = (pjv + 1) % NROT
            pj, bpj = PJR[pjv], B[PJN[pjv]]
            for dc in range(8):
                P.op("pe", lambda e, dc=dc, w=w, pj=pj: e.matmul(out=pj[0:nt, :], lhsT=uT[:, dc, 0:nt], rhs=w[:, dc, :], start=(dc == 0), stop=(dc == 7)),
                     reads=[B["uT"], bw], writes=[bpj])
            P.op("dve", lambda e, j=j, pj=pj: e.tensor_tensor(out=xo[0:nt, j * 512:(j + 1) * 512], in0=pj[0:nt, :], in1=X[0:nt, j * 512:(j + 1) * 512], op=ALU.add),
                 reads=[bpj, BX], writes=[B["xo"]])
        P.op("act", lambda e: e.activation(out=upg[0:nt, :], in_=xo[0:nt, :], func=AF.Square, accum_out=sm1[0:nt, 3:4]),
             reads=[B["xo"]], writes=[B["upg"], B["sm1t"]])
        P.op("act", lambda e: e.activation(out=sm1[0:nt, 3:4], in_=sm1[0:nt, 3:4], func=AF.Ln, scale=1.0 / D, bias=EPS),
             reads=[B["sm1t"]], writes=[B["sm1t"]])
        P.op("act", lambda e: e.activation(out=sm1[0:nt, 3:4], in_=sm1[0:nt, 3:4], func=AF.Exp, scale=-0.5),
             reads=[B["sm1t"]], writes=[B["sm1t"]])
        if y_dst is not None:
            P.op("dve", lambda e: e.scalar_tensor_tensor(out=xo[0:nt, :], in0=xo[0:nt, :], scalar=sm1[0:nt, 3:4], in1=fg[0:nt, :],
                                                         op0=ALU.mult, op1=ALU.mult),
                 reads=[B["xo"], B["sm1t"], B["fg"]], writes=[B["xo"]])
            P.dma("pool", lambda e: e.dma_start(out=y_dst, in_=xo[0:nt, :]), yo_ds, reads=[B["xo"]])

    passes = []
    if with_meta:
        passes.append(("M", meta[:, :], None, 0))
    for ci in range(nchunk):
        passes.append(("P", xp[ci * 128:(ci + 1) * 128, :], y_p[ci * 128:(ci + 1) * 128, :], ci))
    if with_sample:
        passes.append(("S", xs[:, :], y_s[:, :], 0))
        P.dma("pool", lambda e: e.dma_start(out=pool_s[:, 0:11, :], in_=spool.ap().rearrange("(s j) d -> s j d", j=15)[:, 4:15, :]),
              out_ds)
    gens = [run_pass(*p) for p in passes]
    last_p = max([i for i, p in enumerate(passes) if p[0] == "P"])

    def adv(i):
        try:
            next(gens[i])
        except StopIteration:
            pass

    def after_C2(i):
        if passes[i][0] == "S":
            P.dma("pool", lambda e: e.dma_start(out=nT_s[:, :], in_=nst[:, :]), o_ds["nTs"], reads=[B["nst"]])
            P.dma("pool", lambda e: e.dma_start(out=m_s[:, :], in_=decb[0:1, 64:128]), o_ds["ms"], reads=[B["decb"]])
        if i == last_p:
            for h in range(NH):
                P.dma("pool", lambda e, h=h: e.dma_start(out=C_p[h].rearrange("(dc p) v -> p dc v", p=128), in_=Cst[h][:]), cout_ds[h],
                      reads=[B["Cst%d" % h]])
            P.dma("pool", lambda e: e.dma_start(out=nT_p[:, :], in_=nst[:, 0:64]), o_ds["nTp"], reads=[B["nst"]])
            P.dma("pool", lambda e: e.dma_start(out=m_p[:, :], in_=decb[0:1, 64:128]), o_ds["mp"], reads=[B["decb"]])

    def after_D(i):
        if i == last_p:
            P.dma("pool", lambda e: e.dma_start(out=pool_p[:, :], in_=af32[113:128, :]), o_ds["pp"], reads=[B["af32"]])

    def after_E(i):
        if passes[i][0] == "S":
            P.dma("pool", lambda e: [e.dma_start(out=pool_s[s, 11:15, :], in_=af32[4 * s:4 * s + 4, :]) for s in range(16)],
                  o_ds["ps"], reads=[B["af32"]], n=16)

    npass = len(gens)

    def genA(i):
        P.defer_mode = "A"
        adv(i)
        P.defer_mode = None

    genA(0)
    P.flush()
    adv(0)
    for i in range(npass):
        if i + 1 < npass:
            genA(i + 1)
        adv(i)
        adv(i)
        P.flush()
        after_C2(i)
        adv(i)
        after_D(i)
        if i + 1 < npass:
            adv(i + 1)
        adv(i)
        after_E(i)
    P.emit()
    return nc, (cf_np, cb_np), P


def _core_inputs(b, x_prompt, x_sample, state_pool, state_C, state_n, state_m, meta_tokens, norm_g, w_in, gate_bias,
                 w_pool, pool_scale, mh_norm_g, w_branch_pool, w_branch_mlstm, w_out, final_g, consts, nchunk=16):
    cf_np, cb_np = consts
    f = np.float32
    sl = slice(16 * b, 16 * b + 16)
    vecs = np.zeros((128, 128), f)
    vecs[:, 0:8] = np.asarray(norm_g[0], f).reshape(8, 128).T
    vecs[:, 8:16] = np.asarray(pool_scale[0], f).reshape(8, 128).T
    vecs[:, 16:32] = np.asarray(mh_norm_g[0], f).reshape(16, 128).T
    vecs[0:4, 32] = np.asarray(gate_bias[0, 0:4], f)
    vecs[0:4, 33] = np.asarray(gate_bias[0, 4:8], f)
    sn = np.asarray(state_n[0, sl], f)
    snT = np.ascontiguousarray(sn.reshape(16, 4, 4, 128).transpose(3, 1, 2, 0)).reshape(128, 256)
    smT = np.zeros((4, 64), f)
    smT[:, 0:16] = np.asarray(state_m[0, sl], f).T
    return {
        "xp": np.ascontiguousarray(x_prompt[b][:128 * nchunk], f),
        "xs": np.ascontiguousarray(np.asarray(x_sample[sl], f).reshape(64, D)),
        "spool": np.ascontiguousarray(np.asarray(state_pool[0, sl], f).reshape(240, D)),
        "sC": np.ascontiguousarray(state_C[0, sl], f),
        "snT": snT, "smT": smT,
        "meta": np.ascontiguousarray(meta_tokens, f),
        "w_in": np.ascontiguousarray(w_in[0], f),
        "w_pool": np.ascontiguousarray(w_pool[0], f),
        "wbp": np.ascontiguousarray(w_branch_pool[0], f),
        "wbm": np.ascontiguousarray(w_branch_mlstm[0], f),
        "wout": np.ascontiguousarray(w_out[0], f),
        "vecs": vecs,
        "fgb": np.ascontiguousarray(np.asarray(final_g, f).reshape(1, D)),
        "cf": cf_np, "cb": cb_np,
    }


def _gather(results, nb):
    f = np.float32
    y_p = np.stack([r["y_p"] for r in results]).astype(f)
    y_s = np.concatenate([r["y_s"].reshape(16, 4, D) for r in results]).astype(f)
    pool_p = np.stack([r["pool_p"] for r in results])[None].astype(f)
    C_p = np.stack([r["C_p"] for r in results])[None].astype(f)
    n_p = np.stack([r["nT_p"][:, 0:16].reshape(128, 4, 4).transpose(1, 2, 0).reshape(4, 512) for r in results])[None].astype(f)
    m_p = np.stack([r["m_p"][0, 0:4] for r in results])[None].astype(f)
    pool_s = np.concatenate([r["pool_s"] for r in results])[None].astype(f)
    C_s = np.concatenate([r["C_s"] for r in results])[None].astype(f)
    n_s = np.concatenate([r["nT_s"].reshape(128, 4, 4, 16).transpose(3, 1, 2, 0).reshape(16, 4, 512) for r in results])[None].astype(f)
    m_s = np.concatenate([r["m_s"][0].reshape(16, 4) for r in results])[None].astype(f)
    return (y_p, y_s, pool_p, C_p, n_p, m_p, pool_s, C_s, n_s, m_s)


_CACHE = {}


def kernel(**inputs):
    if "nc" not in _CACHE:
        _CACHE["nc"] = build()
    nc, consts, _ = _CACHE["nc"]
    in_maps = [_core_inputs(b, consts=consts, **inputs) for b in range(8)]
    res = run_bass_kernel_spmd(nc, in_maps, core_ids=list(range(8)))
    return _gather(res.results, 8)
```

```python
import math
import numpy as np
import concourse.bass as bass
import concourse.mybir as mybir
from concourse.bass_utils import run_bass_kernel_spmd

F32 = mybir.dt.float32
BF16 = mybir.dt.bfloat16
AF = mybir.ActivationFunctionType
ALU = mybir.AluOpType
AX = mybir.AxisListType

D = 1024
NIN = 14344
NH = 4
HD = 512
EPS = 1e-6
C_A, C_ZP, C_Q, C_K, C_V, C_O, C_ZM, C_I, C_F, C_GP, C_GM = 0, 1024, 2048, 4096, 6144, 8192, 10240, 12288, 12292, 12296, 13320
LNK = -0.5 * math.log(HD)
import os as _os
TICK_ON = _os.environ.get("TICK", "1") == "1"


class Buf:
    __slots__ = ("name", "w", "rs", "excl")

    def __init__(self, name, excl=False):
        self.name = name
        self.w = None
        self.rs = []
        self.excl = excl


class DSem:
    def __init__(self, sem):
        self.sem = sem
        self.count = 0


class Op:
    __slots__ = ("eng", "fn", "reads", "writes", "dsem", "dval", "deps", "signal", "sigval", "waits", "idx", "ndma")


class Prog:
    ENGS = ("pe", "act", "dve", "pool", "sp")

    def __init__(self, nc):
        self.nc = nc
        self.ops = []
        self.esem = {e: nc.alloc_semaphore("es_" + e) for e in self.ENGS}
        self.dsems = []
        self.nbuf = 0
        self.defer_mode = None
        self.deferred = {}
        self._open_group = {}
        self.ngen = {}
        self.nflushed = {}

    def buf(self, name=None, excl=False):
        self.nbuf += 1
        return Buf(name or ("b%d" % self.nbuf), excl)

    def dsem(self, name):
        d = DSem(self.nc.alloc_semaphore("ds_" + name))
        self.dsems.append(d)
        return d

    def op(self, eng, fn, reads=(), writes=(), hold=False):
        o = Op()
        o.eng = eng
        o.fn = fn
        rd = [b for b in reads if b is not None]
        wr = [b for b in writes if b is not None]
        o.writes = wr + [b for b in rd if b.excl]
        o.reads = [b for b in rd if not b.excl]
        o.dsem = None
        o.dval = 0
        o.signal = False
        o.sigval = 0
        if self.defer_mode:
            o.idx = -1
            q = self.deferred.setdefault(self.defer_mode, [])
            if self._open_group.get(self.defer_mode, False) and q:
                q[-1].append(o)
            else:
                q.append([o])
                self.ngen[self.defer_mode] = self.ngen.get(self.defer_mode, 0) + 1
            self._open_group[self.defer_mode] = hold
        else:
            o.idx = len(self.ops)
            self.ops.append(o)
        return o

    def flush(self, n=None, q=None):
        for name in ([q] if q is not None else list(self.deferred.keys())):
            lst = self.deferred.get(name, [])
            k = len(lst) if n is None else min(n, len(lst))
            for grp in lst[:k]:
                for o in grp:
                    o.idx = len(self.ops)
                    self.ops.append(o)
                    if o.dsem is not None:
                        o.dsem.count += 16 * o.ndma
                        o.dval = o.dsem.count
            del lst[:k]
            self.nflushed[name] = self.nflushed.get(name, 0) + k
            if not lst:
                self._open_group[name] = False

    def flush_upto(self, mark, q):
        need = mark - self.nflushed.get(q, 0)
        if need > 0:
            self.flush(need, q)

    def dma(self, eng, fn, dsem, reads=(), writes=(), n=1, hold=False):
        o = self.op(eng, fn, reads, writes, hold=hold)
        o.dsem = dsem
        o.ndma = n
        if o.idx >= 0:
            dsem.count += 16 * n
            o.dval = dsem.count
        return o

    def _skip(self, d, o):
        return d.eng == o.eng and d.eng == "pe" and o.dsem is None and d.dsem is None

    def finalize(self):
        for o in self.ops:
            deps = {}
            for b in o.reads:
                if b.w is not None:
                    deps[b.w.idx] = b.w
            for b in o.writes:
                if b.w is not None:
                    deps[b.w.idx] = b.w
                for r in b.rs:
                    deps[r.idx] = r
            deps.pop(o.idx, None)
            for b in o.reads:
                b.rs.append(o)
            for b in o.writes:
                b.w = o
                b.rs = []
            o.deps = list(deps.values())
        for o in self.ops:
            for d in o.deps:
                if d.dsem is None and not self._skip(d, o):
                    d.signal = True
        cnt = {e: 0 for e in self.ENGS}
        for o in self.ops:
            if o.dsem is None and o.signal:
                cnt[o.eng] += 1
                o.sigval = cnt[o.eng]
        seen = {e: {} for e in self.ENGS}
        nw = 0
        for o in self.ops:
            need = {}
            for d in o.deps:
                if d.dsem is not None:
                    key, sem, val = ("d", id(d.dsem)), d.dsem.sem, d.dval
                else:
                    if self._skip(d, o):
                        continue
                    key, sem, val = ("e", d.eng), self.esem[d.eng], d.sigval
                if seen[o.eng].get(key, 0) >= val:
                    continue
                if key not in need or need[key][1] < val:
                    need[key] = (sem, val)
            o.waits = list(need.values())
            for key, (sem, val) in need.items():
                seen[o.eng][key] = val
            nw += len(o.waits)
        self.nwaits = nw

    def emit(self):
        nc = self.nc
        self.finalize()
        prog = self

        def run(ename, e):
            for o in prog.ops:
                if o.eng != ename:
                    continue
                for sem, val in o.waits:
                    e.wait_ge(sem, val)
                r = o.fn(e)
                if o.dsem is not None:
                    if not isinstance(r, (list, tuple)):
                        r = [r]
                    for ins in r:
                        ins.then_inc(o.dsem.sem, 16)
                elif o.signal:
                    r.then_inc(prog.esem[ename], 1)
            if ename == "sp":
                for d in prog.dsems:
                    if d.count > 0:
                        e.wait_ge(d.sem, d.count)

        with nc.Block() as block:
            @block.tensor
            def _(e):
                run("pe", e)

            @block.scalar
            def _(e):
                run("act", e)

            @block.vector
            def _(e):
                run("dve", e)

            @block.gpsimd
            def _(e):
                run("pool", e)

            @block.sync
            def _(e):
                run("sp", e)


POOL_WINDOWS = (2, 4, 8, 16)


def _pool_mats(kind):
    cur, hist = [], []
    for w in POOL_WINDOWS:
        if kind == "P":
            c = np.zeros((128, 128), np.float32)
            h = np.zeros((128, 128), np.float32)
            for t in range(128):
                for j in range(t - w + 1, t + 1):
                    if j >= 0:
                        c[j, t] += 1.0 / w
                    else:
                        h[128 + j, t] += 1.0 / w
                c[t, t] -= 1.0
            cur.append(c)
            hist.append([h])
        elif kind == "P0":
            h = np.zeros((16, 128), np.float32)
            for t in range(128):
                for j in range(t - w + 1, t + 1):
                    if j < 0:
                        h[16 + j, t] += 1.0 / w
            hist.append([h])
            cur.append(None)
        elif kind == "M":
            c = np.zeros((16, 16), np.float32)
            for t in range(16):
                cnt = min(t + 1, w)
                for j in range(max(0, t - w + 1), t + 1):
                    c[j, t] += 1.0 / cnt
                c[t, t] -= 1.0
            cur.append(c)
            hist.append([])
        elif kind == "S":
            c = np.zeros((64, 64), np.float32)
            h0 = np.zeros((120, 64), np.float32)
            h1 = np.zeros((120, 64), np.float32)
            for s in range(16):
                for t in range(4):
                    col = s * 4 + t
                    for e in range(15 + t - w + 1, 15 + t + 1):
                        if e >= 15:
                            c[s * 4 + (e - 15), col] += 1.0 / w
                        else:
                            r = s * 15 + e
                            if r < 120:
                                h0[r, col] += 1.0 / w
                            else:
                                h1[r - 120, col] += 1.0 / w
                    c[col, col] -= 1.0
            cur.append(c)
            hist.append([h0, h1])
    return cur, hist


class Pack:
    def __init__(self):
        self.items = []
        self.off = {}
        self.w = 0

    def add(self, name, arr):
        arr = np.asarray(arr, np.float32)
        assert arr.ndim == 2 and arr.shape[0] <= 128
        self.off[name] = (self.w, arr.shape[0], arr.shape[1])
        self.items.append(arr)
        self.w += arr.shape[1]

    def build(self):
        w = max(64, self.w)
        out = np.zeros((128, w), np.float32)
        for (name, (o, r, c)), a in zip(self.off.items(), self.items):
            out[:r, o:o + c] = a
        return out


def _struct_consts():
    pf = Pack()
    pb = Pack()
    pf.add("ident", np.eye(128))
    for kind, ntok, nseq, L in (("S", 64, 16, 4), ("M", 16, 1, 16), ("P", 128, 1, 128)):
        seq = np.arange(ntok) // L
        same = (seq[:, None] == seq[None, :])
        causal = same & (np.arange(ntok)[:, None] <= np.arange(ntok)[None, :])
        pf.add("mask" + kind, causal.astype(np.float32))
        E = (np.arange(nseq)[:, None] == seq[None, :]).astype(np.float32)
        pb.add("ET" + kind, E.T)
    pf.add("ones4", np.ones((4, 128)))
    pf.add("I4", np.eye(4))
    for kind in ("S", "M", "P", "P0"):
        cur, hist = _pool_mats(kind)
        for g in range(4):
            if cur[g] is not None:
                pb.add("pc%s%d" % (kind, g), cur[g])
            for j, h in enumerate(hist[g]):
                pb.add("ph%s%d_%d" % (kind, g, j), h)
    cm = np.zeros((16, 64), np.float32)
    for s in range(16):
        cm[s, 4 * s:4 * s + 4] = 1.0
    pb.add("colmask", np.broadcast_to(cm.reshape(1, 1024), (128, 1024)))
    pb.add("ones", np.ones((128, 8)))
    pb.add("identb", np.eye(128))
    seqS = np.arange(64) // 4
    pf.add("ETf", (seqS[:, None] == np.arange(16)[None, :]).astype(np.float32))
    return pf, pb


def build(nchunk=16, with_sample=True, with_meta=True, dbg=None):
    nc = bass.Bass("TRN2", target_bir_lowering=False)
    P = Prog(nc)
    SEQ = 128 * nchunk
    pf, pb = _struct_consts()
    cf_np = pf.build()
    cb_np = pb.build()

    def din(name, shape, dt=F32):
        return nc.dram_tensor(name, list(shape), dt, kind="ExternalInput")

    def dout(name, shape, dt=F32):
        return nc.dram_tensor(name, list(shape), dt, kind="ExternalOutput")

    xp = din("xp", [SEQ, D])
    xs = din("xs", [64, D])
    spool = din("spool", [240, D])
    sC = din("sC", [16, NH, HD, HD])
    snT = din("snT", [128, 256])
    smT = din("smT", [4, 64])
    meta = din("meta", [16, D])
    w_in = din("w_in", [D, NIN])
    w_pool = din("w_pool", [4, 256, 256])
    wbp = din("wbp", [D, D])
    wbm = din("wbm", [2 * D, D])
    wout = din("wout", [D, D])
    vecs = din("vecs", [128, 128])
    fgb = din("fgb", [1, D])
    cfd = din("cf", list(cf_np.shape))
    cbd = din("cb", list(cb_np.shape))
    y_p = dout("y_p", [SEQ, D])
    y_s = dout("y_s", [64, D])
    pool_p = dout("pool_p", [15, D])
    C_p = dout("C_p", [NH, HD, HD])
    nT_p = dout("nT_p", [128, 64])
    m_p = dout("m_p", [1, 64])
    pool_s = dout("pool_s", [16, 15, D])
    C_s = dout("C_s", [16, NH, HD, HD])
    nT_s = dout("nT_s", [128, 256])
    m_s = dout("m_s", [1, 64])
    NBLK = 36
    wblk = nc.dram_tensor("wblk", [NBLK, 128, 8, 512], BF16, kind="Internal")

    def sb(name, shape, dt=F32):
        return nc.alloc_sbuf_tensor(name, list(shape), dt)

    cf = sb("cf_t", cf_np.shape)
    cb = sb("cb_t", cb_np.shape, BF16)
    vec = sb("vec_t", [128, 128])
    fg = sb("fg_t", [128, D])
    wg = sb("wg_t", [128, 8, 64], BF16)
    wpl = sb("wpl_t", [128, 4, 2, 256], BF16)
    NSLOT = 6
    ring = [sb("ring%d" % i, [128, 8, 512], BF16) for i in range(NSLOT)]
    xt = [sb("xt%d" % i, [128, D]) for i in range(3)]
    hnT2 = [sb("hnT%d" % i, [128, 8, 128], BF16) for i in range(2)]
    atok = [sb("atok%d" % i, [128, D], BF16) for i in range(2)]
    af32 = sb("af32", [128, D])
    hs = [sb("hs%d" % i, [120, D], BF16) for i in range(2)]
    szpT = sb("szpT", [128, 8, 128])
    pooledT = sb("pooledT", [128, 8, 128], BF16)
    ypT = sb("ypT", [128, 8, 128], BF16)
    sgm = sb("sgm", [128, D])
    upg = sb("upg", [128, D])
    ubf = sb("ubf", [128, D], BF16)
    uT = sb("uT", [128, 8, 128], BF16)
    xo = sb("xo", [128, D])
    NHB = 2
    qT = [sb("qT%d" % i, [128, 4, 128], BF16) for i in range(NHB)]
    kT = [sb("kT%d" % i, [128, 4, 128], BF16) for i in range(NHB)]
    kw2 = [sb("kw2%d" % i, [128, 512], BF16) for i in range(NHB)]
    qtok = [sb("qtok%d" % i, [128, 512], BF16) for i in range(NHB)]
    ktok = [sb("ktok%d" % i, [128, 512], BF16) for i in range(NHB)]
    vtok = [sb("vtok%d" % i, [128, 512], BF16) for i in range(NHB)]
    Gt = [sb("G%d" % i, [128, 512]) for i in range(NHB)]
    GR = [sb("GR%d" % i, [128, 512]) for i in range(NHB)]
    SWT = [sb("SWT%d" % i, [128, 128], BF16) for i in range(NHB)]
    yln = sb("yln", [128, 2048], BF16)
    yT = sb("yT", [128, 16, 128], BF16)
    Cst = [sb("Cst%d" % i, [128, 4, 512]) for i in range(NH)]
    Cbf = [sb("Cbf%d" % i, [128, 4, 512], BF16) for i in range(NH)]
    nst = sb("nst", [128, 256])
    nbf = sb("nbf", [128, 256], BF16)
    qTm = [sb("qTm%d" % i, [128, 4, 64], BF16) for i in range(2)]
    vm = [sb("vm%d" % i, [64, 512], BF16) for i in range(2)]
    igT = sb("igT", [4, 128])
    efT = sb("efT", [4, 128])
    cs = [sb("cs%d" % i, [4, 128]) for i in range(2)]
    gT = sb("gT", [4, 128])
    argT = sb("argT", [4, 128])
    G3 = sb("G3", [96, 128])
    G3T = sb("G3T", [128, 96])
    mprevT = sb("mprevT", [4, 64])
    mxT = sb("mxT", [4, 16])
    gmax = sb("gmax", [4, 16])
    decT = sb("decT", [4, 16])
    mnewT = sb("mnewT", [4, 16])
    DM = sb("DM", [4, 128])
    decb = sb("decb", [128, 128])
    ngbf = sb("ngbf", [4, 1])
    sm1 = sb("sm1", [128, 16])
    st6 = sb("st6", [128, 8])
    ntmp = sb("ntmp", [128, 16])

    def ps(name, shape, dt=F32):
        return nc.alloc_psum_tensor(name, list(shape), dt)

    NPJ = 3
    PJ = [ps("PJ%d" % i, [128, 512]) for i in range(NPJ)]
    TP = ps("TP", [128, 1024], BF16)
    STp = ps("STp", [128, 512])
    NUM = [ps("NUM%d" % i, [128, 512]) for i in range(1)]
    CU = [ps("CU%d" % i, [128, 512]) for i in range(2)]
    PJR = PJ + CU
    PJN = ["PJ0", "PJ1", "PJ2", "CU0", "CU1"]
    NROT = len(PJR)

    B = {}

    def bf(name, excl=False):
        B[name] = P.buf(name, excl)
        return B[name]

    for n_ in ("cf cb vec fg wg wpl xsq hnT0 hnT1 sm1p sm1t sm1h0 sm1h1 sm1h2 sm1h3 af32 hs0 hs1 szpT pooledT ypT sgp sgm upg ubf uT xo yout yln0 yln1 yln2 yln3 yT0 yT1 yT2 yT3 nst nbf "
               "igT efT cs0 cs1 gT argT G3 G3T mprevT mxT gmax decT mnewT DM decb ngbf sm1 st6 ntmp xt0 xt1 xt2 atok0 atok1 "
               "qTm0 qTm1 vm0 vm1").split():
        bf(n_)
    for i in range(NHB):
        for n_ in ("qT", "kT", "kw2", "qtok", "ktok", "vtok", "og", "G", "GR", "SWT"):
            bf("%s%d" % (n_, i))
    for i in range(NH):
        bf("Cst%d" % i)
        bf("Cbf%d" % i)
    for i in range(NSLOT):
        bf("ring%d" % i)
    for n_ in ("PJ0", "PJ1", "PJ2", "TP", "STp", "NUM0", "CU0", "CU1"):
        bf(n_, excl=True)
    wbuf = [P.buf("wblk%d" % j) for j in range(NBLK)]
    ring_ds = [P.dsem("ring%d" % i) for i in range(NSLOT)]
    x_ds = [P.dsem("x%d" % i) for i in range(3)]
    c_ds = [P.dsem("c%d" % i) for i in range(NH)]
    out_ds = P.dsem("out")
    cout_ds = [P.dsem("co%d" % i) for i in range(NH)]
    o_ds = {k: P.dsem("o_" + k) for k in ("nTs", "ms", "ps", "nTp", "mp", "pp")}
    yo_ds = P.dsem("yo")
    misc_ds = [P.dsem("misc%d" % i) for i in range(12)]
    cv_ds = [P.dsem("cv%d" % i) for i in range(NBLK)]

    def cfs(name):
        o, r, c = pf.off[name]
        return cf[0:r, o:o + c]

    def cbs(name):
        o, r, c = pb.off[name]
        return cb[0:r, o:o + c]

    blocks = []
    BI = {}

    def addblk(key, t, r0, c0):
        BI[key] = len(blocks)
        blocks.append((t, r0, c0))

    for h in range(NH):
        addblk(("q", h), w_in, 0, C_Q + 512 * h)
        addblk(("k", h), w_in, 0, C_K + 512 * h)
        addblk(("v", h), w_in, 0, C_V + 512 * h)
        addblk(("o", h), w_in, 0, C_O + 512 * h)
        addblk(("z", h), w_in, 0, C_ZM + 512 * h)
    for j in range(2):
        addblk(("a", j), w_in, 0, C_A + 512 * j)
    for j in range(2):
        addblk(("zp", j), w_in, 0, C_ZP + 512 * j)
    for j in range(2):
        addblk(("gp", j), w_in, 0, C_GP + 512 * j)
    for j in range(2):
        addblk(("gm", j), w_in, 0, C_GM + 512 * j)
    for j in range(2):
        addblk(("bp", j), wbp, 0, 512 * j)
    for j in range(2):
        for mh in range(2):
            addblk(("bm", j, mh), wbm, 1024 * mh, 512 * j)
    for j in range(2):
        addblk(("wo", j), wout, 0, 512 * j)
    assert len(blocks) == NBLK

    P.dma("sp", lambda e: e.dma_start(out=cf[:], in_=cfd[:, :]), misc_ds[0], writes=[B["cf"]])
    P.dma("pool", lambda e: e.dma_start(out=cb[:], in_=cbd[:, :]), misc_ds[1], writes=[B["cb"]])
    P.dma("sp", lambda e: e.dma_start(out=vec[:], in_=vecs[:, :]), misc_ds[2], writes=[B["vec"]])
    P.dma("sp", lambda e: e.dma_start(out=fg[:], in_=fgb[0:1, :].partition_broadcast(128)), misc_ds[3], writes=[B["fg"]])
    P.dma("pool", lambda e: e.dma_start(out=wg[:], in_=w_in[:, C_I - 56:C_I + 8].rearrange("(kc p) c -> p kc c", p=128)),
          misc_ds[4], writes=[B["wg"]])
    P.dma("pool", lambda e: e.dma_start(out=wpl[:], in_=w_pool.ap().rearrange("g (cl c) d -> c g cl d", c=128)),
          misc_ds[5], writes=[B["wpl"]])
    for j, (t, r0, c0) in enumerate(blocks):
        P.dma("pool", lambda e, j=j, t=t, r0=r0, c0=c0: e.dma_start(
            out=wblk[j].rearrange("p kc c -> kc p c"),
            in_=t[r0:r0 + 1024, c0:c0 + 512].rearrange("(kc p) c -> kc p c", p=128)),
            cv_ds[j], writes=[wbuf[j]])
    P.op("dve", lambda e: e.tensor_scalar(out=ngbf[:], in0=vec[0:4, 33:34], scalar1=-1.0, scalar2=None, op0=ALU.mult),
         reads=[B["vec"]], writes=[B["ngbf"]])
    P.op("pool", lambda e: e.memset(G3[:], 0.0), writes=[B["G3"]])
    P.op("pool", lambda e: e.memset(sm1[:], 0.0), writes=[B["sm1p"], B["sm1t"], B["sm1h0"], B["sm1h1"], B["sm1h2"], B["sm1h3"]])
    P.op("pool", lambda e: e.memset(DM[:], 0.0), writes=[B["DM"]])
    P.op("pool", lambda e: e.memset(nst[:], 0.0), writes=[B["nst"]])

    rstate = {"slot": 0, "xi": 0, "ai": 0}

    def tick(n=1):
        if TICK_ON:
            P.flush(n)

    def load_block(key):
        j = BI[key]
        s = rstate["slot"]
        rstate["slot"] = (s + 1) % NSLOT
        P.dma("sp", lambda e, j=j, s=s: e.dma_start(out=ring[s][:], in_=wblk[j]), ring_ds[s],
              reads=[wbuf[j]], writes=[B["ring%d" % s]])
        return ring[s], B["ring%d" % s]

    def run_pass(kind, x_src, y_dst, ci=0):
        nt = {"S": 64, "M": 16, "P": 128}[kind]
        nseq = 16 if kind == "S" else 1
        L = nt // nseq
        zero_state = (kind == "M") or (kind == "P" and not with_meta and ci == 0)
        xi = rstate["xi"]
        rstate["xi"] = (xi + 1) % 3
        hi = rstate.get("hi", 0)
        rstate["hi"] = 1 - hi
        X, BX = xt[xi], B["xt%d" % xi]
        hnT, BhnT = hnT2[hi], B["hnT%d" % hi]
        ai = rstate["ai"]
        rstate["ai"] = 1 - ai
        A, BA = atok[ai], B["atok%d" % ai]
        Aprev, BAprev = atok[1 - ai], B["atok%d" % (1 - ai)]
        maskT = cfs("mask" + kind)
        lmark = {}
        tmark = [0]
        ident = cfs("ident")
        identb = None

        P.dma("sp", lambda e: e.dma_start(out=X[0:nt, :], in_=x_src), x_ds[xi], writes=[BX])
        if kind == "S":
            P.dma("sp", lambda e: e.dma_start(out=mprevT[:], in_=smT[:, :]), misc_ds[7], writes=[B["mprevT"]])
            for i_ in range(2):
                P.dma("pool", lambda e, i_=i_: e.dma_start(out=hs[i_][:], in_=spool[120 * i_:120 * (i_ + 1), :]),
                      misc_ds[9 + i_], writes=[B["hs%d" % i_]])
        elif zero_state:
            P.op("pool", lambda e: e.memset(mprevT[:], 0.0), writes=[B["mprevT"]])

        P.op("act", lambda e: e.activation(out=xo[0:nt, :], in_=X[0:nt, :], func=AF.Square, accum_out=sm1[0:nt, 0:1]),
             reads=[BX], writes=[B["xo"], B["sm1p"]])
        P.op("act", lambda e: e.activation(out=sm1[0:nt, 1:2], in_=sm1[0:nt, 0:1], func=AF.Ln, scale=1.0 / D, bias=EPS),
             reads=[B["sm1p"]], writes=[B["sm1p"]])
        P.op("act", lambda e: e.activation(out=sm1[0:nt, 2:3], in_=sm1[0:nt, 1:2], func=AF.Exp, scale=-0.5),
             reads=[B["sm1p"]], writes=[B["sm1p"]])
        P.op("dve", lambda e: e.tensor_scalar(out=xo[0:nt, :], in0=X[0:nt, :], scalar1=sm1[0:nt, 2:3], scalar2=None, op0=ALU.mult),
             reads=[BX, B["sm1p"]], writes=[B["xo"]])
        for half in range(2):
            pjv = rstate.get("pj", 0)
            rstate["pj"] = (pjv + 1) % NROT
            pj, bpj = PJR[pjv], B[PJN[pjv]]
            for q4 in range(4):
                kc = half * 4 + q4
                P.op("pe", lambda e, kc=kc, q4=q4, pj=pj: e.transpose(out=pj[:, q4 * nt:(q4 + 1) * nt], in_=xo[0:nt, kc * 128:(kc + 1) * 128],
                                                                     identity=ident[0:nt, 0:nt]),
                     reads=[B["xo"], B["cf"]], writes=[bpj], hold=True)
            for q4 in range(4):
                kc = half * 4 + q4
                P.op("dve", lambda e, kc=kc, q4=q4, pj=pj: e.tensor_scalar(out=hnT[:, kc, 0:nt], in0=pj[:, q4 * nt:(q4 + 1) * nt],
                                                                         scalar1=vec[:, kc:kc + 1], scalar2=None, op0=ALU.mult),
                     reads=[bpj, B["vec"]], writes=[BhnT], hold=(q4 != 3))

        pjv = rstate.get("pj", 0)
        rstate["pj"] = (pjv + 1) % NROT
        GP, bGP = PJR[pjv], B[PJN[pjv]]
        for gi in range(2):
            for kc in range(8):
                P.op("pe", lambda e, gi=gi, kc=kc: e.matmul(out=GP[0:4, gi * 128:gi * 128 + nt], lhsT=wg[:, kc, 56 + gi * 4:60 + gi * 4],
                                                            rhs=hnT[:, kc, 0:nt], start=(kc == 0), stop=(kc == 7)),
                     reads=[B["wg"], BhnT], writes=[bGP], hold=True)
        P.op("act", lambda e: e.activation(out=igT[:, 0:nt], in_=GP[0:4, 0:nt], func=AF.Identity, bias=vec[0:4, 32:33], scale=1.0),
             reads=[bGP, B["vec"]], writes=[B["igT"]], hold=True)
        P.op("act", lambda e: e.activation(out=efT[:, 0:nt], in_=GP[0:4, 128:128 + nt], func=AF.Exp, bias=ngbf[:, 0:1], scale=-1.0),
             reads=[bGP, B["ngbf"]], writes=[B["efT"]])
        P.op("act", lambda e: e.activation(out=cs[0][:, 0:nt], in_=efT[:, 0:nt], func=AF.Ln, bias=1.0, scale=1.0),
             reads=[B["efT"]], writes=[B["cs0"]])
        cur = 0
        sh = 1
        while sh < L:
            src, dst = cs[cur], cs[1 - cur]
            sv = src[:, 0:nt].rearrange("p (s l) -> p s l", l=L)
            dv = dst[:, 0:nt].rearrange("p (s l) -> p s l", l=L)
            P.op("dve", lambda e, sv=sv, dv=dv, sh=sh: e.tensor_copy(out=dv[:, :, 0:sh], in_=sv[:, :, 0:sh]),
                 reads=[B["cs%d" % cur]], writes=[B["cs%d" % (1 - cur)]])
            P.op("dve", lambda e, sv=sv, dv=dv, sh=sh: e.tensor_tensor(out=dv[:, :, sh:L], in0=sv[:, :, sh:L], in1=sv[:, :, 0:L - sh], op=ALU.add),
                 reads=[B["cs%d" % cur]], writes=[B["cs%d" % (1 - cur)]])
            cur = 1 - cur
            sh *= 2
        nbT, BnbT = cs[cur], B["cs%d" % cur]
        P.op("dve", lambda e: e.tensor_tensor(out=gT[:, 0:nt], in0=igT[:, 0:nt], in1=nbT[:, 0:nt], op=ALU.add),
             reads=[B["igT"], BnbT], writes=[B["gT"]])
        P.op("dve", lambda e: e.tensor_reduce(out=gmax[:, 0:nseq], in_=gT[:, 0:nt].rearrange("p (s l) -> p s l", l=L), axis=AX.X, op=ALU.max),
             reads=[B["gT"]], writes=[B["gmax"]])
        P.op("dve", lambda e: e.tensor_tensor(out=mxT[:, 0:nseq], in0=gmax[:, 0:nseq], in1=mprevT[:, 0:nseq], op=ALU.max),
             reads=[B["gmax"], B["mprevT"]], writes=[B["mxT"]])
        nbL = nbT[:, 0:nt].rearrange("p (s l) -> p s l", l=L)[:, :, L - 1]
        P.op("dve", lambda e: e.tensor_tensor(out=mnewT[:, 0:nseq], in0=mxT[:, 0:nseq], in1=nbL, op=ALU.subtract),
             reads=[B["mxT"], BnbT], writes=[B["mnewT"]])
        P.op("dve", lambda e: e.tensor_tensor(out=decT[:, 0:nseq], in0=mprevT[:, 0:nseq], in1=mxT[:, 0:nseq], op=ALU.subtract),
             reads=[B["mxT"], B["mprevT"]], writes=[B["decT"]])
        P.op("act", lambda e: e.activation(out=decT[:, 0:nseq], in_=decT[:, 0:nseq], func=AF.Exp), reads=[B["decT"]], writes=[B["decT"]])

        def bc(t):
            return t[:, 0:nseq].unsqueeze(2).to_broadcast([4, nseq, L])

        g3v = gT[:, 0:nt].rearrange("p (s l) -> p s l", l=L)
        a3v = argT[:, 0:nt].rearrange("p (s l) -> p s l", l=L)
        nb3v = nbT[:, 0:nt].rearrange("p (s l) -> p s l", l=L)
        for (row, in0v, subt, bias, rd) in ((0, g3v, mprevT, LNK, [B["gT"], B["mprevT"]]),
                                             (32, g3v, mxT, LNK, [B["gT"], B["mxT"]]),
                                             (64, nb3v, mprevT, 0.0, [BnbT, B["mprevT"]])):
            P.op("dve", lambda e, in0v=in0v, subt=subt: e.tensor_tensor(out=a3v, in0=in0v, in1=bc(subt), op=ALU.subtract),
                 reads=rd, writes=[B["argT"]])
            if bias != 0.0:
                P.op("dve", lambda e, bias=bias: e.tensor_scalar(out=argT[:, 0:nt], in0=argT[:, 0:nt], scalar1=bias, scalar2=None, op0=ALU.add),
                     reads=[B["argT"]], writes=[B["argT"]])
            P.op("act", lambda e, row=row: e.activation(out=G3[row:row + 4, 0:nt], in_=argT[:, 0:nt], func=AF.Exp),
                 reads=[B["argT"]], writes=[B["G3"]])
        yield "A"
        P.op("pe", lambda e: e.transpose(out=STp[0:nt, 256:352], in_=G3[0:96, 0:nt], identity=ident[0:96, 0:96]),
             reads=[B["G3"], B["cf"]], writes=[B["STp"]], hold=True)
        P.op("dve", lambda e: e.tensor_copy(out=G3T[0:nt, :], in_=STp[0:nt, 256:352]), reads=[B["STp"]], writes=[B["G3T"]])
        I4 = cfs("I4")
        for (off, src, bsrc) in ((0, decT, B["decT"]), (64, mnewT, B["mnewT"])):
            P.op("dve", lambda e, off=off, src=src: e.tensor_tensor(
                out=DM[:, off:off + 4 * nseq].rearrange("p (s h) -> p s h", h=4),
                in0=src[:, 0:nseq].unsqueeze(2).to_broadcast([4, nseq, 4]),
                in1=I4.unsqueeze(1).to_broadcast([4, nseq, 4]), op=ALU.mult),
                reads=[bsrc, B["cf"]], writes=[B["DM"]])
        P.op("pe", lambda e: e.matmul(out=STp[:, 384:512], lhsT=cfs("ones4"), rhs=DM[:, :], start=True, stop=True),
             reads=[B["DM"], B["cf"]], writes=[B["STp"]], hold=True)
        P.op("act", lambda e: e.activation(out=decb[:, :], in_=STp[:, 384:512], func=AF.Copy), reads=[B["STp"]], writes=[B["decb"]])
        if kind != "S":
            P.op("dve", lambda e: e.tensor_copy(out=mprevT[:, 0:1], in_=mnewT[:, 0:1]), reads=[B["mnewT"]], writes=[B["mprevT"]])

        yield "B"
        def proj_tok(key, evac):
            w, bw = load_block(key)
            pjv = rstate.get("pj", 0)
            rstate["pj"] = (pjv + 1) % NROT
            pj, bpj = PJR[pjv], B[PJN[pjv]]
            for kc in range(8):
                P.op("pe", lambda e, kc=kc, w=w, pj=pj: e.matmul(out=pj[0:nt, :], lhsT=hnT[:, kc, 0:nt], rhs=w[:, kc, :],
                                                               start=(kc == 0), stop=(kc == 7)),
                     reads=[BhnT, bw], writes=[bpj])
            evac(pj, bpj)
            tick()
            return w, bw

        def proj_fm(w, bw, evac):
            pjv = rstate.get("pj", 0)
            rstate["pj"] = (pjv + 1) % NROT
            pj, bpj = PJR[pjv], B[PJN[pjv]]
            for cc in range(4):
                for kc in range(8):
                    P.op("pe", lambda e, cc=cc, kc=kc, w=w, pj=pj: e.matmul(out=pj[:, cc * nt:(cc + 1) * nt], lhsT=w[:, kc, cc * 128:(cc + 1) * 128],
                                                                          rhs=hnT[:, kc, 0:nt], start=(kc == 0), stop=(kc == 7)),
                         reads=[BhnT, bw], writes=[bpj])
            evac(pj, bpj)
            tick()

        def head_proj(h):
            hb = h % NHB
            def tok_then_T(key, tokt, btok, dstT, bdst, extra_evac=None):
                w, bw = load_block(key)
                pjv = rstate.get("pj", 0)
                rstate["pj"] = (pjv + 1) % NROT
                pj, bpj = PJR[pjv], B[PJN[pjv]]
                for kc in range(8):
                    P.op("pe", lambda e, kc=kc, w=w, pj=pj: e.matmul(out=pj[0:nt, :], lhsT=hnT[:, kc, 0:nt], rhs=w[:, kc, :],
                                                                   start=(kc == 0), stop=(kc == 7)),
                         reads=[BhnT, bw], writes=[bpj])
                P.op("dve", lambda e, pj=pj: e.tensor_copy(out=tokt[0:nt, :], in_=pj[0:nt, :]), reads=[bpj], writes=[btok])
                if extra_evac is not None:
                    extra_evac(pj, bpj)
                tick()
                prev_mode = P.defer_mode
                P.defer_mode = "L"
                for dc in range(4):
                    P.op("pe", lambda e, dc=dc: e.transpose(out=TP[:, 512 + dc * nt:512 + (dc + 1) * nt], in_=tokt[0:nt, dc * 128:(dc + 1) * 128],
                                                            identity=cbs_ident[0:nt, 0:nt]),
                         reads=[btok, B["cb"]], writes=[B["TP"]], hold=True)
                P.op("act", lambda e: e.activation(out=dstT[:, :, 0:nt], in_=TP[:, 512:512 + 4 * nt].rearrange("p (c t) -> p c t", t=nt), func=AF.Copy),
                     reads=[B["TP"]], writes=[bdst])
                P.defer_mode = prev_mode
                tmark[0] = P.ngen.get("L", 0)
                tick()

            tok_then_T(("q", h), qtok[hb], B["qtok%d" % hb], qT[hb], B["qT%d" % hb])
            tok_then_T(("k", h), ktok[hb], B["ktok%d" % hb], kT[hb], B["kT%d" % hb],
                       extra_evac=lambda pj, bpj: P.op("act", lambda e: e.activation(
                           out=kw2[hb][0:nt, :], in_=pj[0:nt, :], func=AF.Copy, scale=G3T[0:nt, 32 + h:33 + h]),
                           reads=[bpj, B["G3T"]], writes=[B["kw2%d" % hb]]))
            proj_tok(("v", h), lambda pj, bpj: P.op("dve", lambda e: e.tensor_copy(out=vtok[hb][0:nt, :], in_=pj[0:nt, :]),
                                                    reads=[bpj], writes=[B["vtok%d" % hb]]))
            proj_tok(("o", h), lambda pj, bpj: P.op("act", lambda e: e.activation(out=GR[hb][0:nt, :], in_=pj[0:nt, :], func=AF.Sigmoid),
                                                    reads=[bpj], writes=[B["GR%d" % hb]]))

            def ev_z(pj, bpj):
                P.op("act", lambda e: e.activation(out=Gt[hb][0:nt, :], in_=pj[0:nt, :], func=AF.Silu),
                     reads=[bpj], writes=[B["G%d" % hb]])
                P.op("pool", lambda e: e.tensor_tensor(out=Gt[hb][0:nt, :], in0=Gt[hb][0:nt, :], in1=GR[hb][0:nt, :], op=ALU.mult),
                     reads=[B["G%d" % hb], B["GR%d" % hb]], writes=[B["G%d" % hb]])
            proj_tok(("z", h), ev_z)
            P.flush_upto(tmark[0], "L")
            for dc in range(4):
                P.op("pe", lambda e, dc=dc: e.matmul(out=STp[0:nt, 0:nt], lhsT=kT[hb][:, dc, 0:nt], rhs=qT[hb][:, dc, 0:nt],
                                                     start=(dc == 0), stop=(dc == 3)),
                     reads=[B["kT%d" % hb], B["qT%d" % hb]], writes=[B["STp"]])
            P.op("dve", lambda e: e.scalar_tensor_tensor(out=SWT[hb][0:nt, 0:nt], in0=STp[0:nt, 0:nt], scalar=G3T[0:nt, h:h + 1],
                                                         in1=maskT, op0=ALU.mult, op1=ALU.mult),
                 reads=[B["STp"], B["G3T"], B["cf"]], writes=[B["SWT%d" % hb]])
            tick()

        def head_mlstm(h):
            hb = h % NHB
            nm, bnm = NUM[0], B["NUM0"]
            def state_update(s):
                if kind == "S":
                    slot = (h * 16 + s) % NH
                    cst, bcst = Cst[slot], B["Cst%d" % slot]
                    vi = (h * 16 + s) % 2
                    P.op("act", lambda e, vi=vi, s=s: e.activation(out=vm[vi][:, :], in_=vtok[hb][0:64, :], func=AF.Copy, scale=cfs("ETf")[0:64, s:s + 1]),
                         reads=[B["vtok%d" % hb], B["cf"]], writes=[B["vm%d" % vi]])
                    rv, brv = vm[vi], B["vm%d" % vi]
                else:
                    cst, bcst = Cst[h], B["Cst%d" % h]
                    rv, brv = vtok[hb], B["vtok%d" % hb]
                dcol = s * 4 + h
                for dc in range(4):
                    cuv = rstate.get("cu", 0)
                    rstate["cu"] = 1 - cuv
                    cu, bcu = CU[cuv], B["CU%d" % cuv]
                    P.op("pe", lambda e, dc=dc, cu=cu, rv=rv: e.matmul(out=cu[:, :], lhsT=kw2[hb][0:nt, dc * 128:(dc + 1) * 128], rhs=rv[0:nt, :],
                                                                      start=True, stop=True),
                         reads=[B["kw2%d" % hb], brv], writes=[bcu])
                    if zero_state:
                        P.op("dve", lambda e, dc=dc, cu=cu, cst=cst: e.tensor_copy(out=cst[:, dc, :], in_=cu[:, :]), reads=[bcu], writes=[bcst])
                    else:
                        P.op("dve", lambda e, dc=dc, cu=cu, cst=cst: e.scalar_tensor_tensor(out=cst[:, dc, :], in0=cst[:, dc, :], scalar=decb[:, dcol:dcol + 1],
                                                                                          in1=cu[:, :], op0=ALU.mult, op1=ALU.add),
                             reads=[bcu, bcst, B["decb"]], writes=[bcst])
                if kind == "S":
                    P.dma("pool", lambda e, s=s, cst=cst: e.dma_start(out=C_s[s, h].rearrange("(dc p) v -> p dc v", p=128), in_=cst[:]),
                          cout_ds[slot], reads=[bcst])

            def state_cast():
                P.op("pool", lambda e: e.tensor_copy(out=Cbf[h][:], in_=Cst[h][:]), reads=[B["Cst%d" % h]], writes=[B["Cbf%d" % h]])

            ETb = cbs("ET" + kind)
            for dc in range(4):
                P.op("pe", lambda e, dc=dc: e.matmul(out=STp[:, 208 + dc * 16:208 + dc * 16 + nseq], lhsT=kw2[hb][0:nt, dc * 128:(dc + 1) * 128],
                                                     rhs=ETb[0:nt, 0:nseq], start=True, stop=True),
                     reads=[B["kw2%d" % hb], B["cb"]], writes=[B["STp"]])
            nv = nst[:, h * 4 * nseq:(h + 1) * 4 * nseq].rearrange("p (c s) -> p c s", s=nseq) if kind == "S" else \
                nst[:, h * 4:(h + 1) * 4].unsqueeze(2)
            pv = STp[:, 208:272].rearrange("p (c s) -> p c s", s=16)[:, :, 0:nseq]
            if zero_state:
                P.op("dve", lambda e: e.tensor_copy(out=nv, in_=pv), reads=[B["STp"]], writes=[B["nst"]])
            else:
                dv_ = decb[:, 0:4 * nseq].rearrange("p (s h) -> p h s", h=4)[:, h, :].unsqueeze(1).to_broadcast([128, 4, nseq])
                P.op("dve", lambda e: e.tensor_tensor(out=nv, in0=nv, in1=dv_, op=ALU.mult), reads=[B["nst"], B["decb"]], writes=[B["nst"]])
                P.op("dve", lambda e: e.tensor_tensor(out=nv, in0=nv, in1=pv, op=ALU.add), reads=[B["nst"], B["STp"]], writes=[B["nst"]])
            tick()
            ones_b = cbs("ones")
            n_den = 1 + (0 if zero_state else 4 * nseq)
            idx = [0]

            def den_mm(lhsT, rhs, reads):
                i = idx[0]
                idx[0] += 1
                P.op("pe", lambda e: e.matmul(out=STp[0:nt, 200:201], lhsT=lhsT, rhs=rhs, start=(i == 0), stop=(i == n_den - 1)),
                     reads=reads, writes=[B["STp"]])

            if kind != "S":
                state_update(0)
                tick(1)
            n_num = 1 + (0 if zero_state else 4 * nseq)
            P.op("pe", lambda e: e.matmul(out=nm[0:nt, :], lhsT=SWT[hb][0:nt, 0:nt], rhs=vtok[hb][0:nt, :], start=True, stop=(n_num == 1)),
                 reads=[B["SWT%d" % hb], B["vtok%d" % hb]], writes=[bnm])
            den_mm(SWT[hb][0:nt, 0:nt], ones_b[0:nt, 0:1], [B["SWT%d" % hb], B["cb"]])
            cnt = 1
            for s in range(nseq):
                if zero_state:
                    break
                if kind == "S":
                    slot = (h * 16 + s) % NH
                    cst, bcst, cbf, bcbf = Cst[slot], B["Cst%d" % slot], Cbf[slot], B["Cbf%d" % slot]
                    P.dma("sp", lambda e, s=s, cst=cst: e.dma_start(out=cst[:], in_=sC[s, h].rearrange("(dc p) v -> p dc v", p=128)),
                          c_ds[slot], writes=[bcst])
                    P.op("act", lambda e, cst=cst, cbf=cbf: e.activation(out=cbf[:], in_=cst[:], func=AF.Copy), reads=[bcst], writes=[bcbf])
                    qi = (h * 16 + s) % 2
                    cmv = cbs("colmask")[:, s * 64:(s + 1) * 64]
                    P.op("dve", lambda e, qi=qi, cmv=cmv: e.tensor_tensor(out=qTm[qi][:, :, :], in0=qT[hb][:, :, 0:64],
                                                                          in1=cmv.unsqueeze(1).to_broadcast([128, 4, 64]), op=ALU.mult),
                         reads=[B["qT%d" % hb], B["cb"]], writes=[B["qTm%d" % qi]])
                    lq, blq = qTm[qi], B["qTm%d" % qi]
                else:
                    cst, bcst, cbf, bcbf = Cst[h], B["Cst%d" % h], Cbf[h], B["Cbf%d" % h]
                    lq, blq = qT[hb], B["qT%d" % hb]
                for dc in range(4):
                    cnt += 1
                    P.op("pe", lambda e, dc=dc, lq=lq, cbf=cbf, last=(cnt == n_num): e.matmul(
                        out=nm[0:nt, :], lhsT=lq[:, dc, 0:nt], rhs=cbf[:, dc, :], start=False, stop=last),
                        reads=[blq, bcbf], writes=[bnm])
                    ncol = (h * 4 + dc) * nseq + s if kind == "S" else (h * 4 + dc)
                    den_mm(lq[:, dc, 0:nt], nbf[:, ncol:ncol + 1], [blq, B["nbf"]])
                if kind == "S":
                    state_update(s)
            if kind != "S":
                state_cast()
            tick(1)
            c0 = 4 + 3 * h
            prev_mode = P.defer_mode
            P.defer_mode = "L"
            P.op("dve", lambda e: e.tensor_scalar(out=sm1[0:nt, c0 + 1:c0 + 2], in0=STp[0:nt, 200:201], scalar1=-1.0, scalar2=None, op0=ALU.mult),
                 reads=[B["STp"]], writes=[B["sm1h%d" % h]], hold=True)
            P.op("dve", lambda e: e.scalar_tensor_tensor(out=sm1[0:nt, c0:c0 + 1], in0=STp[0:nt, 200:201], scalar=G3T[0:nt, 64 + h:65 + h],
                                                         in1=sm1[0:nt, c0 + 1:c0 + 2], op0=ALU.max, op1=ALU.max),
                 reads=[B["STp"], B["G3T"], B["sm1h%d" % h]], writes=[B["sm1h%d" % h]], hold=True)
            P.op("dve", lambda e: e.bn_stats(out=st6[0:nt, 0:6], in_=nm[0:nt, :]), reads=[bnm], writes=[B["st6"]], hold=True)
            P.op("dve", lambda e: e.bn_aggr(out=st6[0:nt, 6:8], in_=st6[0:nt, 0:6]), reads=[B["st6"]], writes=[B["st6"]], hold=True)
            P.op("dve", lambda e: e.tensor_scalar(out=sm1[0:nt, c0 + 1:c0 + 2], in0=sm1[0:nt, c0:c0 + 1], scalar1=sm1[0:nt, c0:c0 + 1], scalar2=EPS,
                                                  op0=ALU.mult, op1=ALU.mult),
                 reads=[B["sm1h%d" % h]], writes=[B["sm1h%d" % h]], hold=True)
            P.op("dve", lambda e: e.tensor_scalar(out=sm1[0:nt, c0 + 1:c0 + 2], in0=sm1[0:nt, c0 + 1:c0 + 2], scalar1=st6[0:nt, 7:8], scalar2=None, op0=ALU.add),
                 reads=[B["sm1h%d" % h], B["st6"]], writes=[B["sm1h%d" % h]])
            P.op("act", lambda e: e.activation(out=sm1[0:nt, c0 + 1:c0 + 2], in_=sm1[0:nt, c0 + 1:c0 + 2], func=AF.Ln),
                 reads=[B["sm1h%d" % h]], writes=[B["sm1h%d" % h]], hold=True)
            P.op("act", lambda e: e.activation(out=sm1[0:nt, c0 + 2:c0 + 3], in_=sm1[0:nt, c0 + 1:c0 + 2], func=AF.Exp, scale=-0.5),
                 reads=[B["sm1h%d" % h]], writes=[B["sm1h%d" % h]], hold=True)
            P.op("act", lambda e: e.activation(out=GR[hb][0:nt, :], in_=Gt[hb][0:nt, :], func=AF.Copy, scale=sm1[0:nt, c0 + 2:c0 + 3]),
                 reads=[B["sm1h%d" % h], B["G%d" % hb]], writes=[B["GR%d" % hb]])
            P.op("dve", lambda e: e.scalar_tensor_tensor(out=yln[0:nt, h * 512:(h + 1) * 512], in0=nm[0:nt, :], scalar=st6[0:nt, 6:7],
                                                         in1=GR[hb][0:nt, :], op0=ALU.subtract, op1=ALU.mult),
                 reads=[bnm, B["st6"], B["GR%d" % hb]], writes=[B["yln%d" % h]])
            P.defer_mode = prev_mode
            lmark[h] = P.ngen.get("L", 0)
            tick(1)
        def ytrans(h):
            P.flush_upto(lmark.get(h, 0), "L")
            identb = cbs_ident
            for mc in range(4):
                P.op("pe", lambda e, mc=mc: e.transpose(out=TP[:, mc * nt:(mc + 1) * nt], in_=yln[0:nt, h * 512 + mc * 128:h * 512 + (mc + 1) * 128],
                                                        identity=identb[0:nt, 0:nt]),
                     reads=[B["yln%d" % h], B["cb"]], writes=[B["TP"]])
            for mc in range(4):
                P.op("act", lambda e, mc=mc: e.activation(out=yT[:, h * 4 + mc, 0:nt], in_=TP[:, mc * nt:(mc + 1) * nt], func=AF.Copy,
                                                          scale=vec[:, 16 + h * 4 + mc:17 + h * 4 + mc]),
                     reads=[B["TP"], B["vec"]], writes=[B["yT%d" % h]])

        def proj_a():
            for j in range(2):
                def ev_a(pj, bpj, j=j):
                    P.op("act", lambda e: e.activation(out=A[0:nt, j * 512:(j + 1) * 512], in_=pj[0:nt, :], func=AF.Copy), reads=[bpj], writes=[BA])
                    P.op("dve", lambda e: e.tensor_copy(out=af32[0:nt, j * 512:(j + 1) * 512], in_=pj[0:nt, :]), reads=[bpj], writes=[B["af32"]])
                proj_tok(("a", j), ev_a)

        cbs_ident = cbs("identb")
        B.setdefault("identb", B["cb"])

        if kind == "S":
            P.dma("sp", lambda e: e.dma_start(out=nst[:], in_=snT[:, :]), misc_ds[8], writes=[B["nst"]])
            P.op("act", lambda e: e.activation(out=nbf[:], in_=nst[:], func=AF.Copy), reads=[B["nst"]], writes=[B["nbf"]])
        head_proj(0)
        head_proj(1)
        head_mlstm(0)
        head_proj(2)
        yield "C1"
        head_mlstm(1)
        ytrans(0)
        head_proj(3)
        head_mlstm(2)
        proj_a()
        ytrans(1)
        head_mlstm(3)
        ytrans(2)
        yield "C2"
        if kind != "S":
            P.op("act", lambda e: e.activation(out=nbf[:, 0:16], in_=nst[:, 0:16], func=AF.Copy), reads=[B["nst"]], writes=[B["nbf"]])

        if kind == "M":
            yield "D"
            return
        for j in range(2):
            w, bw = load_block(("zp", j))
            proj_fm(w, bw, lambda pj, bpj, j=j: P.op("act", lambda e: e.activation(
                out=szpT[:, 4 * j:4 * j + 4, 0:nt], in_=pj[:, 0:4 * nt].rearrange("p (c t) -> p c t", t=nt), func=AF.Silu),
                reads=[bpj], writes=[B["szpT"]]))
        for j in range(2):
            proj_tok(("gp", j), lambda pj, bpj, j=j: P.op("act", lambda e: e.activation(out=upg[0:nt, j * 512:(j + 1) * 512], in_=pj[0:nt, :], func=AF.Sigmoid),
                                                          reads=[bpj], writes=[B["upg"]]))
        for j in range(2):
            proj_tok(("gm", j), lambda pj, bpj, j=j: P.op("act", lambda e: e.activation(out=sgm[0:nt, j * 512:(j + 1) * 512], in_=pj[0:nt, :], func=AF.Sigmoid),
                                                          reads=[bpj], writes=[B["sgm"]]))
        if kind == "S":
            hists = [(hs[0], B["hs0"], 120, "phS%d_0"), (hs[1], B["hs1"], 120, "phS%d_1")]
            pck = "pcS%d"
        elif kind == "M":
            hists = []
            pck = "pcM%d"
        else:
            pck = "pcP%d"
            if ci == 0:
                hists = [(Aprev, BAprev, 16, "phP0%d_0")] if with_meta else []
            else:
                hists = [(Aprev, BAprev, 128, "phP%d_0")]
        for half in range(2):
            pjv = rstate.get("pj", 0)
            rstate["pj"] = (pjv + 1) % NROT
            pj, bpj = PJR[pjv], B[PJN[pjv]]
            for c4 in range(4):
                cc = half * 4 + c4
                g = cc // 2
                terms = [(A, BA, nt, pck % g)] + [(ht, bh, kk, nm_ % g) for (ht, bh, kk, nm_) in hists]
                for ti, (src, bsrc, kk, pname) in enumerate(terms):
                    P.op("pe", lambda e, c4=c4, cc=cc, src=src, kk=kk, pname=pname, ti=ti, nterm=len(terms), pj=pj: e.matmul(
                        out=pj[:, c4 * nt:(c4 + 1) * nt], lhsT=src[0:kk, cc * 128:(cc + 1) * 128], rhs=cbs(pname)[0:kk, 0:nt],
                        start=(ti == 0), stop=(ti == nterm - 1)),
                        reads=[bsrc, B["cb"]], writes=[bpj])
            P.op("dve", lambda e, half=half, pj=pj: e.tensor_copy(out=pooledT[:, 4 * half:4 * half + 4, 0:nt],
                                                                  in_=pj[:, 0:4 * nt].rearrange("p (c t) -> p c t", t=nt)),
                 reads=[bpj], writes=[B["pooledT"]])
        for half in range(2):
            pjv = rstate.get("pj", 0)
            rstate["pj"] = (pjv + 1) % NROT
            pj, bpj = PJR[pjv], B[PJN[pjv]]
            for d4 in range(4):
                dc = half * 4 + d4
                g, dl = dc // 2, dc % 2
                for cl in range(2):
                    P.op("pe", lambda e, d4=d4, g=g, dl=dl, cl=cl, pj=pj: e.matmul(
                        out=pj[:, d4 * nt:(d4 + 1) * nt], lhsT=wpl[:, g, cl, dl * 128:(dl + 1) * 128], rhs=pooledT[:, 2 * g + cl, 0:nt],
                        start=(cl == 0), stop=(cl == 1)),
                        reads=[B["wpl"], B["pooledT"]], writes=[bpj])
            for d4 in range(4):
                dc = half * 4 + d4
                P.op("dve", lambda e, d4=d4, dc=dc, pj=pj: e.scalar_tensor_tensor(
                    out=ypT[:, dc, 0:nt], in0=pj[:, d4 * nt:(d4 + 1) * nt], scalar=vec[:, 8 + dc:9 + dc], in1=szpT[:, dc, 0:nt],
                    op0=ALU.mult, op1=ALU.mult),
                    reads=[bpj, B["vec"], B["szpT"]], writes=[B["ypT"]])
        for j in range(2):
            w, bw = load_block(("bp", j))
            pjv = rstate.get("pj", 0)
            rstate["pj"] = (pjv + 1) % NROT
            pj, bpj = PJR[pjv], B[PJN[pjv]]
            for pc in range(8):
                P.op("pe", lambda e, pc=pc, w=w, pj=pj: e.matmul(out=pj[0:nt, :], lhsT=ypT[:, pc, 0:nt], rhs=w[:, pc, :], start=(pc == 0), stop=(pc == 7)),
                     reads=[B["ypT"], bw], writes=[bpj])
            P.op("dve", lambda e, j=j, pj=pj: e.tensor_tensor(out=upg[0:nt, j * 512:(j + 1) * 512], in0=pj[0:nt, :], in1=upg[0:nt, j * 512:(j + 1) * 512], op=ALU.mult),
                 reads=[bpj, B["upg"]], writes=[B["upg"]])
        yield "D"
        ytrans(3)
        for j in range(2):
            pjv = rstate.get("pj", 0)
            rstate["pj"] = (pjv + 1) % NROT
            pj, bpj = PJR[pjv], B[PJN[pjv]]
            for mh in range(2):
                w, bw = load_block(("bm", j, mh))
                for m8 in range(8):
                    mc = mh * 8 + m8
                    P.op("pe", lambda e, mc=mc, m8=m8, w=w, pj=pj: e.matmul(out=pj[0:nt, :], lhsT=yT[:, mc, 0:nt], rhs=w[:, m8, :],
                                                                          start=(mc == 0), stop=(mc == 15)),
                         reads=[B["yT%d" % (mc // 4)], bw], writes=[bpj])
            P.op("dve", lambda e, j=j, pj=pj: e.tensor_tensor(out=sgm[0:nt, j * 512:(j + 1) * 512], in0=pj[0:nt, :], in1=sgm[0:nt, j * 512:(j + 1) * 512], op=ALU.mult),
                 reads=[bpj, B["sgm"]], writes=[B["sgm"]])
            P.op("pool", lambda e, j=j: e.tensor_tensor(out=ubf[0:nt, j * 512:(j + 1) * 512], in0=sgm[0:nt, j * 512:(j + 1) * 512],
                                                        in1=upg[0:nt, j * 512:(j + 1) * 512], op=ALU.add),
                 reads=[B["sgm"], B["upg"]], writes=[B["ubf"]])
        for dc in range(8):
            P.op("pe", lambda e, dc=dc: e.transpose(out=TP[:, dc * nt:(dc + 1) * nt], in_=ubf[0:nt, dc * 128:(dc + 1) * 128], identity=cbs_ident[0:nt, 0:nt]),
                 reads=[B["ubf"], B["cb"]], writes=[B["TP"]])
        P.op("act", lambda e: e.activation(out=uT[:, :, 0:nt], in_=TP[:, 0:8 * nt].rearrange("p (c t) -> p c t", t=nt), func=AF.Copy),
             reads=[B["TP"]], writes=[B["uT"]])
        for j in range(2):
            w, bw = load_block(("wo", j))
            pjv = rstate.get("pj", 0)
            rstate["pj"] = (pjv + 1) % NROT
            pj, bpj = PJR[pjv], B[PJN[pjv]]
            for dc in range(8):
                P.op("pe", lambda e, dc=dc, w=w, pj=pj: e.matmul(out=pj[0:nt, :], lhsT=uT[:, dc, 0:nt], rhs=w[:, dc, :], start=(dc == 0), stop=(dc == 7)),
                     reads=[B["uT"], bw], writes=[bpj])
            P.op("dve", lambda e, j=j, pj=pj: e.tensor_tensor(out=xo[0:nt, j * 512:(j + 1) * 512], in0=pj[0:nt, :], in1=X[0:nt, j * 512:(j + 1) * 512], op=ALU.add),
                 reads=[bpj, BX], writes=[B["xo"]])
        P.op("act", lambda e: e.activation(out=upg[0:nt, :], in_=xo[0:nt, :], func=AF.Square, accum_out=sm1[0:nt, 3:4]),
             reads=[B["xo"]], writes=[B["upg"], B["sm1t"]])
        P.op("act", lambda e: e.activation(out=sm1[0:nt, 3:4], in_=sm1[0:nt, 3:4], func=AF.Ln, scale=1.0 / D, bias=EPS),
             reads=[B["sm1t"]], writes=[B["sm1t"]])
        P.op("act", lambda e: e.activation(out=sm1[0:nt, 3:4], in_=sm1[0:nt, 3:4], func=AF.Exp, scale=-0.5),
             reads=[B["sm1t"]], writes=[B["sm1t"]])
        if y_dst is not None:
            P.op("dve", lambda e: e.scalar_tensor_tensor(out=xo[0:nt, :], in0=xo[0:nt, :], scalar=sm1[0:nt, 3:4], in1=fg[0:nt, :],
                                                         op0=ALU.mult, op1=ALU.mult),
                 reads=[B["xo"], B["sm1t"], B["fg"]], writes=[B["xo"]])
            P.dma("pool", lambda e: e.dma_start(out=y_dst, in_=xo[0:nt, :]), yo_ds, reads=[B["xo"]])

    passes = []
    if with_meta:
        passes.append(("M", meta[:, :], None, 0))
    for ci in range(nchunk):
        passes.append(("P", xp[ci * 128:(ci + 1) * 128, :], y_p[ci * 128:(ci + 1) * 128, :], ci))
    if with_sample:
        passes.append(("S", xs[:, :], y_s[:, :], 0))
        P.dma("pool", lambda e: e.dma_start(out=pool_s[:, 0:11, :], in_=spool.ap().rearrange("(s j) d -> s j d", j=15)[:, 4:15, :]),
              out_ds)
    gens = [run_pass(*p) for p in passes]
    last_p = max([i for i, p in enumerate(passes) if p[0] == "P"])

    def adv(i):
        try:
            next(gens[i])
        except StopIteration:
            pass

    def after_C2(i):
        if passes[i][0] == "S":
            P.dma("pool", lambda e: e.dma_start(out=nT_s[:, :], in_=nst[:, :]), o_ds["nTs"], reads=[B["nst"]])
            P.dma("pool", lambda e: e.dma_start(out=m_s[:, :], in_=decb[0:1, 64:128]), o_ds["ms"], reads=[B["decb"]])
        if i == last_p:
            for h in range(NH):
                P.dma("pool", lambda e, h=h: e.dma_start(out=C_p[h].rearrange("(dc p) v -> p dc v", p=128), in_=Cst[h][:]), cout_ds[h],
                      reads=[B["Cst%d" % h]])
            P.dma("pool", lambda e: e.dma_start(out=nT_p[:, :], in_=nst[:, 0:64]), o_ds["nTp"], reads=[B["nst"]])
            P.dma("pool", lambda e: e.dma_start(out=m_p[:, :], in_=decb[0:1, 64:128]), o_ds["mp"], reads=[B["decb"]])

    def after_D(i):
        if i == last_p:
            P.dma("pool", lambda e: e.dma_start(out=pool_p[:, :], in_=af32[113:128, :]), o_ds["pp"], reads=[B["af32"]])

    def after_E(i):
        if passes[i][0] == "S":
            P.dma("pool", lambda e: [e.dma_start(out=pool_s[s, 11:15, :], in_=af32[4 * s:4 * s + 4, :]) for s in range(16)],
                  o_ds["ps"], reads=[B["af32"]], n=16)

    npass = len(gens)

    def genA(i):
        P.defer_mode = "A"
        adv(i)
        P.defer_mode = None

    genA(0)
    P.flush()
    adv(0)
    for i in range(npass):
        if i + 1 < npass:
            genA(i + 1)
        adv(i)
        adv(i)
        P.flush(q="A")
        after_C2(i)
        if i + 1 < npass:
            P.defer_mode = "L"
            adv(i + 1)
            P.defer_mode = None
        adv(i)
        P.flush()
        after_D(i)
        adv(i)
        after_E(i)
    P.emit()
    return nc, (cf_np, cb_np), P


def _core_inputs(b, x_prompt, x_sample, state_pool, state_C, state_n, state_m, meta_tokens, norm_g, w_in, gate_bias,
                 w_pool, pool_scale, mh_norm_g, w_branch_pool, w_branch_mlstm, w_out, final_g, consts, nchunk=16):
    cf_np, cb_np = consts
    f = np.float32
    sl = slice(16 * b, 16 * b + 16)
    vecs = np.zeros((128, 128), f)
    vecs[:, 0:8] = np.asarray(norm_g[0], f).reshape(8, 128).T
    vecs[:, 8:16] = np.asarray(pool_scale[0], f).reshape(8, 128).T
    vecs[:, 16:32] = np.asarray(mh_norm_g[0], f).reshape(16, 128).T
    vecs[0:4, 32] = np.asarray(gate_bias[0, 0:4], f)
    vecs[0:4, 33] = np.asarray(gate_bias[0, 4:8], f)
    sn = np.asarray(state_n[0, sl], f)
    snT = np.ascontiguousarray(sn.reshape(16, 4, 4, 128).transpose(3, 1, 2, 0)).reshape(128, 256)
    smT = np.zeros((4, 64), f)
    smT[:, 0:16] = np.asarray(state_m[0, sl], f).T
    return {
        "xp": np.ascontiguousarray(x_prompt[b][:128 * nchunk], f),
        "xs": np.ascontiguousarray(np.asarray(x_sample[sl], f).reshape(64, D)),
        "spool": np.ascontiguousarray(np.asarray(state_pool[0, sl], f).reshape(240, D)),
        "sC": np.ascontiguousarray(state_C[0, sl], f),
        "snT": snT, "smT": smT,
        "meta": np.ascontiguousarray(meta_tokens, f),
        "w_in": np.ascontiguousarray(w_in[0], f),
        "w_pool": np.ascontiguousarray(w_pool[0], f),
        "wbp": np.ascontiguousarray(w_branch_pool[0], f),
        "wbm": np.ascontiguousarray(w_branch_mlstm[0], f),
        "wout": np.ascontiguousarray(w_out[0], f),
        "vecs": vecs,
        "fgb": np.ascontiguousarray(np.asarray(final_g, f).reshape(1, D)),
        "cf": cf_np, "cb": cb_np,
    }


def _gather(results, nb):
    f = np.float32
    y_p = np.stack([r["y_p"] for r in results]).astype(f)
    y_s = np.concatenate([r["y_s"].reshape(16, 4, D) for r in results]).astype(f)
    pool_p = np.stack([r["pool_p"] for r in results])[None].astype(f)
    C_p = np.stack([r["C_p"] for r in results])[None].astype(f)
    n_p = np.stack([r["nT_p"][:, 0:16].reshape(128, 4, 4).transpose(1, 2, 0).reshape(4, 512) for r in results])[None].astype(f)
    m_p = np.stack([r["m_p"][0, 0:4] for r in results])[None].astype(f)
    pool_s = np.concatenate([r["pool_s"] for r in results])[None].astype(f)
    C_s = np.concatenate([r["C_s"] for r in results])[None].astype(f)
    n_s = np.concatenate([r["nT_s"].reshape(128, 4, 4, 16).transpose(3, 1, 2, 0).reshape(16, 4, 512) for r in results])[None].astype(f)
    m_s = np.concatenate([r["m_s"][0].reshape(16, 4) for r in results])[None].astype(f)
    return (y_p, y_s, pool_p, C_p, n_p, m_p, pool_s, C_s, n_s, m_s)


_CACHE = {}


def kernel(**inputs):
    if "nc" not in _CACHE:
        _CACHE["nc"] = build()
    nc, consts, _ = _CACHE["nc"]
    in_maps = [_core_inputs(b, consts=consts, **inputs) for b in range(8)]
    res = run_bass_kernel_spmd(nc, in_maps, core_ids=list(range(8)))
    return _gather(res.results, 8)
```
